# Optimizing a Trainium2 kernel written in Bass

```python
import jax, jax.numpy as jnp
from jax import lax
import numpy as np

D_MODEL = 2048
BATCH = 4
SEQ = 8192
DEPTH = 1

HEAD_DIM = 128
SB_HEADS = D_MODEL // (2 * HEAD_DIM)
RET_HEADS = D_MODEL // (2 * HEAD_DIM)
SB_WIDTH = SB_HEADS * HEAD_DIM
RET_WIDTH = RET_HEADS * HEAD_DIM
MIX_WIDTH = SB_WIDTH + RET_WIDTH
IN_WIDTH = 4 * SB_WIDTH + 4 * RET_WIDTH
Q_BLOCK = 128
RET_CHUNK = 128
ROPE_BASE = 10000.0
EPS = 1e-6

kernel_name = "hymba_stickbreak_retention_adaln"


def rms_norm(x, gain):
    xf = x.astype(jnp.float32)
    y = xf * lax.rsqrt(jnp.mean(xf * xf, axis=-1, keepdims=True) + EPS)
    return (y * gain.astype(jnp.float32)).astype(x.dtype)


def split_heads(t, n_heads):
    b, s, _ = t.shape
    return t.reshape(b, s, n_heads, HEAD_DIM).transpose(0, 2, 1, 3)


def merge_heads(t):
    b, h, s, d = t.shape
    return t.transpose(0, 2, 1, 3).reshape(b, s, h * d)


def rotary(t):
    s, d = t.shape[2], t.shape[3]
    half = d // 2
    inv_freq = ROPE_BASE ** (-jnp.arange(half, dtype=jnp.float32) / half)
    ang = jnp.arange(s, dtype=jnp.float32)[:, None] * inv_freq[None, :]
    cos, sin = jnp.cos(ang), jnp.sin(ang)
    tf = t.astype(jnp.float32)
    t1, t2 = tf[..., :half], tf[..., half:]
    return jnp.concatenate([t1 * cos - t2 * sin, t1 * sin + t2 * cos], axis=-1)


def stick_breaking_attention(q, k, v):
    b, h, s, d = q.shape
    nb = s // Q_BLOCK
    qb = q.astype(jnp.float32).reshape(b, h, nb, Q_BLOCK, d).transpose(2, 0, 1, 3, 4)
    kf = k.astype(jnp.float32)
    vf = v.astype(jnp.float32)
    key_pos = jnp.arange(s)
    inv_sqrt_d = float(1.0 / np.sqrt(d))

    def one_block(args):
        q_blk, blk = args
        q_pos = blk * Q_BLOCK + jnp.arange(Q_BLOCK)
        z = jnp.einsum('bhqd,bhsd->bhqs', q_blk, kf) * inv_sqrt_d
        past = key_pos[None, :] < q_pos[:, None]
        log_not = jnp.where(past, jax.nn.log_sigmoid(-z), 0.0)
        later = lax.cumsum(log_not, axis=3, reverse=True) - log_not
        w = jnp.where(past, jnp.exp(jax.nn.log_sigmoid(z) + later), 0.0)
        return jnp.einsum('bhqs,bhsd->bhqd', w, vf)

    out = lax.map(one_block, (qb, jnp.arange(nb)))
    return out.transpose(1, 2, 0, 3, 4).reshape(b, h, s, d)


def chunkwise_retention(q, k, v):
    b, h, s, d = q.shape
    c = RET_CHUNK
    n = s // c
    log_gamma = jnp.log1p(-jnp.exp2(-5.0 - jnp.arange(h, dtype=jnp.float32)))
    idx = jnp.arange(c, dtype=jnp.float32)
    rel = idx[:, None] - idx[None, :]
    decay_in = jnp.where(rel[None] >= 0, jnp.exp(jnp.maximum(rel, 0.0)[None] * log_gamma[:, None, None]), 0.0)
    q_decay = jnp.exp((idx[None, :] + 1.0) * log_gamma[:, None])[..., None]
    k_decay = jnp.exp((c - 1.0 - idx[None, :]) * log_gamma[:, None])[..., None]
    chunk_decay = jnp.exp(c * log_gamma)[:, None, None]

    def chunks(t):
        return t.reshape(b, h, n, c, d).transpose(2, 0, 1, 3, 4)

    def step(state, inp):
        qc, kc, vc = inp
        inner = jnp.einsum('bhij,bhjd->bhid', jnp.einsum('bhid,bhjd->bhij', qc, kc) * decay_in, vc)
        cross = jnp.einsum('bhid,bhde->bhie', qc * q_decay, state)
        new_state = state * chunk_decay + jnp.einsum('bhjd,bhje->bhde', kc * k_decay, vc)
        return new_state, inner + cross

    state0 = jnp.zeros((b, h, d, d), jnp.float32)
    _, out = lax.scan(step, state0, (chunks(q), chunks(k), chunks(v)))
    return out.transpose(1, 2, 0, 3, 4).reshape(b, h, s, d)


def head_group_norm(o, gain):
    mu = jnp.mean(o, axis=-1, keepdims=True)
    var = jnp.mean(jnp.square(o - mu), axis=-1, keepdims=True)
    return merge_heads((o - mu) * lax.rsqrt(var + EPS)) * gain.astype(jnp.float32)


def setup_inputs(seed: int = 0) -> dict:
    key = jax.random.key(seed)
    ks = jax.random.split(key, 10)
    f32 = jnp.float32
    x = jax.random.normal(ks[0], (BATCH, SEQ, D_MODEL), f32)
    c = jax.random.normal(ks[1], (BATCH, D_MODEL), f32)
    w_ada = jax.random.normal(ks[2], (DEPTH, D_MODEL, 3 * D_MODEL), f32) * D_MODEL ** -0.5
    b_ada = 0.02 * jax.random.normal(ks[3], (DEPTH, 3 * D_MODEL), f32)
    norm_gain = 1.0 + 0.02 * jax.random.normal(ks[4], (DEPTH, D_MODEL), f32)
    w_in = jax.random.normal(ks[5], (DEPTH, D_MODEL, IN_WIDTH), f32) * D_MODEL ** -0.5
    sb_q_gain = 1.0 + 0.02 * jax.random.normal(ks[6], (DEPTH, HEAD_DIM), f32)
    sb_k_gain = 1.0 + 0.02 * jax.random.normal(ks[7], (DEPTH, HEAD_DIM), f32)
    ret_norm_gain = 1.0 + 0.02 * jax.random.normal(ks[8], (DEPTH, RET_WIDTH), f32)
    w_out = jax.random.normal(ks[9], (DEPTH, MIX_WIDTH, D_MODEL), f32) * MIX_WIDTH ** -0.5
    return {"x": x, "c": c, "w_ada": w_ada, "b_ada": b_ada, "norm_gain": norm_gain,
            "w_in": w_in, "sb_q_gain": sb_q_gain, "sb_k_gain": sb_k_gain,
            "ret_norm_gain": ret_norm_gain, "w_out": w_out}


def reference(x, c, w_ada, b_ada, norm_gain, w_in, sb_q_gain, sb_k_gain, ret_norm_gain, w_out):
    split_at = [SB_WIDTH * i for i in range(1, 5)] + [4 * SB_WIDTH + RET_WIDTH * i for i in range(1, 4)]
    for layer in range(DEPTH):
        mod = jax.nn.silu(c) @ w_ada[layer] + b_ada[layer]
        shift, scale, gate = jnp.split(mod, 3, axis=-1)
        h = rms_norm(x, norm_gain[layer]) * (1.0 + scale[:, None, :]) + shift[:, None, :]

        proj = h @ w_in[layer]
        q_sb, k_sb, v_sb, z_sb, q_r, k_r, v_r, z_r = jnp.split(proj, split_at, axis=-1)

        qa = rms_norm(split_heads(q_sb, SB_HEADS), sb_q_gain[layer])
        ka = rms_norm(split_heads(k_sb, SB_HEADS), sb_k_gain[layer])
        oa = merge_heads(stick_breaking_attention(qa, ka, split_heads(v_sb, SB_HEADS)))
        oa = oa * jax.nn.silu(z_sb.astype(jnp.float32))

        qr = rotary(split_heads(q_r, RET_HEADS))
        kr = rotary(split_heads(k_r, RET_HEADS)) * (HEAD_DIM ** -0.5)
        vr = split_heads(v_r, RET_HEADS).astype(jnp.float32)
        ob = head_group_norm(chunkwise_retention(qr, kr, vr), ret_norm_gain[layer])
        ob = ob * jax.nn.silu(z_r.astype(jnp.float32))

        y = jnp.concatenate([oa, ob], axis=-1).astype(x.dtype) @ w_out[layer]
        x = x + (gate[:, None, :] * y).astype(x.dtype)
    return x
```

```python
import math
from contextlib import ExitStack

import numpy as np
import ml_dtypes

import concourse.bass as bass
import concourse.mybir as mybir
from concourse.bass_utils import run_bass_kernel_spmd

F32 = mybir.dt.float32
BF16 = mybir.dt.bfloat16
AF = mybir.ActivationFunctionType
ALU = mybir.AluOpType
AX = mybir.AxisListType

D = 2048
KC = D // 128
HL = 4
EPS = 1e-6
RSQ_D = 1.0 / math.sqrt(128.0)


class G:
    pass


def build_program(S, n_cores=8, gather=True, phases="0ARBC", dbg=False):
    NT = S // 512
    NB = S // 128
    HALF = S // 2
    nc = bass.Bass("TRN2", target_bir_lowering=False)
    g = G()
    g.gather = gather

    def din(name, shape, dt=F32):
        return nc.dram_tensor(name, list(shape), dt, kind="ExternalInput")

    x_d = din("x", [S, D])
    xres_d = din("xres", [S, 1024])
    cT_d = din("cT", [128, KC])
    wss_d = din("w_ada_ss", [D, 4096])
    wg_d = din("w_ada_g", [D, 1024])
    bss_d = din("b_ss", [1, 4096])
    bg_d = din("b_g", [1, 1024])
    ng_d = din("ng", [128, KC])
    win_d = din("w_in_c", [D, 4096])
    gq_d = din("gq", [128, 1])
    gk_d = din("gk", [128, 1])
    rg_d = din("rgain", [128, 512])
    wout_d = din("w_out_c", [D, 1024])
    ident_d = din("ident", [128, 128], BF16)
    tri_d = din("tri", [128, 128], BF16)
    ones_d = din("ones", [128, 128], BF16)
    cm4_d = din("cmask4", [128, 512], BF16)
    dmask_d = din("dmask", [128, 4 * 512], BF16)
    rtab_d = din("rtab", [NB, 128, 4 * 256])
    gc_d = din("gc", [128, HL])
    out_d = nc.dram_tensor("out", [S, 1024], F32, kind="ExternalOutput")

    def dscr(name, shape, dt, d=dbg):
        if d:
            return nc.dram_tensor(name, list(shape), dt, kind="ExternalOutput")
        return nc.dram_tensor(name, list(shape), dt)

    hT_s = dscr("hT_s", [NT, 128, KC * 512], BF16)
    qT_s = dscr("qT_s", [HL, 128, S], BF16)
    kT_s = dscr("kT_s", [HL, 128, S], BF16)
    zg_s = dscr("zg_s", [HL, 128, S], BF16)
    v_s = dscr("v_s", [HL, 128, NB * 128], BF16)
    mixc = [[dscr(f"mixc_{i}_{hf}", [128, HALF], BF16, dbg and not gather) for hf in range(2)] for i in range(8)]
    if gather:
        gath = [[dscr(f"gath_{i}_{hf}", [256, HALF], BF16, False) for hf in range(2)] for i in range(8)]
    rgroups = [[2 * i, 2 * i + 1] for i in range(n_cores // 2)]
    gdump = nc.dram_tensor("gdump", [16, 256, HALF], BF16, kind="ExternalOutput") if (dbg and gather) else None

    es0 = ExitStack()
    with es0:
        def sb(name, shape, dt, es=es0):
            return es.enter_context(nc.sbuf_tensor("sb_" + name, list(shape), dt))

        def ps(name, shape, dt, es=es0):
            return es.enter_context(nc.psum_tensor("pt_" + name, list(shape), dt))

        phase_sems = []
        s_cc = nc.alloc_semaphore("s_cc_persist")

        def sem(name, es=None):
            h = nc.alloc_semaphore(name)
            phase_sems.append(h)
            return h

        def end_phase():
            nc.clear_and_free_semaphores(list(phase_sems))
            nc.all_engine_barrier()
            phase_sems.clear()

        ident = sb("ident", [128, 128], BF16)
        tri = sb("tri", [128, 128], BF16)
        ones = sb("ones", [128, 128], BF16)
        cm4 = sb("cm4", [128, 512], BF16)
        dmask = sb("dmask", [128, 4, 512], BF16)
        AB = sb("AB", [128, 32], F32)
        gate_bc = sb("gate_bc", [128, 1024], F32)
        gqs = sb("gqs", [128, 1], F32)
        gks = sb("gks", [128, 1], F32)
        rgain = sb("rgain", [128, 512], F32)
        gcs = sb("gcs", [128, HL], F32)
        onesf = sb("onesf", [1, 128], F32)

        es_a1w = ExitStack()
        a1_Wb = sb("a1_Wb", [128, KC, 2048], BF16, es_a1w)
        a1_wst = sb("a1_wst", [128, 2, 2048], F32, es_a1w)
        with ExitStack() as es:
            cT = sb("cT", [128, KC], F32, es)
            t0 = sb("t0", [128, KC], F32, es)
            t1 = sb("t1", [128, KC], F32, es)
            scv = sb("scv", [128, KC], F32, es)
            bssr = sb("bssr", [1, 4096], F32, es)
            rowss = sb("rowss", [1, 4096], F32, es)
            ngt = sb("ngt", [128, KC], F32, es)
            modsb = sb("modsb", [128, 32], F32, es)
            bgt = sb("bgt", [1, 1024], F32, es)
            grow = sb("grow", [1, 1024], F32, es)
            wss = sb("wss", [128, 3, 4096], F32, es)
            wgt = sb("wgt", [128, 2, 1024], F32, es)
            one1 = sb("one1", [1, 1], F32, es)
            pr = [ps(f"p0r{i}", [128, 512], F32, es) for i in range(8)]
            s_cst = sem("s_cst", es)
            s_wfull = [sem(f"s_wfull{i}", es) for i in range(3)]
            s_gfull = [sem(f"s_gfull{i}", es) for i in range(2)]
            s_wfree, s_gfree = sem("s_wfree", es), sem("s_gfree", es)
            s_a1w = [sem(f"s_a1w{i}", es) for i in range(2)]
            s_a1cv, s_a1cg = sem("s_a1cv", es), sem("s_a1cg", es)
            s_a, s_b, s_c, s_d, s_e, s_f, s_z, s_r, s_t = (sem("s_p0" + n, es) for n in "abcdefzrt")
            NCST = 13
            with nc.Block() as block:
                @block.sync
                def _(e):
                    for dst, src in ((ident[:], ident_d[:, :]), (tri[:], tri_d[:, :]), (ones[:], ones_d[:, :]),
                                     (cm4[:], cm4_d[:, :]),
                                     (dmask[:].rearrange("p r t -> p (r t)"), dmask_d[:, :]),
                                     (cT[:], cT_d[:, :]), (bssr[:], bss_d[:, :]), (ngt[:], ng_d[:, :]),
                                     (gqs[:], gq_d[:, :]), (gks[:], gk_d[:, :]), (rgain[:], rg_d[:, :]),
                                     (gcs[:], gc_d[:, :]), (bgt[:], bg_d[:, :])):
                        e.dma_start(out=dst, in_=src).then_inc(s_cst, 16)
                    for kc in range(KC):
                        if kc >= 3:
                            e.wait_ge(s_wfree, kc - 2)
                        e.dma_start(out=wss[:, kc % 3, :], in_=wss_d[kc * 128:(kc + 1) * 128, :]).then_inc(s_wfull[kc % 3], 16)
                        if kc >= 2:
                            e.wait_ge(s_a1cv if kc % 2 == 0 else s_a1cg, kc // 2)
                        e.dma_start(out=a1_wst[:, kc % 2, :], in_=win_d[kc * 128:(kc + 1) * 128, 0:2048]
                                    ).then_inc(s_a1w[kc % 2], 16)
                    for kc in range(KC):
                        if kc >= 2:
                            e.wait_ge(s_gfree, kc - 1)
                        e.dma_start(out=wgt[:, kc % 2, :], in_=wg_d[kc * 128:(kc + 1) * 128, :]).then_inc(s_gfull[kc % 2], 16)

                @block.scalar
                def _(a):
                    a.wait_ge(s_cst, 16 * NCST)
                    a.activation(t0[:], cT[:], AF.Exp, scale=-1.0).then_inc(s_a, 1)
                    a.wait_ge(s_a, 1)
                    a.activation(t1[:], t0[:], AF.Ln, bias=1.0).then_inc(s_a, 1)
                    a.wait_ge(s_a, 2)
                    a.activation(t0[:], t1[:], AF.Exp, scale=-1.0).then_inc(s_a, 1)

                @block.gpsimd
                def _(p):
                    for kc in range(1, KC, 2):
                        p.wait_ge(s_a1w[1], 16 * (kc // 2 + 1))
                        p.tensor_copy(a1_Wb[:, kc, :], a1_wst[:, 1, :]).then_inc(s_a1cg, 1)
                    p.wait_ge(s_a1cg, KC // 2)

                @block.vector
                def _(v):
                    v.memset(onesf[:], 1.0).then_inc(s_z, 1)
                    v.memset(one1[:], 1.0).then_inc(s_z, 1)
                    v.wait_ge(s_a, 3)
                    v.tensor_tensor(scv[:], cT[:], t0[:], ALU.mult).then_inc(s_b, 1)
                    v.tensor_scalar(gqs[:], gqs[:], -RSQ_D, None, ALU.mult).then_inc(s_z, 1)
                    for kc in range(0, KC, 2):
                        v.wait_ge(s_a1w[0], 16 * (kc // 2 + 1))
                        v.tensor_copy(a1_Wb[:, kc, :], a1_wst[:, 0, :]).then_inc(s_a1cv, 1)
                    v.wait_ge(s_c, 1)
                    for i in range(8):
                        v.tensor_tensor(rowss[:, i * 512:(i + 1) * 512], pr[i][0:1, :], bssr[:, i * 512:(i + 1) * 512],
                                        ALU.add).then_inc(s_r, 1)
                    v.wait_ge(s_d, 1)
                    for i in range(2):
                        v.tensor_tensor(grow[:, i * 512:(i + 1) * 512], pr[i][0:1, :], bgt[:, i * 512:(i + 1) * 512],
                                        ALU.add).then_inc(s_e, 1)
                    v.wait_ge(s_t, 1)
                    v.tensor_copy(modsb[:], pr[2][:, 0:32]).then_inc(s_f, 1)
                    v.wait_ge(s_f, 1)
                    v.tensor_scalar(AB[:, 0:16], modsb[:, 16:32], 1.0, None, ALU.add).then_inc(s_f, 1)
                    v.wait_ge(s_f, 2)
                    v.tensor_tensor(AB[:, 0:16], AB[:, 0:16], ngt[:], ALU.mult).then_inc(s_f, 1)
                    v.tensor_copy(AB[:, 16:32], modsb[:, 0:16]).then_inc(s_f, 1)
                    v.wait_ge(s_t, 3)
                    for i in range(2):
                        v.tensor_copy(gate_bc[:, i * 512:(i + 1) * 512], pr[3 + i][:]).then_inc(s_f, 1)
                    v.wait_ge(s_f, 6)
                    v.wait_ge(s_z, 3)
                    v.wait_ge(s_a1cv, KC // 2)

                @block.tensor
                def _(t):
                    t.wait_ge(s_b, 1)
                    t.wait_ge(s_z, 2)
                    for kc in range(KC):
                        t.wait_ge(s_wfull[kc % 3], 16 * (kc // 3 + 1))
                        for i in range(8):
                            ins = t.matmul(pr[i][0:1, :], scv[:, kc:kc + 1], wss[:, kc % 3, i * 512:(i + 1) * 512],
                                           start=(kc == 0), stop=(kc == KC - 1))
                        ins.then_inc(s_wfree, 1)
                    t.wait_ge(s_wfree, KC)
                    t.drain().then_inc(s_c, 1)
                    t.wait_ge(s_r, 8)
                    for kc in range(KC):
                        t.wait_ge(s_gfull[kc % 2], 16 * (kc // 2 + 1))
                        for i in range(2):
                            ins = t.matmul(pr[i][0:1, :], scv[:, kc:kc + 1], wgt[:, kc % 2, i * 512:(i + 1) * 512],
                                           start=(kc == 0), stop=(kc == KC - 1))
                        ins.then_inc(s_gfree, 1)
                    t.wait_ge(s_gfree, KC)
                    t.drain().then_inc(s_d, 1)
                    for j in range(32):
                        ins = t.matmul(pr[2][:, j:j + 1], rowss[0:1, j * 128:(j + 1) * 128], one1[0:1, 0:1],
                                       start=True, stop=True)
                    ins.then_inc(s_t, 1)
                    t.wait_ge(s_e, 2)
                    for i in range(2):
                        t.matmul(pr[3 + i][:], onesf[0:1, :], grow[0:1, i * 512:(i + 1) * 512],
                                 start=True, stop=True).then_inc(s_t, 1)

        end_phase()
        for k, v in list(locals().items()):
            if k not in ("g", "es", "block", "_"):
                setattr(g, k, v)
        if "A" in phases:
            phase_a1(g)
            end_phase()
        es_a1w.close()
        if "R" in phases:
            phase_a2(g)
            end_phase()
        es_wo = ExitStack()
        g.d_Wo = sb("d_Wo", [128, KC, 1024], BF16, es_wo)
        g.d_wst = sb("d_wst", [128, 2, 1024], F32, es_wo)
        g.wo_prefetched = False
        if "B" in phases:
            phase_b(g)
            end_phase()
        if "C" in phases:
            phase_cd(g)
            end_phase()
        es_wo.close()
    return nc


def _bf(a):
    return np.ascontiguousarray(a).astype(ml_dtypes.bfloat16)


def const_tables(S, hh):
    NB = S // 128
    j = np.arange(128)
    ident = np.eye(128, dtype=np.float32)
    tri = (j[:, None] >= j[None, :]).astype(np.float32)
    ones = np.ones((128, 128), np.float32)
    cm = (j[:, None] <= j[None, :]).astype(np.float32)
    cm4 = np.tile(cm, (1, 4))
    t = np.arange(512)
    dmask = np.stack([((r * 128 + j)[:, None] < t[None, :]).astype(np.float32) for r in range(4)], 1)
    half = 64
    inv_freq = (10000.0 ** (-np.arange(half, dtype=np.float32) / half)).astype(np.float32)
    ang = (np.arange(S, dtype=np.float32)[:, None] * inv_freq[None, :]).astype(np.float32)
    cos, sin = np.cos(ang).astype(np.float32), np.sin(ang).astype(np.float32)
    gh = hh * HL + np.arange(HL)
    log_gamma = np.log1p(-np.exp2(-5.0 - gh.astype(np.float32))).astype(np.float32)
    p = (np.arange(S) % 128).astype(np.float32)
    dq = np.exp((p[:, None] + 1.0) * log_gamma[None, :]).astype(np.float32)
    dk = (np.exp(-(p[:, None] + 1.0) * log_gamma[None, :]) * (128.0 ** -0.5)).astype(np.float32)
    cq = cos[:, None, :] * dq[:, :, None]
    sq = sin[:, None, :] * dq[:, :, None]
    ck = cos[:, None, :] * dk[:, :, None]
    sk = sin[:, None, :] * dk[:, :, None]
    rtab = np.stack([cq, sq, ck, sk], 1).astype(np.float32)
    rtab = rtab.reshape(NB, 128, 4 * 256)
    gc = np.broadcast_to(np.exp(128.0 * log_gamma)[None, :], (128, HL)).astype(np.float32)
    return dict(ident=_bf(ident), tri=_bf(tri), ones=_bf(ones), cmask4=_bf(cm4),
                dmask=_bf(dmask.reshape(128, 4 * 512)), rtab=np.ascontiguousarray(rtab),
                gc=np.ascontiguousarray(gc))


def chunkT(v):
    return np.ascontiguousarray(v.reshape(-1, 128).T)


def prep_core_inputs(inp, core, S):
    b, hh = core // 2, core % 2
    f = lambda a: np.ascontiguousarray(a, dtype=np.float32)
    x = inp["x"][b, :S]
    w_ada, b_ada = inp["w_ada"][0], inp["b_ada"][0]
    w_in, w_out = inp["w_in"][0], inp["w_out"][0]
    hs = slice(hh * 512, (hh + 1) * 512)
    grp = lambda gi: w_in[:, gi * 1024:(gi + 1) * 1024][:, hs]
    w_in_c = np.concatenate([grp(0), grp(1), grp(3), grp(2), grp(4), grp(5), grp(6), grp(7)], 1)
    rows = []
    for i in range(8):
        for r in range(2):
            gh = r * HL + (i % HL)
            base = gh * 128 if i < HL else 1024 + gh * 128
            rows.append(w_out[base:base + 128, hh * 1024:(hh + 1) * 1024])
    w_out_c = np.concatenate(rows, 0)
    d = dict(
        x=f(x), xres=f(x[:, hh * 1024:(hh + 1) * 1024]), cT=f(chunkT(inp["c"][b])),
        w_ada_ss=f(w_ada[:, 0:4096]), w_ada_g=f(w_ada[:, 4096 + hh * 1024:4096 + (hh + 1) * 1024]),
        b_ss=f(b_ada[0:4096][None, :]), b_g=f(b_ada[4096 + hh * 1024:4096 + (hh + 1) * 1024][None, :]),
        ng=f(chunkT(inp["norm_gain"][0])), w_in_c=f(w_in_c),
        gq=f(inp["sb_q_gain"][0][:, None]), gk=f(inp["sb_k_gain"][0][:, None]),
        rgain=f(np.broadcast_to(inp["ret_norm_gain"][0][hs][None, :], (128, 512))),
        w_out_c=f(w_out_c),
    )
    d.update(const_tables(S, hh))
    return d


_PROG = {}


def kernel(x, c, w_ada, b_ada, norm_gain, w_in, sb_q_gain, sb_k_gain, ret_norm_gain, w_out):
    inp = dict(x=np.asarray(x), c=np.asarray(c), w_ada=np.asarray(w_ada), b_ada=np.asarray(b_ada),
               norm_gain=np.asarray(norm_gain), w_in=np.asarray(w_in), sb_q_gain=np.asarray(sb_q_gain),
               sb_k_gain=np.asarray(sb_k_gain), ret_norm_gain=np.asarray(ret_norm_gain), w_out=np.asarray(w_out))
    B, S, _ = inp["x"].shape
    n = 2 * B
    if S not in _PROG:
        _PROG[S] = build_program(S, n_cores=n, gather=True)
    nc = _PROG[S]
    in_maps = [prep_core_inputs(inp, core, S) for core in range(n)]
    res = run_bass_kernel_spmd(nc, in_maps, core_ids=list(range(n)))
    out = np.empty((B, S, D), np.float32)
    for core in range(n):
        b, hh = core // 2, core % 2
        out[b, :, hh * 1024:(hh + 1) * 1024] = res.results[core]["out"]
    return out


def phase_a1(g):
    nc, S, NT, NB = g.nc, g.S, g.NT, g.NB
    with ExitStack() as es:
        sb = lambda n, s, d: g.sb("a1_" + n, s, d, es)
        ps = lambda n, s, d: g.ps("a1_" + n, s, d, es)
        sem = lambda n: g.sem("a1_" + n, es)
        Wb = g.a1_Wb
        xin = sb("xin", [128, 3, 2048], F32)
        junk = sb("junk", [128, 2048], BF16)
        xs = sb("xs", [128, 8, 2048], BF16)
        hT = sb("hT", [128, 2, KC, 512], BF16)
        sqb = sb("sqb", [128, 2, 512], BF16)
        lnb = sb("lnb", [128, 2, 512], F32)
        rsb = sb("rsb", [128, 2, 512], F32)
        stg = sb("stg", [128, 4, 512], BF16)
        vst = sb("vst", [128, 2, 512], BF16)
        ssq = sb("ssq", [128, NB], F32)
        lnx = sb("lnx", [128, NB], F32)
        rstd = sb("rstd", [128, NB], F32)
        tp = [ps(f"tp{i}", [128, 1024], BF16) for i in range(2)]
        pp = [ps(f"pp{i}", [128, 512], F32) for i in range(3)]
        mp = ps("mp", [128, 512], F32)
        pv = [ps(f"pv{i}", [128, 512], F32) for i in range(2)]
        s_xin = [sem(f"xin{i}") for i in range(3)]
        s_xs, s_tp, s_hT, s_pp, s_sq, s_ms, s_ln, s_rs = (sem(n) for n in
                                                         ("xs", "tp", "hT", "pp", "sq", "ms", "ln", "rs"))
        s_stgf, s_pv, s_vstf, s_act = sem("stgf"), sem("pv"), sem("vstf"), sem("act")
        s_stgd = [sem(f"stgd{i}") for i in range(4)]
        s_vstd = [sem(f"vstd{i}") for i in range(2)]
        s_hsp = [sem(f"hsp{i}") for i in range(2)]
        AB, gqs, gks, ident, ones = g.AB, g.gqs, g.gks, g.ident, g.ones
        dests = [g.qT_s] * 4 + [g.kT_s] * 4 + [g.zg_s] * 4

        with nc.Block() as block:
            @block.sync
            def _(e):
                for gb in range(NB):
                    if gb >= 3:
                        e.wait_ge(s_xs, gb - 2)
                    e.dma_start(out=xin[:, gb % 3, :], in_=g.x_d[gb * 128:(gb + 1) * 128, :]
                                ).then_inc(s_xin[gb % 3], 16)

            @block.scalar
            def _(a):
                na = [0]

                def selfsync(ins):
                    ins.then_inc(s_act, 1)
                    na[0] += 1
                    a.wait_ge(s_act, na[0])

                def blocks(T):
                    for b4 in range(4):
                        gb = 4 * T + b4
                        a.wait_ge(s_xin[gb % 3], 16 * (gb // 3 + 1))
                        selfsync(a.activation(junk[:], xin[:, gb % 3, :], AF.Square, accum_out=ssq[:, gb:gb + 1]))
                        selfsync(a.activation(lnx[:, gb:gb + 1], ssq[:, gb:gb + 1], AF.Ln, bias=EPS, scale=1.0 / D))
                        selfsync(a.activation(rstd[:, gb:gb + 1], lnx[:, gb:gb + 1], AF.Exp, scale=-0.5))
                        if T >= 2:
                            a.wait_ge(s_tp, 16 * (T - 1))
                        a.activation(xs[:, (T % 2) * 4 + b4, :], xin[:, gb % 3, :], AF.Copy,
                                     scale=rstd[:, gb:gb + 1]).then_inc(s_xs, 1)

                def lnexp(T, c):
                    n, m = 12 * T + c, 8 * T + c
                    a.wait_ge(s_ms, m + 1)
                    a.activation(lnb[:, m % 2, :], mp[:], AF.Ln, bias=EPS, scale=1.0 / 128).then_inc(s_ln, 1)
                    a.wait_ge(s_ln, m + 1)
                    if n >= 2:
                        a.wait_ge(s_stgf, n - 1)
                    a.activation(rsb[:, n % 2, :], lnb[:, m % 2, :], AF.Exp, scale=-0.5).then_inc(s_rs, 1)

                def chunks(T):
                    for c in range(12):
                        n = 12 * T + c
                        if c < 8:
                            m = 8 * T + c
                            a.wait_ge(s_pp, n + 1)
                            if m >= 2:
                                a.wait_ge(s_ms, m - 1)
                            a.activation(sqb[:, m % 2, :], pp[n % 3][:], AF.Square).then_inc(s_sq, 1)
                            if c >= 1:
                                lnexp(T, c - 1)
                        else:
                            if c == 8:
                                lnexp(T, 7)
                            a.wait_ge(s_pp, n + 1)
                            selfsync(a.activation(lnb[:, 0, :], pp[n % 3][:], AF.Exp, scale=-1.0))
                            selfsync(a.activation(lnb[:, 1, :], lnb[:, 0, :], AF.Ln, bias=1.0))
                            if n >= 2:
                                a.wait_ge(s_stgf, n - 1)
                            a.activation(rsb[:, n % 2, :], lnb[:, 1, :], AF.Exp, scale=-1.0).then_inc(s_rs, 1)

                blocks(0)
                for T in range(NT):
                    if T + 1 < NT:
                        blocks(T + 1)
                    chunks(T)

            @block.vector
            def _(v):
                def evac(T):
                    for kc in range(KC):
                        j = 16 * T + kc
                        v.wait_ge(s_tp, j + 1)
                        if kc == 0 and T >= 2:
                            v.wait_ge(s_pv, 4 * (T - 1))
                            v.wait_ge(s_hsp[T % 2], 16 * ((T - 2) // 2 + 1))
                        v.tensor_scalar(hT[:, T % 2, kc, :], tp[j % 2][:, 0:512], AB[:, kc:kc + 1],
                                        AB[:, 16 + kc:17 + kc], ALU.mult, ALU.add).then_inc(s_hT, 1)

                def finals(T):
                    for c in range(12):
                        n = 12 * T + c
                        v.wait_ge(s_rs, n + 1)
                        if n >= 4:
                            v.wait_ge(s_stgd[n % 4], 16 * ((n - 4) // 4 + 1))
                        if c < 8:
                            v.scalar_tensor_tensor(stg[:, n % 4, :], pp[n % 3][:], (gqs if c < 4 else gks)[:, 0:1],
                                                   rsb[:, n % 2, :], ALU.mult, ALU.mult).then_inc(s_stgf, 1)
                        else:
                            v.tensor_tensor(stg[:, n % 4, :], pp[n % 3][:], rsb[:, n % 2, :], ALU.mult
                                            ).then_inc(s_stgf, 1)
                    for b4 in range(4):
                        gb = 4 * T + b4
                        v.wait_ge(s_pv, gb + 1)
                        if gb >= 2:
                            v.wait_ge(s_vstd[gb % 2], 16 * ((gb - 2) // 2 + 1))
                        v.tensor_copy(vst[:, gb % 2, :], pv[gb % 2][:]).then_inc(s_vstf, 1)

                evac(0)
                for T in range(NT):
                    if T + 1 < NT:
                        evac(T + 1)
                    finals(T)

            @block.gpsimd
            def _(p):
                for T in range(NT):
                    p.wait_ge(s_hT, 16 * (T + 1))
                    p.dma_start(out=g.hT_s[T, :, :], in_=hT[:, T % 2, :, :].rearrange("p k t -> p (k t)")
                                ).then_inc(s_hsp[T % 2], 16)
                    for c in range(12):
                        n = 12 * T + c
                        p.wait_ge(s_stgf, n + 1)
                        p.dma_start(out=dests[c][c % 4, :, T * 512:(T + 1) * 512], in_=stg[:, n % 4, :]
                                    ).then_inc(s_stgd[n % 4], 16)
                    for b4 in range(4):
                        gb = 4 * T + b4
                        p.wait_ge(s_vstf, gb + 1)
                        p.dma_start(out=g.v_s.ap()[:, :, gb * 128:(gb + 1) * 128].rearrange("h p d -> p h d"),
                                    in_=vst[:, gb % 2, :].rearrange("p (h d) -> p h d", h=4)
                                    ).then_inc(s_vstd[gb % 2], 16)
                ntot = 12 * NT
                for i in range(4):
                    cnt = len([n for n in range(ntot) if n % 4 == i])
                    p.wait_ge(s_stgd[i], 16 * cnt)
                for i in range(2):
                    p.wait_ge(s_vstd[i], 16 * len([x for x in range(NB) if x % 2 == i]))
                    p.wait_ge(s_hsp[i], 16 * len([x for x in range(NT) if x % 2 == i]))

            @block.tensor
            def _(t):
                def trans(T):
                    t.wait_ge(s_xs, 4 * T + 4)
                    for kc in range(KC):
                        j = 16 * T + kc
                        if j >= 2:
                            t.wait_ge(s_hT, j - 1)
                        for b4 in range(4):
                            ins = t.transpose(tp[j % 2][:, b4 * 128:(b4 + 1) * 128],
                                              xs[:, (T % 2) * 4 + b4, kc * 128:(kc + 1) * 128], ident[:])
                        ins.then_inc(s_tp, 1)

                def ones_mm(T, c):
                    m = 8 * T + c
                    t.wait_ge(s_sq, m + 1)
                    if m >= 1:
                        t.wait_ge(s_ln, m)
                    t.matmul(mp[:], ones[:], sqb[:, m % 2, :], start=True, stop=True).then_inc(s_ms, 1)

                def inproj(T):
                    t.wait_ge(s_hT, 16 * (T + 1))
                    for c in range(12):
                        n = 12 * T + c
                        if n >= 3:
                            t.wait_ge(s_stgf, n - 2)
                        for kc in range(KC):
                            ins = t.matmul(pp[n % 3][:], Wb[:, kc, c * 128:(c + 1) * 128], hT[:, T % 2, kc, :],
                                           start=(kc == 0), stop=(kc == KC - 1))
                        ins.then_inc(s_pp, 1)
                        if 1 <= c <= 8:
                            ones_mm(T, c - 1)
                    for b4 in range(4):
                        gb = 4 * T + b4
                        if gb >= 2:
                            t.wait_ge(s_vstf, gb - 1)
                        for kc in range(KC):
                            ins = t.matmul(pv[gb % 2][:], hT[:, T % 2, kc, b4 * 128:(b4 + 1) * 128],
                                           Wb[:, kc, 1536:2048], start=(kc == 0), stop=(kc == KC - 1))
                        ins.then_inc(s_pv, 1)

                trans(0)
                for T in range(NT):
                    if T + 1 < NT:
                        trans(T + 1)
                    inproj(T)


def phase_a2(g):
    nc, S, NT, NB = g.nc, g.S, g.NT, g.NB
    TPH = NT // 2
    with ExitStack() as es:
        sb = lambda n, s, d: g.sb("a2_" + n, s, d, es)
        ps = lambda n, s, d: g.ps("a2_" + n, s, d, es)
        sem = lambda n: g.sem("a2_" + n, es)
        Wb = sb("Wb", [128, KC, 2048], BF16)
        wst = sb("wst", [128, 4, 1024], F32)
        hT = sb("hT", [128, 2, KC, 512], BF16)
        rtab = sb("rtab", [128, 3, 1024], F32)
        rt = sb("rt", [128, 8, 256], F32)
        qktm = sb("qktm", [128, 4, 2, 512], BF16)
        vb = sb("vb", [128, 6, 512], BF16)
        sgt = sb("sgt", [128, 2, 512], F32)
        sig = sb("sig", [128, 2, 512], F32)
        zgb = sb("zgb", [128, 6, 512], BF16)
        qkT = sb("qkT", [128, 4, 1024], BF16)
        PT = sb("PT", [128, 3, 512], BF16)
        Tst = sb("Tst", [128, 512], F32)
        Sb = sb("Sb", [128, 4, 512], BF16)
        st6 = sb("st6", [128, 4, 6], F32)
        mv = sb("mv", [128, 4, 2], F32)
        lnr = sb("lnr", [128, 4], F32)
        rs4 = sb("rs4", [128, 2, 4], F32)
        nmr = sb("nmr", [128, 2, 4], F32)
        yb = sb("yb", [128, 2, 512], F32)
        y2 = sb("y2", [128, 512], F32)
        mixtm = sb("mixtm", [128, 2, 512], BF16)
        mTs = sb("mTs", [128, 2, 4, 512], BF16)
        ip = [ps(f"ip{i}", [128, 512], F32) for i in range(3)]
        tq = ps("tq", [128, 1024], BF16)
        sc = ps("sc", [128, 512], F32)
        dsp = ps("ds", [128, 512], F32)
        ou = ps("ou", [128, 512], F32)
        mt = ps("mt", [128, 1024], BF16)
        s_wst = [sem(f"wst{i}") for i in range(4)]
        s_wcv, s_wcg = sem("wcv"), sem("wcg")
        s_h2 = [sem(f"h2{i}") for i in range(2)]
        s_rt = [sem(f"rt{i}") for i in range(3)]
        s_mst = [sem(f"mst{i}") for i in range(2)]
        (s_ip, s_rq, s_rk, s_cq, s_ck, s_v, s_sig, s_zg, s_tq, s_tqc, s_sc, s_ds, s_pt, s_tu, s_sbc, s_ou,
         s_bn, s_mv, s_r4, s_nm, s_yd, s_y2, s_mx, s_mt, s_mtc, s_act, s_init) = (
            sem(n) for n in ("ip", "rq", "rk", "cq", "ck", "v", "sig", "zg", "tq", "tqc", "sc", "ds", "pt", "tu",
                             "sbc", "ou", "bn", "mv", "r4", "nm", "yd", "y2", "mx", "mt", "mtc", "act", "init"))
        ident, cm4, rgain, gcs = g.ident, g.cm4, g.rgain, g.gcs
        NS = NB + 6
        okb = lambda b: 0 <= b < NB

        def v4(ap2d, j):
            return ap2d.rearrange("p (h two i) -> p h two i", h=4, two=2)[:, :, j, :]

        def t4(ap2d):
            return ap2d.rearrange("p (h i) -> p h i", h=4)

        with nc.Block() as block:
            @block.sync
            def _(e):
                for i in range(2 * KC):
                    if i >= 4:
                        e.wait_ge(s_wcv if i % 2 == 0 else s_wcg, i // 2 - 1)
                    kc, hf = i // 2, i % 2
                    e.dma_start(out=wst[:, i % 4, :],
                                in_=g.win_d[kc * 128:(kc + 1) * 128, 2048 + hf * 1024:2048 + (hf + 1) * 1024]
                                ).then_inc(s_wst[i % 4], 16)

                def loads(T):
                    if T >= 2:
                        e.wait_ge(s_ip, 16 * (T - 1))
                    e.dma_start(out=hT[:, T % 2, :, :].rearrange("p k t -> p (k t)"), in_=g.hT_s[T, :, :]
                                ).then_inc(s_h2[T % 2], 16)
                    for b4 in range(4):
                        gb = 4 * T + b4
                        if gb >= 3:
                            e.wait_ge(s_rk, gb - 2)
                        e.dma_start(out=rtab[:, gb % 3, :], in_=g.rtab_d[gb, :, :]).then_inc(s_rt[gb % 3], 16)

                def stores(T):
                    e.wait_ge(s_mtc, 4 * (T + 1))
                    for h in range(4):
                        e.dma_start(out=g.mixc[4 + h][T // TPH][:, (T % TPH) * 512:(T % TPH + 1) * 512],
                                    in_=mTs[:, T % 2, h, :]).then_inc(s_mst[T % 2], 16)

                for T in range(NT):
                    loads(T)
                    if T >= 2:
                        stores(T - 2)
                for T in range(max(NT - 2, 0), NT):
                    stores(T)
                for i in range(2):
                    e.wait_ge(s_mst[i], 64 * len([x for x in range(NT) if x % 2 == i]))

            @block.tensor
            def _(t):
                def IP(b):
                    T, b4 = b // 4, b % 4
                    if b4 == 0:
                        t.wait_ge(s_h2[T % 2], 16 * (T // 2 + 1))
                    for tau in range(4):
                        u = 4 * b + tau
                        if u >= 3:
                            pu = u - 3
                            t.wait_ge((s_rq, s_rk, s_v, s_zg)[pu % 4], pu // 4 + 1)
                        for kc in range(KC):
                            ins = t.matmul(ip[u % 3][:], hT[:, T % 2, kc, b4 * 128:(b4 + 1) * 128],
                                           Wb[:, kc, tau * 512:(tau + 1) * 512], start=(kc == 0), stop=(kc == KC - 1))
                        ins.then_inc(s_ip, 1)

                def TQ(b):
                    t.wait_ge(s_cq, b + 1)
                    t.wait_ge(s_ck, b + 1)
                    if b >= 1:
                        t.wait_ge(s_tqc, b)
                    for j in range(2):
                        for h in range(4):
                            ins = t.transpose(tq[:, j * 512 + h * 128:j * 512 + (h + 1) * 128],
                                              qktm[:, b % 4, j, h * 128:(h + 1) * 128], ident[:])
                    ins.then_inc(s_tq, 1)

                def SC(b):
                    t.wait_ge(s_tqc, b + 1)
                    if b >= 1:
                        t.wait_ge(s_pt, b)
                    for h in range(4):
                        ins = t.matmul(sc[:, h * 128:(h + 1) * 128], qkT[:, b % 4, 512 + h * 128:512 + (h + 1) * 128],
                                       qkT[:, b % 4, h * 128:(h + 1) * 128], start=True, stop=True)
                    ins.then_inc(s_sc, 1)
                    t.wait_ge(s_v, b + 1)
                    if b >= 1:
                        t.wait_ge(s_tu, 4 * b)
                    for h in range(4):
                        ins = t.matmul(dsp[:, h * 128:(h + 1) * 128], qktm[:, b % 4, 1, h * 128:(h + 1) * 128],
                                       vb[:, b % 6, h * 128:(h + 1) * 128], start=True, stop=True)
                    ins.then_inc(s_ds, 1)

                def OU(b):
                    t.wait_ge(s_pt, b + 1)
                    if b >= 1:
                        t.wait_ge(s_sbc, 4 * b)
                        t.wait_ge(s_yd, 4 * b)
                    for h in range(4):
                        hsl = slice(h * 128, (h + 1) * 128)
                        ins = t.matmul(ou[:, hsl], PT[:, b % 3, hsl], vb[:, b % 6, hsl], start=True, stop=(b == 0))
                        if b >= 1:
                            ins = t.matmul(ou[:, hsl], qkT[:, b % 4, hsl], Sb[:, b % 4, hsl], start=False, stop=True)
                    ins.then_inc(s_ou, 1)

                def MT(b):
                    t.wait_ge(s_mx, b + 1)
                    if b >= 1:
                        t.wait_ge(s_mtc, b)
                    for h in range(4):
                        ins = t.transpose(mt[:, h * 128:(h + 1) * 128], mixtm[:, b % 2, h * 128:(h + 1) * 128], ident[:])
                    ins.then_inc(s_mt, 1)

                t.wait_ge(s_wcv, KC)
                t.wait_ge(s_wcg, KC)
                for s in range(NS):
                    for fn, lag in ((IP, 0), (TQ, 1), (SC, 2), (OU, 3), (MT, 5)):
                        if okb(s - lag):
                            fn(s - lag)

            @block.vector
            def _(v):
                v.memset(Tst[:], 0.0).then_inc(s_init, 1)
                for i in range(0, 2 * KC, 2):
                    v.wait_ge(s_wst[i % 4], 16 * (i // 4 + 1))
                    v.tensor_copy(Wb[:, i // 2, 0:1024], wst[:, i % 4, :]).then_inc(s_wcv, 1)

                def ZG(b):
                    v.wait_ge(s_sig, b + 1)
                    if b >= 6:
                        v.wait_ge(s_mx, b - 5)
                    v.tensor_tensor(zgb[:, b % 6, :], ip[(4 * b + 3) % 3][:], sig[:, b % 2, :], ALU.mult
                                    ).then_inc(s_zg, 1)

                def PTM(b):
                    v.wait_ge(s_sc, b + 1)
                    if b >= 3:
                        v.wait_ge(s_ou, b - 2)
                    v.tensor_tensor(PT[:, b % 3, :], sc[:], cm4[:], ALU.mult).then_inc(s_pt, 1)

                def TU(b):
                    v.wait_ge(s_ds, b + 1)
                    if b >= 1:
                        v.wait_ge(s_sbc, 4 * b)
                    for h in range(4):
                        hsl = slice(h * 128, (h + 1) * 128)
                        v.scalar_tensor_tensor(Tst[:, hsl], Tst[:, hsl], gcs[:, h:h + 1], dsp[:, hsl],
                                               ALU.mult, ALU.add).then_inc(s_tu, 1)

                def ST(b):
                    v.wait_ge(s_ou, b + 1)
                    if b >= 1:
                        v.wait_ge(s_nm, b)
                        v.wait_ge(s_r4, b)
                    for h in range(4):
                        v.bn_stats(st6[:, h, :], ou[:, h * 128:(h + 1) * 128]).then_inc(s_bn, 1)
                    v.wait_ge(s_bn, 4 * (b + 1))
                    for h in range(4):
                        v.bn_aggr(mv[:, h, :], st6[:, h, :]).then_inc(s_mv, 1)

                def RR(b, which):
                    u = 4 * b + which
                    if which == 0:
                        v.wait_ge(s_rt[b % 3], 16 * (b // 3 + 1))
                    v.wait_ge(s_ip, u + 1)
                    if b >= 1:
                        v.wait_ge((s_cq, s_ck)[which], b)
                    src = ip[u % 3][:]
                    x1, x2 = v4(src, 0), v4(src, 1)
                    ct = t4(rtab[:, b % 3, which * 512:which * 512 + 256])
                    stt = t4(rtab[:, b % 3, which * 512 + 256:which * 512 + 512])
                    o = which * 4
                    v.tensor_tensor(t4(rt[:, o + 0, :]), x1, ct, ALU.mult)
                    v.tensor_tensor(t4(rt[:, o + 1, :]), x2, stt, ALU.mult)
                    v.tensor_tensor(t4(rt[:, o + 2, :]), x1, stt, ALU.mult)
                    v.tensor_tensor(t4(rt[:, o + 3, :]), x2, ct, ALU.mult).then_inc((s_rq, s_rk)[which], 1)

                def NMR(b):
                    v.wait_ge(s_r4, b + 1)
                    if b >= 2:
                        v.wait_ge(s_yd, 4 * (b - 1))
                    v.scalar_tensor_tensor(nmr[:, b % 2, :], mv[:, :, 0], -1.0, rs4[:, b % 2, :], ALU.mult, ALU.mult
                                           ).then_inc(s_nm, 1)

                for s in range(NS):
                    if okb(s - 1):
                        ZG(s - 1)
                    if okb(s - 3):
                        PTM(s - 3)
                        TU(s - 3)
                    if okb(s - 4):
                        ST(s - 4)
                    if okb(s):
                        RR(s, 0)
                        RR(s, 1)
                    if okb(s - 4):
                        NMR(s - 4)

            @block.gpsimd
            def _(p):
                for i in range(1, 2 * KC, 2):
                    p.wait_ge(s_wst[i % 4], 16 * (i // 4 + 1))
                    p.tensor_copy(Wb[:, i // 2, 1024:2048], wst[:, i % 4, :]).then_inc(s_wcg, 1)

                def SBC(b):
                    p.wait_ge(s_tu, 4 * (b + 1))
                    if b >= 3:
                        p.wait_ge(s_ou, b - 2)
                    for h in range(4):
                        hsl = slice(h * 128, (h + 1) * 128)
                        p.tensor_scalar(Sb[:, (b + 1) % 4, hsl], Tst[:, hsl], gcs[:, h:h + 1], 1.0, ALU.mult, ALU.mult
                                        ).then_inc(s_sbc, 1)

                def CC(b, which):
                    p.wait_ge((s_rq, s_rk)[which], b + 1)
                    if b >= 4:
                        p.wait_ge(s_ds, b - 3)
                    o = which * 4
                    dst = qktm[:, b % 4, which, :]
                    p.tensor_tensor(v4(dst, 0), t4(rt[:, o + 0, :]), t4(rt[:, o + 1, :]), ALU.subtract)
                    p.tensor_tensor(v4(dst, 1), t4(rt[:, o + 2, :]), t4(rt[:, o + 3, :]), ALU.add
                                    ).then_inc((s_cq, s_ck)[which], 1)

                def MIX(b):
                    p.wait_ge(s_yd, 4 * (b + 1))
                    if b >= 1:
                        p.wait_ge(s_mx, b)
                    p.tensor_tensor(y2[:], yb[:, b % 2, :], rgain[:], ALU.mult).then_inc(s_y2, 1)
                    p.wait_ge(s_y2, b + 1)
                    p.wait_ge(s_zg, b + 1)
                    if b >= 2:
                        p.wait_ge(s_mt, b - 1)
                    p.tensor_tensor(mixtm[:, b % 2, :], y2[:], zgb[:, b % 6, :], ALU.mult).then_inc(s_mx, 1)

                for s in range(NS):
                    if okb(s - 3):
                        SBC(s - 3)
                    if okb(s):
                        CC(s, 0)
                        CC(s, 1)
                    if okb(s - 4):
                        MIX(s - 4)

            @block.scalar
            def _(a):
                na = [0]

                def selfsync(ins):
                    ins.then_inc(s_act, 1)
                    na[0] += 1
                    a.wait_ge(s_act, na[0])

                def MTC(b):
                    T, b4 = b // 4, b % 4
                    a.wait_ge(s_mt, b + 1)
                    if b4 == 0 and T >= 2:
                        a.wait_ge(s_mst[T % 2], 64 * ((T - 2) // 2 + 1))
                    a.activation(mTs[:, T % 2, :, b4 * 128:(b4 + 1) * 128], t4(mt[:, 0:512]), AF.Copy
                                 ).then_inc(s_mtc, 1)

                def R4(b):
                    a.wait_ge(s_mv, 4 * (b + 1))
                    selfsync(a.activation(lnr[:], mv[:, :, 1], AF.Ln, bias=EPS))
                    if b >= 2:
                        a.wait_ge(s_yd, 4 * (b - 1))
                    a.activation(rs4[:, b % 2, :], lnr[:], AF.Exp, scale=-0.5).then_inc(s_r4, 1)

                def NORM(b):
                    a.wait_ge(s_nm, b + 1)
                    if b >= 2:
                        a.wait_ge(s_y2, b - 1)
                    for h in range(4):
                        hsl = slice(h * 128, (h + 1) * 128)
                        a.activation(yb[:, b % 2, hsl], ou[:, hsl], AF.Identity, bias=nmr[:, b % 2, h:h + 1],
                                     scale=rs4[:, b % 2, h:h + 1]).then_inc(s_yd, 1)

                def VZ(b):
                    a.wait_ge(s_ip, 4 * b + 3)
                    if b >= 6:
                        a.wait_ge(s_ou, b - 5)
                    a.activation(vb[:, b % 6, :], ip[(4 * b + 2) % 3][:], AF.Copy).then_inc(s_v, 1)
                    a.wait_ge(s_ip, 4 * b + 4)
                    pz = ip[(4 * b + 3) % 3][:]
                    selfsync(a.activation(sgt[:, 0, :], pz, AF.Exp, scale=-1.0))
                    selfsync(a.activation(sgt[:, 1, :], sgt[:, 0, :], AF.Ln, bias=1.0))
                    if b >= 2:
                        a.wait_ge(s_zg, b - 1)
                    a.activation(sig[:, b % 2, :], sgt[:, 1, :], AF.Exp, scale=-1.0).then_inc(s_sig, 1)

                def TQC(b):
                    a.wait_ge(s_tq, b + 1)
                    if b >= 4:
                        a.wait_ge(s_ou, b - 3)
                    a.activation(qkT[:, b % 4, :], tq[:], AF.Copy).then_inc(s_tqc, 1)

                for s in range(NS):
                    if okb(s - 6):
                        MTC(s - 6)
                    if okb(s - 4):
                        R4(s - 4)
                        NORM(s - 4)
                    if okb(s):
                        VZ(s)
                    if okb(s - 1):
                        TQC(s - 1)


def phase_b(g):
    nc, S, NT, NB = g.nc, g.S, g.NT, g.NB
    TPH = NT // 2
    tiles = []
    for h in range(HL):
        for qt in range(NT):
            for kb in range(4 * qt + 3, -1, -1):
                tiles.append(dict(h=h, qt=qt, kb=kb, first=(kb == 4 * qt + 3), last=(kb == 0),
                                  r=(kb - 4 * qt if kb >= 4 * qt else None), G=h * NT + qt))
    N = len(tiles)
    NG = HL * NT
    nmask = [0] * (N + 1)
    nrup = [0] * (N + 1)
    for i, tl in enumerate(tiles):
        nmask[i + 1] = nmask[i] + (1 if tl["r"] is not None else 0)
        nrup[i + 1] = nrup[i] + (0 if tl["last"] else 1)
    last_of_G = {}
    for i, tl in enumerate(tiles):
        last_of_G[tl["G"]] = i

    with ExitStack() as es:
        sb = lambda n, s, d: g.sb("b_" + n, s, d, es)
        ps = lambda n, s, d: g.ps("b_" + n, s, d, es)
        sem = lambda n: g.sem("b_" + n, es)
        KT = sb("KT", [128, 2, S], BF16)
        VV = sb("VV", [128, 2, S], BF16)
        QT = sb("QT", [128, 2, S], BF16)
        ZG = sb("ZG", [128, 2, S], BF16)
        EE = sb("EE", [128, 6, 2, 512], BF16)
        Lp = sb("Lp", [128, 3, 512], BF16)
        W = sb("W", [128, 3, 512], BF16)
        R = sb("R", [128, 3, 512], BF16)
        Rz = sb("Rz", [128, 512], BF16)
        mst = sb("mst", [128, 2, 512], BF16)
        ZC = [ps(f"ZC{i}", [128, 1024], F32) for i in range(2)]
        Zp = lambda i: ZC[i % 2][:, 0:512]
        Cp = lambda i: ZC[(i + 1) % 2][:, 512:1024]
        E = lambda i: EE[:, i % 6, 0, :]
        EC = lambda i: EE[:, (i + 3) % 6, 1, :]
        Op = [ps(f"Op{i}", [128, 512], F32) for i in range(2)]
        s_hd = [sem(f"hd{i}") for i in range(2)]
        s_mst = [sem(f"mst{i}") for i in range(2)]
        s_qk, s_x, s_mask, s_ln, s_rup, s_cum, s_mul, s_pv, s_fin, s_init, s_hdone = (
            sem(n) for n in ("qk", "x", "mask", "ln", "rup", "cum", "mul", "pv", "fin", "init", "hdone"))
        s_wod = [sem(f"wod{i}") for i in range(2)]
        s_woc = sem("woc")
        g.wo_prefetched = True
        first_of_head = {}
        for i_, tl_ in enumerate(tiles):
            first_of_head.setdefault(tl_["h"], i_)

        def cc(p, i_chunk):
            for hf in range(2):
                p.collective_compute("AllGather", ALU.bypass, replica_groups=g.rgroups,
                                     ins=[g.mixc[i_chunk][hf].ap().opt()], outs=[g.gath[i_chunk][hf].ap().opt()]
                                     ).then_inc(g.s_cc, 1)
        tri, ones, dmask = g.tri, g.ones, g.dmask
        NS = N + 8
        ok = lambda i: 0 <= i < N

        with nc.Block() as block:
            @block.sync
            def _(e):
                def load(h):
                    if h >= 2:
                        e.wait_ge(s_fin, (h - 1) * NT)
                    for dst, src in ((KT, g.kT_s), (VV, g.v_s), (QT, g.qT_s), (ZG, g.zg_s)):
                        e.dma_start(out=dst[:, h % 2, :], in_=src[h, :, :]).then_inc(s_hd[h % 2], 16)

                def stores(h):
                    for qt in range(NT):
                        G = h * NT + qt
                        e.wait_ge(s_fin, G + 1)
                        e.dma_start(out=g.mixc[h][qt // TPH][:, (qt % TPH) * 512:(qt % TPH + 1) * 512],
                                    in_=mst[:, G % 2, :]).then_inc(s_mst[G % 2], 16)

                load(0)
                if HL > 1:
                    load(1)
                for kc in range(KC):
                    if kc >= 2:
                        e.wait_ge(s_woc, kc - 1)
                    e.dma_start(out=g.d_wst[:, kc % 2, :], in_=g.wout_d[kc * 128:(kc + 1) * 128, :]
                                ).then_inc(s_wod[kc % 2], 16)
                for h in range(HL):
                    stores(h)
                    for i in range(2):
                        e.wait_ge(s_mst[i], 16 * len([x for x in range((h + 1) * NT) if x % 2 == i]))
                    e.nop().then_inc(s_hdone, 1)
                    if h + 2 < HL:
                        load(h + 2)

            @block.tensor
            def _(t):
                def QK(i):
                    tl = tiles[i]
                    h, qt, kb = tl["h"], tl["qt"], tl["kb"]
                    if tl["first"] and qt == 0:
                        t.wait_ge(s_hd[h % 2], 64 * (h // 2 + 1))
                    if i >= 2:
                        t.wait_ge(s_x, i - 1)
                    t.matmul(Zp(i), KT[:, h % 2, kb * 128:(kb + 1) * 128], QT[:, h % 2, qt * 512:(qt + 1) * 512],
                             start=True, stop=True).then_inc(s_qk, 1)

                def CUM(i):
                    tl = tiles[i]
                    t.wait_ge(s_ln, i + 1)
                    if i >= 2:
                        t.wait_ge(s_x, i + 2)
                    ins = t.matmul(Cp(i), tri[:], Lp[:, i % 3, :], start=True, stop=tl["first"])
                    if not tl["first"]:
                        t.wait_ge(s_rup, nrup[i])
                        ins = t.matmul(Cp(i), ones[:], R[:, i % 3, :], start=False, stop=True)
                    ins.then_inc(s_cum, 1)

                def PV(i):
                    tl = tiles[i]
                    h, kb, G = tl["h"], tl["kb"], tl["G"]
                    t.wait_ge(s_mul, i + 1)
                    if tl["first"] and G >= 2:
                        t.wait_ge(s_fin, G - 1)
                    t.matmul(Op[G % 2][:], VV[:, h % 2, kb * 128:(kb + 1) * 128], W[:, i % 3, :],
                             start=tl["first"], stop=tl["last"]).then_inc(s_pv, 1)

                for s in range(NS):
                    if ok(s):
                        QK(s)
                    if ok(s - 3):
                        CUM(s - 3)
                    if ok(s - 6):
                        PV(s - 6)

            @block.scalar
            def _(a):
                def XEXP(s_):
                    i1, i4 = s_ - 1, s_ - 4
                    if ok(i1):
                        a.wait_ge(s_qk, i1 + 1)
                        if i1 >= 6:
                            a.wait_ge(s_mul, i1 - 5)
                    if ok(i4):
                        a.wait_ge(s_cum, i4 + 1)
                        if i4 >= 3:
                            a.wait_ge(s_mul, i4 - 2)
                    if ok(i1) and ok(i4):
                        a.activation(EE[:, i1 % 6, :, :].rearrange("p a t -> p (a t)"), ZC[i1 % 2][:], AF.Exp, scale=-1.0
                                     ).then_inc(s_x, 1)
                    elif ok(i1):
                        a.activation(E(i1), Zp(i1), AF.Exp, scale=-1.0).then_inc(s_x, 1)
                    else:
                        a.activation(EC(i4), Cp(i4), AF.Exp, scale=-1.0).then_inc(s_x, 1)

                def LN(i):
                    tl = tiles[i]
                    a.wait_ge(s_x, i + 1)
                    if tl["r"] is not None:
                        a.wait_ge(s_mask, nmask[i + 1])
                    if i >= 3:
                        a.wait_ge(s_cum, i - 2)
                        a.wait_ge(s_rup, nrup[i - 2])
                    a.activation(Lp[:, i % 3, :], E(i), AF.Ln, bias=1.0).then_inc(s_ln, 1)

                for s in range(NS):
                    if ok(s - 1) or ok(s - 4):
                        XEXP(s)
                    if ok(s - 2):
                        LN(s - 2)

            @block.gpsimd
            def _(p):
                p.memset(Rz[:], 0.0).then_inc(s_init, 1)
                p.wait_ge(s_init, 1)
                if g.gather:
                    for i_chunk in range(HL, 2 * HL):
                        cc(p, i_chunk)

                def MASK(i):
                    tl = tiles[i]
                    if tl["r"] is None:
                        return
                    p.wait_ge(s_x, i + 1)
                    p.tensor_tensor(E(i), E(i), dmask[:, tl["r"], :], ALU.mult).then_inc(s_mask, 1)

                def RUP(i):
                    tl = tiles[i]
                    if tl["last"]:
                        return
                    p.wait_ge(s_ln, i + 1)
                    if i >= 2:
                        p.wait_ge(s_cum, i - 1)
                    p.wait_ge(s_rup, nrup[i])
                    src = Rz[:] if tl["first"] else R[:, i % 3, :]
                    p.tensor_tensor(R[:, (i + 1) % 3, :], src, Lp[:, i % 3, :], ALU.add).then_inc(s_rup, 1)

                for s in range(NS):
                    if ok(s - 1):
                        MASK(s - 1)
                    if ok(s - 3):
                        RUP(s - 3)
                    if g.gather:
                        for h in range(HL - 1):
                            if s - 3 == min(first_of_head[h + 1] + 24, N - 1):
                                p.wait_ge(s_hdone, h + 1)
                                cc(p, h)

            @block.vector
            def _(v):
                for kc in range(KC):
                    v.wait_ge(s_wod[kc % 2], 16 * (kc // 2 + 1))
                    v.tensor_copy(g.d_Wo[:, kc, :], g.d_wst[:, kc % 2, :]).then_inc(s_woc, 1)
                v.wait_ge(s_woc, KC)

                def MUL(i):
                    v.wait_ge(s_x, i + 4)
                    if i >= 3:
                        v.wait_ge(s_pv, i - 2)
                    v.tensor_tensor(W[:, i % 3, :], E(i), EC(i), ALU.mult).then_inc(s_mul, 1)

                def FIN(G):
                    h, qt = G // NT, G % NT
                    v.wait_ge(s_pv, last_of_G[G] + 1)
                    if G >= 2:
                        v.wait_ge(s_mst[G % 2], 16 * ((G - 2) // 2 + 1))
                    v.tensor_tensor(mst[:, G % 2, :], Op[G % 2][:], ZG[:, h % 2, qt * 512:(qt + 1) * 512], ALU.mult
                                    ).then_inc(s_fin, 1)

                for s in range(NS):
                    if ok(s - 5):
                        MUL(s - 5)
                    if ok(s - 7) and tiles[s - 7]["last"]:
                        FIN(tiles[s - 7]["G"])


def phase_cd(g):
    nc, S, NT, NB = g.nc, g.S, g.NT, g.NB
    TPH = NT // 2
    gather = g.gather
    with ExitStack() as es:
        sb = lambda n, s, d: g.sb("d_" + n, s, d, es)
        ps = lambda n, s, d: g.ps("d_" + n, s, d, es)
        sem = lambda n: g.sem("d_" + n, es)
        Wo, wst = g.d_Wo, g.d_wst
        pre = g.wo_prefetched
        mg = sb("mg", [128, 2, KC, 512], BF16)
        xr = sb("xr", [128, 3, 1024], F32)
        yt = sb("yt", [128, 2, 512], F32)
        ost = sb("ost", [128, 3, 1024], F32)
        po = [ps(f"po{i}", [128, 512], F32) for i in range(4)]
        s_wst = [sem(f"wst{i}") for i in range(2)]
        s_wcv, s_wcg = sem("wcv"), sem("wcg")
        s_mg = [sem(f"mg{i}") for i in range(2)]
        s_xr = [sem(f"xr{i}") for i in range(3)]
        s_ost = [sem(f"ost{i}") for i in range(3)]
        s_cc = g.s_cc
        s_po, s_y1, s_y2 = sem("po"), sem("y1"), sem("y2")
        gate_bc = g.gate_bc
        NCC = 16

        with nc.Block() as block:
            @block.sync
            def _(e):
                for kc in range(0 if pre else KC):
                    if kc >= 2:
                        e.wait_ge(s_wcv if kc % 2 == 0 else s_wcg, kc // 2)
                    e.dma_start(out=wst[:, kc % 2, :], in_=g.wout_d[kc * 128:(kc + 1) * 128, :]
                                ).then_inc(s_wst[kc % 2], 16)
                if gather:
                    e.wait_ge(s_cc, NCC)
                    if g.gdump is not None:
                        s_gd = g.sem("d_gd")
                        for i in range(8):
                            for hf2 in range(2):
                                for r2 in range(2):
                                    e.dma_start(out=g.gdump[2 * i + hf2, r2 * 128:(r2 + 1) * 128, :],
                                                in_=g.gath[i][hf2][r2 * 128:(r2 + 1) * 128, :]).then_inc(s_gd, 16)
                        e.wait_ge(s_gd, 16 * 32)
                for T in range(NT):
                    hf, tt = T // TPH, T % TPH
                    if T >= 2:
                        e.wait_ge(s_po, 8 * (T - 1))
                    for i in range(8):
                        if gather:
                            for r in range(2):
                                e.dma_start(out=mg[:, T % 2, 2 * i + r, :],
                                            in_=g.gath[i][hf][r * 128:(r + 1) * 128, tt * 512:(tt + 1) * 512]
                                            ).then_inc(s_mg[T % 2], 16)
                        else:
                            for r in range(2):
                                e.dma_start(out=mg[:, T % 2, 2 * i + r, :], in_=g.mixc[i][hf][:, tt * 512:(tt + 1) * 512]
                                            ).then_inc(s_mg[T % 2], 16)
                    for b4 in range(4):
                        gb = 4 * T + b4
                        if gb >= 3:
                            e.wait_ge(s_y2, 2 * gb - 4)
                        e.dma_start(out=xr[:, gb % 3, :], in_=g.xres_d[gb * 128:(gb + 1) * 128, :]
                                    ).then_inc(s_xr[gb % 3], 16)

            @block.tensor
            def _(t):
                if not pre:
                    t.wait_ge(s_wcv, KC // 2)
                    t.wait_ge(s_wcg, KC // 2)
                for T in range(NT):
                    t.wait_ge(s_mg[T % 2], 256 * (T // 2 + 1))
                    for b4 in range(4):
                        gb = 4 * T + b4
                        for nh in range(2):
                            idx = 2 * gb + nh
                            if idx >= 4:
                                t.wait_ge(s_y1, idx - 3)
                            chs = range(KC) if gather else range(0, KC, 2)
                            for n_, ch in enumerate(chs):
                                ins = t.matmul(po[idx % 4][:], mg[:, T % 2, ch, b4 * 128:(b4 + 1) * 128],
                                               Wo[:, ch, nh * 512:(nh + 1) * 512],
                                               start=(n_ == 0), stop=(n_ == len(chs) - 1))
                            ins.then_inc(s_po, 1)

            @block.vector
            def _(v):
                for kc in range(0, 0 if pre else KC, 2):
                    v.wait_ge(s_wst[0], 16 * (kc // 2 + 1))
                    v.tensor_copy(Wo[:, kc, :], wst[:, 0, :]).then_inc(s_wcv, 1)
                for idx in range(2 * NB):
                    nh = idx % 2
                    v.wait_ge(s_po, idx + 1)
                    if idx >= 2:
                        v.wait_ge(s_y2, idx - 1)
                    v.tensor_tensor(yt[:, idx % 2, :], po[idx % 4][:], gate_bc[:, nh * 512:(nh + 1) * 512], ALU.mult
                                    ).then_inc(s_y1, 1)

            @block.gpsimd
            def _(p):
                if gather:
                    for hf in range(2):
                        p.collective_compute("AllGather", ALU.bypass, replica_groups=g.rgroups,
                                             ins=[g.mixc[HL - 1][hf].ap().opt()], outs=[g.gath[HL - 1][hf].ap().opt()]
                                             ).then_inc(s_cc, 1)
                for kc in range(1, 1 if pre else KC, 2):
                    p.wait_ge(s_wst[1], 16 * (kc // 2 + 1))
                    p.tensor_copy(Wo[:, kc, :], wst[:, 1, :]).then_inc(s_wcg, 1)
                for gb in range(NB):
                    for nh in range(2):
                        idx = 2 * gb + nh
                        p.wait_ge(s_y1, idx + 1)
                        if nh == 0:
                            p.wait_ge(s_xr[gb % 3], 16 * (gb // 3 + 1))
                            if gb >= 3:
                                p.wait_ge(s_ost[gb % 3], 16 * ((gb - 3) // 3 + 1))
                        p.tensor_tensor(ost[:, gb % 3, nh * 512:(nh + 1) * 512], yt[:, idx % 2, :],
                                        xr[:, gb % 3, nh * 512:(nh + 1) * 512], ALU.add).then_inc(s_y2, 1)
                    p.wait_ge(s_y2, 2 * gb + 2)
                    p.dma_start(out=g.out_d[gb * 128:(gb + 1) * 128, :], in_=ost[:, gb % 3, :]
                                ).then_inc(s_ost[gb % 3], 16)
                for i in range(3):
                    p.wait_ge(s_ost[i], 16 * len([x for x in range(NB) if x % 3 == i]))
```

```python
import math
from contextlib import ExitStack

import numpy as np
import ml_dtypes

import concourse.bass as bass
import concourse.mybir as mybir
from concourse.bass_utils import run_bass_kernel_spmd

F32 = mybir.dt.float32
BF16 = mybir.dt.bfloat16
AF = mybir.ActivationFunctionType
ALU = mybir.AluOpType
AX = mybir.AxisListType

D = 2048
KC = D // 128
HL = 4
EPS = 1e-6
RSQ_D = 1.0 / math.sqrt(128.0)


class G:
    pass


def build_program(S, n_cores=8, gather=True, phases="0ARBC", dbg=False):
    NT = S // 512
    NB = S // 128
    HALF = S // 2
    nc = bass.Bass("TRN2", target_bir_lowering=False)
    g = G()
    g.gather = gather

    def din(name, shape, dt=F32):
        return nc.dram_tensor(name, list(shape), dt, kind="ExternalInput")

    x_d = din("x", [S, D])
    xres_d = din("xres", [S, 1024])
    cT_d = din("cT", [128, KC])
    wss_d = din("w_ada_ss", [D, 4096])
    wg_d = din("w_ada_g", [D, 1024])
    bss_d = din("b_ss", [1, 4096])
    bg_d = din("b_g", [1, 1024])
    ng_d = din("ng", [128, KC])
    win_d = din("w_in_c", [D, 4096])
    gq_d = din("gq", [128, 1])
    gk_d = din("gk", [128, 1])
    rg_d = din("rgain", [128, 512])
    wout_d = din("w_out_c", [D, 1024])
    ident_d = din("ident", [128, 128], BF16)
    tri_d = din("tri", [128, 128], BF16)
    ones_d = din("ones", [128, 128], BF16)
    cm4_d = din("cmask4", [128, 512], BF16)
    dmask_d = din("dmask", [128, 4 * 512], BF16)
    rtab_d = din("rtab", [NB, 128, 4 * 256])
    gc_d = din("gc", [128, HL])
    out_d = nc.dram_tensor("out", [S, 1024], F32, kind="ExternalOutput")

    def dscr(name, shape, dt, d=dbg):
        if d:
            return nc.dram_tensor(name, list(shape), dt, kind="ExternalOutput")
        return nc.dram_tensor(name, list(shape), dt)

    hT_s = dscr("hT_s", [NT, 128, KC * 512], BF16)
    qT_s = dscr("qT_s", [HL, 128, S], BF16)
    kT_s = dscr("kT_s", [HL, 128, S], BF16)
    zg_s = dscr("zg_s", [HL, 128, S], BF16)
    v_s = dscr("v_s", [HL, 128, NB * 128], BF16)
    mixc = [[dscr(f"mixc_{i}_{hf}", [128, HALF], BF16, dbg and not gather) for hf in range(2)] for i in range(8)]
    if gather:
        gath = [[dscr(f"gath_{i}_{hf}", [256, HALF], BF16, False) for hf in range(2)] for i in range(8)]
    rgroups = [[2 * i, 2 * i + 1] for i in range(n_cores // 2)]
    gdump = nc.dram_tensor("gdump", [16, 256, HALF], BF16, kind="ExternalOutput") if (dbg and gather) else None

    es0 = ExitStack()
    with es0:
        def sb(name, shape, dt, es=es0):
            return es.enter_context(nc.sbuf_tensor("sb_" + name, list(shape), dt))

        def ps(name, shape, dt, es=es0):
            return es.enter_context(nc.psum_tensor("pt_" + name, list(shape), dt))

        phase_sems = []
        s_cc = nc.alloc_semaphore("s_cc_persist")

        def sem(name, es=None):
            h = nc.alloc_semaphore(name)
            phase_sems.append(h)
            return h

        def end_phase():
            nc.clear_and_free_semaphores(list(phase_sems))
            nc.all_engine_barrier()
            phase_sems.clear()

        ident = sb("ident", [128, 128], BF16)
        tri = sb("tri", [128, 128], BF16)
        ones = sb("ones", [128, 128], BF16)
        cm4 = sb("cm4", [128, 512], BF16)
        dmask = sb("dmask", [128, 4, 512], BF16)
        AB = sb("AB", [128, 32], F32)
        gate_bc = sb("gate_bc", [128, 1024], F32)
        gqs = sb("gqs", [128, 1], F32)
        gks = sb("gks", [128, 1], F32)
        rgain = sb("rgain", [128, 512], F32)
        gcs = sb("gcs", [128, HL], F32)
        onesf = sb("onesf", [1, 128], F32)

        es_a1w = ExitStack()
        a1_Wb = sb("a1_Wb", [128, KC, 2048], BF16, es_a1w)
        a1_wst = sb("a1_wst", [128, 2, 2048], F32, es_a1w)
        with ExitStack() as es:
            cT = sb("cT", [128, KC], F32, es)
            t0 = sb("t0", [128, KC], F32, es)
            t1 = sb("t1", [128, KC], F32, es)
            scv = sb("scv", [128, KC], F32, es)
            bssr = sb("bssr", [1, 4096], F32, es)
            rowss = sb("rowss", [1, 4096], F32, es)
            ngt = sb("ngt", [128, KC], F32, es)
            modsb = sb("modsb", [128, 32], F32, es)
            bgt = sb("bgt", [1, 1024], F32, es)
            grow = sb("grow", [1, 1024], F32, es)
            wss = sb("wss", [128, 3, 4096], F32, es)
            wgt = sb("wgt", [128, 2, 1024], F32, es)
            one1 = sb("one1", [1, 1], F32, es)
            pr = [ps(f"p0r{i}", [128, 512], F32, es) for i in range(8)]
            s_cst = sem("s_cst", es)
            s_wfull = [sem(f"s_wfull{i}", es) for i in range(3)]
            s_gfull = [sem(f"s_gfull{i}", es) for i in range(2)]
            s_wfree, s_gfree = sem("s_wfree", es), sem("s_gfree", es)
            s_a1w = [sem(f"s_a1w{i}", es) for i in range(2)]
            s_a1cv, s_a1cg = sem("s_a1cv", es), sem("s_a1cg", es)
            s_a, s_b, s_c, s_d, s_e, s_f, s_z, s_r, s_t = (sem("s_p0" + n, es) for n in "abcdefzrt")
            NCST = 13
            with nc.Block() as block:
                @block.sync
                def _(e):
                    for dst, src in ((ident[:], ident_d[:, :]), (tri[:], tri_d[:, :]), (ones[:], ones_d[:, :]),
                                     (cm4[:], cm4_d[:, :]),
                                     (dmask[:].rearrange("p r t -> p (r t)"), dmask_d[:, :]),
                                     (cT[:], cT_d[:, :]), (bssr[:], bss_d[:, :]), (ngt[:], ng_d[:, :]),
                                     (gqs[:], gq_d[:, :]), (gks[:], gk_d[:, :]), (rgain[:], rg_d[:, :]),
                                     (gcs[:], gc_d[:, :]), (bgt[:], bg_d[:, :])):
                        e.dma_start(out=dst, in_=src).then_inc(s_cst, 16)
                    for kc in range(KC):
                        if kc >= 3:
                            e.wait_ge(s_wfree, kc - 2)
                        e.dma_start(out=wss[:, kc % 3, :], in_=wss_d[kc * 128:(kc + 1) * 128, :]).then_inc(s_wfull[kc % 3], 16)
                        if kc >= 2:
                            e.wait_ge(s_a1cv if kc % 2 == 0 else s_a1cg, kc // 2)
                        e.dma_start(out=a1_wst[:, kc % 2, :], in_=win_d[kc * 128:(kc + 1) * 128, 0:2048]
                                    ).then_inc(s_a1w[kc % 2], 16)
                    for kc in range(KC):
                        if kc >= 2:
                            e.wait_ge(s_gfree, kc - 1)
                        e.dma_start(out=wgt[:, kc % 2, :], in_=wg_d[kc * 128:(kc + 1) * 128, :]).then_inc(s_gfull[kc % 2], 16)

                @block.scalar
                def _(a):
                    a.wait_ge(s_cst, 16 * NCST)
                    a.activation(t0[:], cT[:], AF.Exp, scale=-1.0).then_inc(s_a, 1)
                    a.wait_ge(s_a, 1)
                    a.activation(t1[:], t0[:], AF.Ln, bias=1.0).then_inc(s_a, 1)
                    a.wait_ge(s_a, 2)
                    a.activation(t0[:], t1[:], AF.Exp, scale=-1.0).then_inc(s_a, 1)

                @block.gpsimd
                def _(p):
                    for kc in range(1, KC, 2):
                        p.wait_ge(s_a1w[1], 16 * (kc // 2 + 1))
                        p.tensor_copy(a1_Wb[:, kc, :], a1_wst[:, 1, :]).then_inc(s_a1cg, 1)
                    p.wait_ge(s_a1cg, KC // 2)

                @block.vector
                def _(v):
                    v.memset(onesf[:], 1.0).then_inc(s_z, 1)
                    v.memset(one1[:], 1.0).then_inc(s_z, 1)
                    v.wait_ge(s_a, 3)
                    v.tensor_tensor(scv[:], cT[:], t0[:], ALU.mult).then_inc(s_b, 1)
                    v.tensor_scalar(gqs[:], gqs[:], -RSQ_D, None, ALU.mult).then_inc(s_z, 1)
                    for kc in range(0, KC, 2):
                        v.wait_ge(s_a1w[0], 16 * (kc // 2 + 1))
                        v.tensor_copy(a1_Wb[:, kc, :], a1_wst[:, 0, :]).then_inc(s_a1cv, 1)
                    v.wait_ge(s_c, 1)
                    for i in range(8):
                        v.tensor_tensor(rowss[:, i * 512:(i + 1) * 512], pr[i][0:1, :], bssr[:, i * 512:(i + 1) * 512],
                                        ALU.add).then_inc(s_r, 1)
                    v.wait_ge(s_d, 1)
                    for i in range(2):
                        v.tensor_tensor(grow[:, i * 512:(i + 1) * 512], pr[i][0:1, :], bgt[:, i * 512:(i + 1) * 512],
                                        ALU.add).then_inc(s_e, 1)
                    v.wait_ge(s_t, 1)
                    v.tensor_copy(modsb[:], pr[2][:, 0:32]).then_inc(s_f, 1)
                    v.wait_ge(s_f, 1)
                    v.tensor_scalar(AB[:, 0:16], modsb[:, 16:32], 1.0, None, ALU.add).then_inc(s_f, 1)
                    v.wait_ge(s_f, 2)
                    v.tensor_tensor(AB[:, 0:16], AB[:, 0:16], ngt[:], ALU.mult).then_inc(s_f, 1)
                    v.tensor_copy(AB[:, 16:32], modsb[:, 0:16]).then_inc(s_f, 1)
                    v.wait_ge(s_t, 3)
                    for i in range(2):
                        v.tensor_copy(gate_bc[:, i * 512:(i + 1) * 512], pr[3 + i][:]).then_inc(s_f, 1)
                    v.wait_ge(s_f, 6)
                    v.wait_ge(s_z, 3)
                    v.wait_ge(s_a1cv, KC // 2)

                @block.tensor
                def _(t):
                    t.wait_ge(s_b, 1)
                    t.wait_ge(s_z, 2)
                    for kc in range(KC):
                        t.wait_ge(s_wfull[kc % 3], 16 * (kc // 3 + 1))
                        for i in range(8):
                            ins = t.matmul(pr[i][0:1, :], scv[:, kc:kc + 1], wss[:, kc % 3, i * 512:(i + 1) * 512],
                                           start=(kc == 0), stop=(kc == KC - 1))
                        ins.then_inc(s_wfree, 1)
                    t.wait_ge(s_wfree, KC)
                    t.drain().then_inc(s_c, 1)
                    t.wait_ge(s_r, 8)
                    for kc in range(KC):
                        t.wait_ge(s_gfull[kc % 2], 16 * (kc // 2 + 1))
                        for i in range(2):
                            ins = t.matmul(pr[i][0:1, :], scv[:, kc:kc + 1], wgt[:, kc % 2, i * 512:(i + 1) * 512],
                                           start=(kc == 0), stop=(kc == KC - 1))
                        ins.then_inc(s_gfree, 1)
                    t.wait_ge(s_gfree, KC)
                    t.drain().then_inc(s_d, 1)
                    for j in range(32):
                        ins = t.matmul(pr[2][:, j:j + 1], rowss[0:1, j * 128:(j + 1) * 128], one1[0:1, 0:1],
                                       start=True, stop=True)
                    ins.then_inc(s_t, 1)
                    t.wait_ge(s_e, 2)
                    for i in range(2):
                        t.matmul(pr[3 + i][:], onesf[0:1, :], grow[0:1, i * 512:(i + 1) * 512],
                                 start=True, stop=True).then_inc(s_t, 1)

        end_phase()
        for k, v in list(locals().items()):
            if k not in ("g", "es", "block", "_"):
                setattr(g, k, v)
        if "A" in phases:
            phase_a1(g)
            end_phase()
        es_a1w.close()
        if "R" in phases:
            phase_a2(g)
            end_phase()
        es_wo = ExitStack()
        g.d_Wo = sb("d_Wo", [128, KC, 1024], BF16, es_wo)
        g.d_wst = sb("d_wst", [128, 2, 1024], F32, es_wo)
        g.wo_prefetched = False
        if "B" in phases:
            phase_b(g)
            end_phase()
        if "C" in phases:
            phase_cd(g)
            end_phase()
        es_wo.close()
    return nc


def _bf(a):
    return np.ascontiguousarray(a).astype(ml_dtypes.bfloat16)


def const_tables(S, hh):
    NB = S // 128
    j = np.arange(128)
    ident = np.eye(128, dtype=np.float32)
    tri = (j[:, None] >= j[None, :]).astype(np.float32)
    ones = np.ones((128, 128), np.float32)
    cm = (j[:, None] <= j[None, :]).astype(np.float32)
    cm4 = np.tile(cm, (1, 4))
    t = np.arange(512)
    dmask = np.stack([((r * 128 + j)[:, None] < t[None, :]).astype(np.float32) for r in range(4)], 1)
    half = 64
    inv_freq = (10000.0 ** (-np.arange(half, dtype=np.float32) / half)).astype(np.float32)
    ang = (np.arange(S, dtype=np.float32)[:, None] * inv_freq[None, :]).astype(np.float32)
    cos, sin = np.cos(ang).astype(np.float32), np.sin(ang).astype(np.float32)
    gh = hh * HL + np.arange(HL)
    log_gamma = np.log1p(-np.exp2(-5.0 - gh.astype(np.float32))).astype(np.float32)
    p = (np.arange(S) % 128).astype(np.float32)
    dq = np.exp((p[:, None] + 1.0) * log_gamma[None, :]).astype(np.float32)
    dk = (np.exp(-(p[:, None] + 1.0) * log_gamma[None, :]) * (128.0 ** -0.5)).astype(np.float32)
    cq = cos[:, None, :] * dq[:, :, None]
    sq = sin[:, None, :] * dq[:, :, None]
    ck = cos[:, None, :] * dk[:, :, None]
    sk = sin[:, None, :] * dk[:, :, None]
    rtab = np.stack([cq, sq, ck, sk], 1).astype(np.float32)
    rtab = rtab.reshape(NB, 128, 4 * 256)
    gc = np.broadcast_to(np.exp(128.0 * log_gamma)[None, :], (128, HL)).astype(np.float32)
    return dict(ident=_bf(ident), tri=_bf(tri), ones=_bf(ones), cmask4=_bf(cm4),
                dmask=_bf(dmask.reshape(128, 4 * 512)), rtab=np.ascontiguousarray(rtab),
                gc=np.ascontiguousarray(gc))


def chunkT(v):
    return np.ascontiguousarray(v.reshape(-1, 128).T)


def prep_core_inputs(inp, core, S):
    b, hh = core // 2, core % 2
    f = lambda a: np.ascontiguousarray(a, dtype=np.float32)
    x = inp["x"][b, :S]
    w_ada, b_ada = inp["w_ada"][0], inp["b_ada"][0]
    w_in, w_out = inp["w_in"][0], inp["w_out"][0]
    hs = slice(hh * 512, (hh + 1) * 512)
    grp = lambda gi: w_in[:, gi * 1024:(gi + 1) * 1024][:, hs]
    w_in_c = np.concatenate([grp(0), grp(1), grp(3), grp(2), grp(4), grp(5), grp(6), grp(7)], 1)
    rows = []
    for i in range(8):
        for r in range(2):
            gh = r * HL + (i % HL)
            base = gh * 128 if i < HL else 1024 + gh * 128
            rows.append(w_out[base:base + 128, hh * 1024:(hh + 1) * 1024])
    w_out_c = np.concatenate(rows, 0)
    d = dict(
        x=f(x), xres=f(x[:, hh * 1024:(hh + 1) * 1024]), cT=f(chunkT(inp["c"][b])),
        w_ada_ss=f(w_ada[:, 0:4096]), w_ada_g=f(w_ada[:, 4096 + hh * 1024:4096 + (hh + 1) * 1024]),
        b_ss=f(b_ada[0:4096][None, :]), b_g=f(b_ada[4096 + hh * 1024:4096 + (hh + 1) * 1024][None, :]),
        ng=f(chunkT(inp["norm_gain"][0])), w_in_c=f(w_in_c),
        gq=f(inp["sb_q_gain"][0][:, None]), gk=f(inp["sb_k_gain"][0][:, None]),
        rgain=f(np.broadcast_to(inp["ret_norm_gain"][0][hs][None, :], (128, 512))),
        w_out_c=f(w_out_c),
    )
    d.update(const_tables(S, hh))
    return d


_PROG = {}


def kernel(x, c, w_ada, b_ada, norm_gain, w_in, sb_q_gain, sb_k_gain, ret_norm_gain, w_out):
    inp = dict(x=np.asarray(x), c=np.asarray(c), w_ada=np.asarray(w_ada), b_ada=np.asarray(b_ada),
               norm_gain=np.asarray(norm_gain), w_in=np.asarray(w_in), sb_q_gain=np.asarray(sb_q_gain),
               sb_k_gain=np.asarray(sb_k_gain), ret_norm_gain=np.asarray(ret_norm_gain), w_out=np.asarray(w_out))
    B, S, _ = inp["x"].shape
    n = 2 * B
    if S not in _PROG:
        _PROG[S] = build_program(S, n_cores=n, gather=True)
    nc = _PROG[S]
    in_maps = [prep_core_inputs(inp, core, S) for core in range(n)]
    res = run_bass_kernel_spmd(nc, in_maps, core_ids=list(range(n)))
    out = np.empty((B, S, D), np.float32)
    for core in range(n):
        b, hh = core // 2, core % 2
        out[b, :, hh * 1024:(hh + 1) * 1024] = res.results[core]["out"]
    return out


def phase_a1(g):
    nc, S, NT, NB = g.nc, g.S, g.NT, g.NB
    with ExitStack() as es:
        sb = lambda n, s, d: g.sb("a1_" + n, s, d, es)
        ps = lambda n, s, d: g.ps("a1_" + n, s, d, es)
        sem = lambda n: g.sem("a1_" + n, es)
        Wb = g.a1_Wb
        xin = sb("xin", [128, 3, 2048], F32)
        junk = sb("junk", [128, 2048], BF16)
        xs = sb("xs", [128, 8, 2048], BF16)
        hT = sb("hT", [128, 2, KC, 512], BF16)
        sqb = sb("sqb", [128, 2, 512], BF16)
        lnb = sb("lnb", [128, 2, 512], F32)
        rsb = sb("rsb", [128, 2, 512], F32)
        stg = sb("stg", [128, 4, 512], BF16)
        vst = sb("vst", [128, 2, 512], BF16)
        ssq = sb("ssq", [128, NB], F32)
        lnx = sb("lnx", [128, NB], F32)
        rstd = sb("rstd", [128, NB], F32)
        tp = [ps(f"tp{i}", [128, 1024], BF16) for i in range(2)]
        pp = [ps(f"pp{i}", [128, 512], F32) for i in range(3)]
        mp = ps("mp", [128, 512], F32)
        pv = [ps(f"pv{i}", [128, 512], F32) for i in range(2)]
        s_xin = [sem(f"xin{i}") for i in range(3)]
        s_xs, s_tp, s_hT, s_pp, s_sq, s_ms, s_ln, s_rs = (sem(n) for n in
                                                         ("xs", "tp", "hT", "pp", "sq", "ms", "ln", "rs"))
        s_stgf, s_pv, s_vstf, s_act = sem("stgf"), sem("pv"), sem("vstf"), sem("act")
        s_stgd = [sem(f"stgd{i}") for i in range(4)]
        s_vstd = [sem(f"vstd{i}") for i in range(2)]
        s_hsp = [sem(f"hsp{i}") for i in range(2)]
        AB, gqs, gks, ident, ones = g.AB, g.gqs, g.gks, g.ident, g.ones
        dests = [g.qT_s] * 4 + [g.kT_s] * 4 + [g.zg_s] * 4

        with nc.Block() as block:
            @block.sync
            def _(e):
                for gb in range(NB):
                    if gb >= 3:
                        e.wait_ge(s_xs, gb - 2)
                    e.dma_start(out=xin[:, gb % 3, :], in_=g.x_d[gb * 128:(gb + 1) * 128, :]
                                ).then_inc(s_xin[gb % 3], 16)

            @block.scalar
            def _(a):
                na = [0]

                def selfsync(ins):
                    ins.then_inc(s_act, 1)
                    na[0] += 1
                    a.wait_ge(s_act, na[0])

                def blocks(T):
                    for b4 in range(4):
                        gb = 4 * T + b4
                        a.wait_ge(s_xin[gb % 3], 16 * (gb // 3 + 1))
                        selfsync(a.activation(junk[:], xin[:, gb % 3, :], AF.Square, accum_out=ssq[:, gb:gb + 1]))
                        selfsync(a.activation(lnx[:, gb:gb + 1], ssq[:, gb:gb + 1], AF.Ln, bias=EPS, scale=1.0 / D))
                        selfsync(a.activation(rstd[:, gb:gb + 1], lnx[:, gb:gb + 1], AF.Exp, scale=-0.5))
                        if T >= 2:
                            a.wait_ge(s_tp, 16 * (T - 1))
                        a.activation(xs[:, (T % 2) * 4 + b4, :], xin[:, gb % 3, :], AF.Copy,
                                     scale=rstd[:, gb:gb + 1]).then_inc(s_xs, 1)

                def lnexp(T, c):
                    n, m = 12 * T + c, 8 * T + c
                    a.wait_ge(s_ms, m + 1)
                    a.activation(lnb[:, m % 2, :], mp[:], AF.Ln, bias=EPS, scale=1.0 / 128).then_inc(s_ln, 1)
                    a.wait_ge(s_ln, m + 1)
                    if n >= 2:
                        a.wait_ge(s_stgf, n - 1)
                    a.activation(rsb[:, n % 2, :], lnb[:, m % 2, :], AF.Exp, scale=-0.5).then_inc(s_rs, 1)

                def chunks(T):
                    for c in range(12):
                        n = 12 * T + c
                        if c < 8:
                            m = 8 * T + c
                            a.wait_ge(s_pp, n + 1)
                            if m >= 2:
                                a.wait_ge(s_ms, m - 1)
                            a.activation(sqb[:, m % 2, :], pp[n % 3][:], AF.Square).then_inc(s_sq, 1)
                            if c >= 1:
                                lnexp(T, c - 1)
                        else:
                            if c == 8:
                                lnexp(T, 7)
                            a.wait_ge(s_pp, n + 1)
                            selfsync(a.activation(lnb[:, 0, :], pp[n % 3][:], AF.Exp, scale=-1.0))
                            selfsync(a.activation(lnb[:, 1, :], lnb[:, 0, :], AF.Ln, bias=1.0))
                            if n >= 2:
                                a.wait_ge(s_stgf, n - 1)
                            a.activation(rsb[:, n % 2, :], lnb[:, 1, :], AF.Exp, scale=-1.0).then_inc(s_rs, 1)

                blocks(0)
                for T in range(NT):
                    if T + 1 < NT:
                        blocks(T + 1)
                    chunks(T)

            @block.vector
            def _(v):
                def evac(T):
                    for kc in range(KC):
                        j = 16 * T + kc
                        v.wait_ge(s_tp, j + 1)
                        if kc == 0 and T >= 2:
                            v.wait_ge(s_pv, 4 * (T - 1))
                            v.wait_ge(s_hsp[T % 2], 16 * ((T - 2) // 2 + 1))
                        v.tensor_scalar(hT[:, T % 2, kc, :], tp[j % 2][:, 0:512], AB[:, kc:kc + 1],
                                        AB[:, 16 + kc:17 + kc], ALU.mult, ALU.add).then_inc(s_hT, 1)

                def finals(T):
                    for c in range(12):
                        n = 12 * T + c
                        v.wait_ge(s_rs, n + 1)
                        if n >= 4:
                            v.wait_ge(s_stgd[n % 4], 16 * ((n - 4) // 4 + 1))
                        if c < 8:
                            v.scalar_tensor_tensor(stg[:, n % 4, :], pp[n % 3][:], (gqs if c < 4 else gks)[:, 0:1],
                                                   rsb[:, n % 2, :], ALU.mult, ALU.mult).then_inc(s_stgf, 1)
                        else:
                            v.tensor_tensor(stg[:, n % 4, :], pp[n % 3][:], rsb[:, n % 2, :], ALU.mult
                                            ).then_inc(s_stgf, 1)
                    for b4 in range(4):
                        gb = 4 * T + b4
                        v.wait_ge(s_pv, gb + 1)
                        if gb >= 2:
                            v.wait_ge(s_vstd[gb % 2], 16 * ((gb - 2) // 2 + 1))
                        v.tensor_copy(vst[:, gb % 2, :], pv[gb % 2][:]).then_inc(s_vstf, 1)

                evac(0)
                for T in range(NT):
                    if T + 1 < NT:
                        evac(T + 1)
                    finals(T)

            @block.gpsimd
            def _(p):
                for T in range(NT):
                    p.wait_ge(s_hT, 16 * (T + 1))
                    p.dma_start(out=g.hT_s[T, :, :], in_=hT[:, T % 2, :, :].rearrange("p k t -> p (k t)")
                                ).then_inc(s_hsp[T % 2], 16)
                    for c in range(12):
                        n = 12 * T + c
                        p.wait_ge(s_stgf, n + 1)
                        p.dma_start(out=dests[c][c % 4, :, T * 512:(T + 1) * 512], in_=stg[:, n % 4, :]
                                    ).then_inc(s_stgd[n % 4], 16)
                    for b4 in range(4):
                        gb = 4 * T + b4
                        p.wait_ge(s_vstf, gb + 1)
                        p.dma_start(out=g.v_s.ap()[:, :, gb * 128:(gb + 1) * 128].rearrange("h p d -> p h d"),
                                    in_=vst[:, gb % 2, :].rearrange("p (h d) -> p h d", h=4)
                                    ).then_inc(s_vstd[gb % 2], 16)
                ntot = 12 * NT
                for i in range(4):
                    cnt = len([n for n in range(ntot) if n % 4 == i])
                    p.wait_ge(s_stgd[i], 16 * cnt)
                for i in range(2):
                    p.wait_ge(s_vstd[i], 16 * len([x for x in range(NB) if x % 2 == i]))
                    p.wait_ge(s_hsp[i], 16 * len([x for x in range(NT) if x % 2 == i]))

            @block.tensor
            def _(t):
                def trans(T):
                    t.wait_ge(s_xs, 4 * T + 4)
                    for kc in range(KC):
                        j = 16 * T + kc
                        if j >= 2:
                            t.wait_ge(s_hT, j - 1)
                        for b4 in range(4):
                            ins = t.transpose(tp[j % 2][:, b4 * 128:(b4 + 1) * 128],
                                              xs[:, (T % 2) * 4 + b4, kc * 128:(kc + 1) * 128], ident[:])
                        ins.then_inc(s_tp, 1)

                def ones_mm(T, c):
                    m = 8 * T + c
                    t.wait_ge(s_sq, m + 1)
                    if m >= 1:
                        t.wait_ge(s_ln, m)
                    t.matmul(mp[:], ones[:], sqb[:, m % 2, :], start=True, stop=True).then_inc(s_ms, 1)

                def inproj(T):
                    t.wait_ge(s_hT, 16 * (T + 1))
                    for c in range(12):
                        n = 12 * T + c
                        if n >= 3:
                            t.wait_ge(s_stgf, n - 2)
                        for kc in range(KC):
                            ins = t.matmul(pp[n % 3][:], Wb[:, kc, c * 128:(c + 1) * 128], hT[:, T % 2, kc, :],
                                           start=(kc == 0), stop=(kc == KC - 1))
                        ins.then_inc(s_pp, 1)
                        if 1 <= c <= 8:
                            ones_mm(T, c - 1)
                    for b4 in range(4):
                        gb = 4 * T + b4
                        if gb >= 2:
                            t.wait_ge(s_vstf, gb - 1)
                        for kc in range(KC):
                            ins = t.matmul(pv[gb % 2][:], hT[:, T % 2, kc, b4 * 128:(b4 + 1) * 128],
                                           Wb[:, kc, 1536:2048], start=(kc == 0), stop=(kc == KC - 1))
                        ins.then_inc(s_pv, 1)

                trans(0)
                for T in range(NT):
                    if T + 1 < NT:
                        trans(T + 1)
                    inproj(T)


def phase_a2(g):
    nc, S, NT, NB = g.nc, g.S, g.NT, g.NB
    TPH = NT // 2
    with ExitStack() as es:
        sb = lambda n, s, d: g.sb("a2_" + n, s, d, es)
        ps = lambda n, s, d: g.ps("a2_" + n, s, d, es)
        sem = lambda n: g.sem("a2_" + n, es)
        Wb = sb("Wb", [128, KC, 2048], BF16)
        wst = sb("wst", [128, 4, 1024], F32)
        hT = sb("hT", [128, 2, KC, 512], BF16)
        rtab = sb("rtab", [128, 3, 1024], F32)
        rt = sb("rt", [128, 8, 256], F32)
        qktm = sb("qktm", [128, 4, 2, 512], BF16)
        vb = sb("vb", [128, 6, 512], BF16)
        sgt = sb("sgt", [128, 2, 512], F32)
        sig = sb("sig", [128, 2, 512], F32)
        zgb = sb("zgb", [128, 6, 512], BF16)
        qkT = sb("qkT", [128, 4, 1024], BF16)
        PT = sb("PT", [128, 3, 512], BF16)
        Tst = sb("Tst", [128, 512], F32)
        Sb = sb("Sb", [128, 4, 512], BF16)
        st6 = sb("st6", [128, 4, 6], F32)
        mv = sb("mv", [128, 4, 2], F32)
        lnr = sb("lnr", [128, 4], F32)
        rs4 = sb("rs4", [128, 2, 4], F32)
        nmr = sb("nmr", [128, 2, 4], F32)
        yb = sb("yb", [128, 2, 512], F32)
        y2 = sb("y2", [128, 512], F32)
        mixtm = sb("mixtm", [128, 2, 512], BF16)
        mTs = sb("mTs", [128, 2, 4, 512], BF16)
        ip = [ps(f"ip{i}", [128, 512], F32) for i in range(3)]
        tq = ps("tq", [128, 1024], BF16)
        sc = ps("sc", [128, 512], F32)
        dsp = ps("ds", [128, 512], F32)
        ou = ps("ou", [128, 512], F32)
        mt = ps("mt", [128, 1024], BF16)
        s_wst = [sem(f"wst{i}") for i in range(4)]
        s_wcv, s_wcg = sem("wcv"), sem("wcg")
        s_h2 = [sem(f"h2{i}") for i in range(2)]
        s_rt = [sem(f"rt{i}") for i in range(3)]
        s_mst = [sem(f"mst{i}") for i in range(2)]
        (s_ip, s_rq, s_rk, s_cq, s_ck, s_v, s_sig, s_zg, s_tq, s_tqc, s_sc, s_ds, s_pt, s_tu, s_sbc, s_ou,
         s_bn, s_mv, s_r4, s_nm, s_yd, s_y2, s_mx, s_mt, s_mtc, s_act, s_init) = (
            sem(n) for n in ("ip", "rq", "rk", "cq", "ck", "v", "sig", "zg", "tq", "tqc", "sc", "ds", "pt", "tu",
                             "sbc", "ou", "bn", "mv", "r4", "nm", "yd", "y2", "mx", "mt", "mtc", "act", "init"))
        ident, cm4, rgain, gcs = g.ident, g.cm4, g.rgain, g.gcs
        NS = NB + 6
        okb = lambda b: 0 <= b < NB

        def v4(ap2d, j):
            return ap2d.rearrange("p (h two i) -> p h two i", h=4, two=2)[:, :, j, :]

        def t4(ap2d):
            return ap2d.rearrange("p (h i) -> p h i", h=4)

        with nc.Block() as block:
            @block.sync
            def _(e):
                for i in range(2 * KC):
                    if i >= 4:
                        e.wait_ge(s_wcv if i % 2 == 0 else s_wcg, i // 2 - 1)
                    kc, hf = i // 2, i % 2
                    e.dma_start(out=wst[:, i % 4, :],
                                in_=g.win_d[kc * 128:(kc + 1) * 128, 2048 + hf * 1024:2048 + (hf + 1) * 1024]
                                ).then_inc(s_wst[i % 4], 16)

                def loads(T):
                    if T >= 2:
                        e.wait_ge(s_ip, 16 * (T - 1))
                    e.dma_start(out=hT[:, T % 2, :, :].rearrange("p k t -> p (k t)"), in_=g.hT_s[T, :, :]
                                ).then_inc(s_h2[T % 2], 16)
                    for b4 in range(4):
                        gb = 4 * T + b4
                        if gb >= 3:
                            e.wait_ge(s_rk, gb - 2)
                        e.dma_start(out=rtab[:, gb % 3, :], in_=g.rtab_d[gb, :, :]).then_inc(s_rt[gb % 3], 16)

                def stores(T):
                    e.wait_ge(s_mtc, 4 * (T + 1))
                    for h in range(4):
                        e.dma_start(out=g.mixc[4 + h][T // TPH][:, (T % TPH) * 512:(T % TPH + 1) * 512],
                                    in_=mTs[:, T % 2, h, :]).then_inc(s_mst[T % 2], 16)

                for T in range(NT):
                    loads(T)
                    if T >= 2:
                        stores(T - 2)
                for T in range(max(NT - 2, 0), NT):
                    stores(T)
                for i in range(2):
                    e.wait_ge(s_mst[i], 64 * len([x for x in range(NT) if x % 2 == i]))

            @block.tensor
            def _(t):
                def IP(b):
                    T, b4 = b // 4, b % 4
                    if b4 == 0:
                        t.wait_ge(s_h2[T % 2], 16 * (T // 2 + 1))
                    for tau in range(4):
                        u = 4 * b + tau
                        if u >= 3:
                            pu = u - 3
                            t.wait_ge((s_rq, s_rk, s_v, s_zg)[pu % 4], pu // 4 + 1)
                        for kc in range(KC):
                            ins = t.matmul(ip[u % 3][:], hT[:, T % 2, kc, b4 * 128:(b4 + 1) * 128],
                                           Wb[:, kc, tau * 512:(tau + 1) * 512], start=(kc == 0), stop=(kc == KC - 1))
                        ins.then_inc(s_ip, 1)

                def TQ(b):
                    t.wait_ge(s_cq, b + 1)
                    t.wait_ge(s_ck, b + 1)
                    if b >= 1:
                        t.wait_ge(s_tqc, b)
                    for j in range(2):
                        for h in range(4):
                            ins = t.transpose(tq[:, j * 512 + h * 128:j * 512 + (h + 1) * 128],
                                              qktm[:, b % 4, j, h * 128:(h + 1) * 128], ident[:])
                    ins.then_inc(s_tq, 1)

                def SC(b):
                    t.wait_ge(s_tqc, b + 1)
                    if b >= 1:
                        t.wait_ge(s_pt, b)
                    for h in range(4):
                        ins = t.matmul(sc[:, h * 128:(h + 1) * 128], qkT[:, b % 4, 512 + h * 128:512 + (h + 1) * 128],
                                       qkT[:, b % 4, h * 128:(h + 1) * 128], start=True, stop=True)
                    ins.then_inc(s_sc, 1)
                    t.wait_ge(s_v, b + 1)
                    if b >= 1:
                        t.wait_ge(s_tu, 4 * b)
                    for h in range(4):
                        ins = t.matmul(dsp[:, h * 128:(h + 1) * 128], qktm[:, b % 4, 1, h * 128:(h + 1) * 128],
                                       vb[:, b % 6, h * 128:(h + 1) * 128], start=True, stop=True)
                    ins.then_inc(s_ds, 1)

                def OU(b):
                    t.wait_ge(s_pt, b + 1)
                    if b >= 1:
                        t.wait_ge(s_sbc, 4 * b)
                        t.wait_ge(s_yd, 4 * b)
                    for h in range(4):
                        hsl = slice(h * 128, (h + 1) * 128)
                        ins = t.matmul(ou[:, hsl], PT[:, b % 3, hsl], vb[:, b % 6, hsl], start=True, stop=(b == 0))
                        if b >= 1:
                            ins = t.matmul(ou[:, hsl], qkT[:, b % 4, hsl], Sb[:, b % 4, hsl], start=False, stop=True)
                    ins.then_inc(s_ou, 1)

                def MT(b):
                    t.wait_ge(s_mx, b + 1)
                    if b >= 1:
                        t.wait_ge(s_mtc, b)
                    for h in range(4):
                        ins = t.transpose(mt[:, h * 128:(h + 1) * 128], mixtm[:, b % 2, h * 128:(h + 1) * 128], ident[:])
                    ins.then_inc(s_mt, 1)

                t.wait_ge(s_wcv, KC)
                t.wait_ge(s_wcg, KC)
                for s in range(NS):
                    for fn, lag in ((IP, 0), (TQ, 1), (SC, 2), (OU, 3), (MT, 5)):
                        if okb(s - lag):
                            fn(s - lag)

            @block.vector
            def _(v):
                v.memset(Tst[:], 0.0).then_inc(s_init, 1)
                for i in range(0, 2 * KC, 2):
                    v.wait_ge(s_wst[i % 4], 16 * (i // 4 + 1))
                    v.tensor_copy(Wb[:, i // 2, 0:1024], wst[:, i % 4, :]).then_inc(s_wcv, 1)

                def ZG(b):
                    v.wait_ge(s_sig, b + 1)
                    if b >= 6:
                        v.wait_ge(s_mx, b - 5)
                    v.tensor_tensor(zgb[:, b % 6, :], ip[(4 * b + 3) % 3][:], sig[:, b % 2, :], ALU.mult
                                    ).then_inc(s_zg, 1)

                def PTM(b):
                    v.wait_ge(s_sc, b + 1)
                    if b >= 3:
                        v.wait_ge(s_ou, b - 2)
                    v.tensor_tensor(PT[:, b % 3, :], sc[:], cm4[:], ALU.mult).then_inc(s_pt, 1)

                def TU(b):
                    v.wait_ge(s_ds, b + 1)
                    if b >= 1:
                        v.wait_ge(s_sbc, 4 * b)
                    for h in range(4):
                        hsl = slice(h * 128, (h + 1) * 128)
                        v.scalar_tensor_tensor(Tst[:, hsl], Tst[:, hsl], gcs[:, h:h + 1], dsp[:, hsl],
                                               ALU.mult, ALU.add).then_inc(s_tu, 1)

                def ST(b):
                    v.wait_ge(s_ou, b + 1)
                    if b >= 1:
                        v.wait_ge(s_nm, b)
                        v.wait_ge(s_r4, b)
                    for h in range(4):
                        v.bn_stats(st6[:, h, :], ou[:, h * 128:(h + 1) * 128]).then_inc(s_bn, 1)
                    v.wait_ge(s_bn, 4 * (b + 1))
                    for h in range(4):
                        v.bn_aggr(mv[:, h, :], st6[:, h, :]).then_inc(s_mv, 1)

                def RR(b, which):
                    u = 4 * b + which
                    if which == 0:
                        v.wait_ge(s_rt[b % 3], 16 * (b // 3 + 1))
                    v.wait_ge(s_ip, u + 1)
                    if b >= 1:
                        v.wait_ge((s_cq, s_ck)[which], b)
                    src = ip[u % 3][:]
                    x1, x2 = v4(src, 0), v4(src, 1)
                    ct = t4(rtab[:, b % 3, which * 512:which * 512 + 256])
                    stt = t4(rtab[:, b % 3, which * 512 + 256:which * 512 + 512])
                    o = which * 4
                    v.tensor_tensor(t4(rt[:, o + 0, :]), x1, ct, ALU.mult)
                    v.tensor_tensor(t4(rt[:, o + 1, :]), x2, stt, ALU.mult)
                    v.tensor_tensor(t4(rt[:, o + 2, :]), x1, stt, ALU.mult)
                    v.tensor_tensor(t4(rt[:, o + 3, :]), x2, ct, ALU.mult).then_inc((s_rq, s_rk)[which], 1)

                def NMR(b):
                    v.wait_ge(s_r4, b + 1)
                    if b >= 2:
                        v.wait_ge(s_yd, 4 * (b - 1))
                    v.scalar_tensor_tensor(nmr[:, b % 2, :], mv[:, :, 0], -1.0, rs4[:, b % 2, :], ALU.mult, ALU.mult
                                           ).then_inc(s_nm, 1)

                for s in range(NS):
                    if okb(s - 1):
                        ZG(s - 1)
                    if okb(s - 3):
                        PTM(s - 3)
                        TU(s - 3)
                    if okb(s - 4):
                        ST(s - 4)
                    if okb(s):
                        RR(s, 0)
                        RR(s, 1)
                    if okb(s - 4):
                        NMR(s - 4)

            @block.gpsimd
            def _(p):
                for i in range(1, 2 * KC, 2):
                    p.wait_ge(s_wst[i % 4], 16 * (i // 4 + 1))
                    p.tensor_copy(Wb[:, i // 2, 1024:2048], wst[:, i % 4, :]).then_inc(s_wcg, 1)

                def SBC(b):
                    p.wait_ge(s_tu, 4 * (b + 1))
                    if b >= 3:
                        p.wait_ge(s_ou, b - 2)
                    for h in range(4):
                        hsl = slice(h * 128, (h + 1) * 128)
                        p.tensor_scalar(Sb[:, (b + 1) % 4, hsl], Tst[:, hsl], gcs[:, h:h + 1], 1.0, ALU.mult, ALU.mult
                                        ).then_inc(s_sbc, 1)

                def CC(b, which):
                    p.wait_ge((s_rq, s_rk)[which], b + 1)
                    if b >= 4:
                        p.wait_ge(s_ds, b - 3)
                    o = which * 4
                    dst = qktm[:, b % 4, which, :]
                    p.tensor_tensor(v4(dst, 0), t4(rt[:, o + 0, :]), t4(rt[:, o + 1, :]), ALU.subtract)
                    p.tensor_tensor(v4(dst, 1), t4(rt[:, o + 2, :]), t4(rt[:, o + 3, :]), ALU.add
                                    ).then_inc((s_cq, s_ck)[which], 1)

                def MIX(b):
                    p.wait_ge(s_yd, 4 * (b + 1))
                    if b >= 1:
                        p.wait_ge(s_mx, b)
                    p.tensor_tensor(y2[:], yb[:, b % 2, :], rgain[:], ALU.mult).then_inc(s_y2, 1)
                    p.wait_ge(s_y2, b + 1)
                    p.wait_ge(s_zg, b + 1)
                    if b >= 2:
                        p.wait_ge(s_mt, b - 1)
                    p.tensor_tensor(mixtm[:, b % 2, :], y2[:], zgb[:, b % 6, :], ALU.mult).then_inc(s_mx, 1)

                for s in range(NS):
                    if okb(s - 3):
                        SBC(s - 3)
                    if okb(s):
                        CC(s, 0)
                        CC(s, 1)
                    if okb(s - 4):
                        MIX(s - 4)

            @block.scalar
            def _(a):
                na = [0]

                def selfsync(ins):
                    ins.then_inc(s_act, 1)
                    na[0] += 1
                    a.wait_ge(s_act, na[0])

                def MTC(b):
                    T, b4 = b // 4, b % 4
                    a.wait_ge(s_mt, b + 1)
                    if b4 == 0 and T >= 2:
                        a.wait_ge(s_mst[T % 2], 64 * ((T - 2) // 2 + 1))
                    a.activation(mTs[:, T % 2, :, b4 * 128:(b4 + 1) * 128], t4(mt[:, 0:512]), AF.Copy
                                 ).then_inc(s_mtc, 1)

                def R4(b):
                    a.wait_ge(s_mv, 4 * (b + 1))
                    selfsync(a.activation(lnr[:], mv[:, :, 1], AF.Ln, bias=EPS))
                    if b >= 2:
                        a.wait_ge(s_yd, 4 * (b - 1))
                    a.activation(rs4[:, b % 2, :], lnr[:], AF.Exp, scale=-0.5).then_inc(s_r4, 1)

                def NORM(b):
                    a.wait_ge(s_nm, b + 1)
                    if b >= 2:
                        a.wait_ge(s_y2, b - 1)
                    for h in range(4):
                        hsl = slice(h * 128, (h + 1) * 128)
                        a.activation(yb[:, b % 2, hsl], ou[:, hsl], AF.Identity, bias=nmr[:, b % 2, h:h + 1],
                                     scale=rs4[:, b % 2, h:h + 1]).then_inc(s_yd, 1)

                def VZ(b):
                    a.wait_ge(s_ip, 4 * b + 3)
                    if b >= 6:
                        a.wait_ge(s_ou, b - 5)
                    a.activation(vb[:, b % 6, :], ip[(4 * b + 2) % 3][:], AF.Copy).then_inc(s_v, 1)
                    a.wait_ge(s_ip, 4 * b + 4)
                    pz = ip[(4 * b + 3) % 3][:]
                    selfsync(a.activation(sgt[:, 0, :], pz, AF.Exp, scale=-1.0))
                    selfsync(a.activation(sgt[:, 1, :], sgt[:, 0, :], AF.Ln, bias=1.0))
                    if b >= 2:
                        a.wait_ge(s_zg, b - 1)
                    a.activation(sig[:, b % 2, :], sgt[:, 1, :], AF.Exp, scale=-1.0).then_inc(s_sig, 1)

                def TQC(b):
                    a.wait_ge(s_tq, b + 1)
                    if b >= 4:
                        a.wait_ge(s_ou, b - 3)
                    a.activation(qkT[:, b % 4, :], tq[:], AF.Copy).then_inc(s_tqc, 1)

                for s in range(NS):
                    if okb(s - 6):
                        MTC(s - 6)
                    if okb(s - 4):
                        R4(s - 4)
                        NORM(s - 4)
                    if okb(s):
                        VZ(s)
                    if okb(s - 1):
                        TQC(s - 1)


def phase_b(g):
    nc, S, NT, NB = g.nc, g.S, g.NT, g.NB
    TPH = NT // 2
    tiles = []
    for h in range(HL):
        for qt in range(NT):
            for kb in range(4 * qt + 3, -1, -1):
                tiles.append(dict(h=h, qt=qt, kb=kb, first=(kb == 4 * qt + 3), last=(kb == 0),
                                  r=(kb - 4 * qt if kb >= 4 * qt else None), G=h * NT + qt))
    N = len(tiles)
    NG = HL * NT
    nmask = [0] * (N + 1)
    nrup = [0] * (N + 1)
    for i, tl in enumerate(tiles):
        nmask[i + 1] = nmask[i] + (1 if tl["r"] is not None else 0)
        nrup[i + 1] = nrup[i] + (0 if tl["last"] else 1)
    last_of_G = {}
    for i, tl in enumerate(tiles):
        last_of_G[tl["G"]] = i

    with ExitStack() as es:
        sb = lambda n, s, d: g.sb("b_" + n, s, d, es)
        ps = lambda n, s, d: g.ps("b_" + n, s, d, es)
        sem = lambda n: g.sem("b_" + n, es)
        KT = sb("KT", [128, 2, S], BF16)
        VV = sb("VV", [128, 2, S], BF16)
        QT = sb("QT", [128, 2, S], BF16)
        ZG = sb("ZG", [128, 2, S], BF16)
        EE = sb("EE", [128, 6, 2, 512], BF16)
        Lp = sb("Lp", [128, 3, 512], BF16)
        W = sb("W", [128, 3, 512], BF16)
        R = sb("R", [128, 3, 512], BF16)
        Rz = sb("Rz", [128, 512], BF16)
        mst = sb("mst", [128, 2, 512], BF16)
        ZC = [ps(f"ZC{i}", [128, 1024], F32) for i in range(2)]
        Zp = lambda i: ZC[i % 2][:, 0:512]
        Cp = lambda i: ZC[(i + 1) % 2][:, 512:1024]
        E = lambda i: EE[:, i % 6, 0, :]
        EC = lambda i: EE[:, (i + 3) % 6, 1, :]
        Op = [ps(f"Op{i}", [128, 512], F32) for i in range(2)]
        s_hd = [sem(f"hd{i}") for i in range(2)]
        s_hdB = [sem(f"hdB{i}") for i in range(2)]
        s_mst = [sem(f"mst{i}") for i in range(2)]
        s_qk, s_x, s_mask, s_ln, s_rup, s_cum, s_mul, s_pv, s_fin, s_init, s_hdone = (
            sem(n) for n in ("qk", "x", "mask", "ln", "rup", "cum", "mul", "pv", "fin", "init", "hdone"))
        s_wod = [sem(f"wod{i}") for i in range(2)]
        s_woc = sem("woc")
        g.wo_prefetched = True
        first_of_head = {}
        for i_, tl_ in enumerate(tiles):
            first_of_head.setdefault(tl_["h"], i_)

        def cc(p, i_chunk):
            for hf in range(2):
                p.collective_compute("AllGather", ALU.bypass, replica_groups=g.rgroups,
                                     ins=[g.mixc[i_chunk][hf].ap().opt()], outs=[g.gath[i_chunk][hf].ap().opt()]
                                     ).then_inc(g.s_cc, 1)
        tri, ones, dmask = g.tri, g.ones, g.dmask
        NS = N + 8
        ok = lambda i: 0 <= i < N

        with nc.Block() as block:
            @block.sync
            def _(e):
                def load(h):
                    if h >= 2:
                        e.wait_ge(s_fin, (h - 1) * NT)
                    for dst, src in ((KT, g.kT_s), (QT, g.qT_s)):
                        e.dma_start(out=dst[:, h % 2, :], in_=src[h, :, :]).then_inc(s_hd[h % 2], 16)
                    for dst, src in ((VV, g.v_s), (ZG, g.zg_s)):
                        e.dma_start(out=dst[:, h % 2, :], in_=src[h, :, :]).then_inc(s_hdB[h % 2], 16)

                def stores(h):
                    for qt in range(NT):
                        G = h * NT + qt
                        e.wait_ge(s_fin, G + 1)
                        e.dma_start(out=g.mixc[h][qt // TPH][:, (qt % TPH) * 512:(qt % TPH + 1) * 512],
                                    in_=mst[:, G % 2, :]).then_inc(s_mst[G % 2], 16)

                load(0)
                e.wait_ge(s_hd[0], 32)
                e.wait_ge(s_hdB[0], 32)
                if HL > 1:
                    load(1)
                for kc in range(KC):
                    if kc >= 2:
                        e.wait_ge(s_woc, kc - 1)
                    e.dma_start(out=g.d_wst[:, kc % 2, :], in_=g.wout_d[kc * 128:(kc + 1) * 128, :]
                                ).then_inc(s_wod[kc % 2], 16)
                for h in range(HL):
                    stores(h)
                    for i in range(2):
                        e.wait_ge(s_mst[i], 16 * len([x for x in range((h + 1) * NT) if x % 2 == i]))
                    e.nop().then_inc(s_hdone, 1)
                    if h + 2 < HL:
                        load(h + 2)

            @block.tensor
            def _(t):
                def QK(i):
                    tl = tiles[i]
                    h, qt, kb = tl["h"], tl["qt"], tl["kb"]
                    if tl["first"] and qt == 0:
                        t.wait_ge(s_hd[h % 2], 32 * (h // 2 + 1))
                    if i >= 2:
                        t.wait_ge(s_x, i - 1)
                    t.matmul(Zp(i), KT[:, h % 2, kb * 128:(kb + 1) * 128], QT[:, h % 2, qt * 512:(qt + 1) * 512],
                             start=True, stop=True).then_inc(s_qk, 1)

                def CUM(i):
                    tl = tiles[i]
                    t.wait_ge(s_ln, i + 1)
                    if i >= 2:
                        t.wait_ge(s_x, i + 2)
                    ins = t.matmul(Cp(i), tri[:], Lp[:, i % 3, :], start=True, stop=tl["first"])
                    if not tl["first"]:
                        t.wait_ge(s_rup, nrup[i])
                        ins = t.matmul(Cp(i), ones[:], R[:, i % 3, :], start=False, stop=True)
                    ins.then_inc(s_cum, 1)

                def PV(i):
                    tl = tiles[i]
                    h, kb, G = tl["h"], tl["kb"], tl["G"]
                    t.wait_ge(s_mul, i + 1)
                    if tl["first"] and tl["qt"] == 0:
                        t.wait_ge(s_hdB[h % 2], 32 * (h // 2 + 1))
                    if tl["first"] and G >= 2:
                        t.wait_ge(s_fin, G - 1)
                    t.matmul(Op[G % 2][:], VV[:, h % 2, kb * 128:(kb + 1) * 128], W[:, i % 3, :],
                             start=tl["first"], stop=tl["last"]).then_inc(s_pv, 1)

                for s in range(NS):
                    if ok(s):
                        QK(s)
                    if ok(s - 3):
                        CUM(s - 3)
                    if ok(s - 6):
                        PV(s - 6)

            @block.scalar
            def _(a):
                def XEXP(s_):
                    i1, i4 = s_ - 1, s_ - 4
                    if ok(i1):
                        a.wait_ge(s_qk, i1 + 1)
                        if i1 >= 6:
                            a.wait_ge(s_mul, i1 - 5)
                    if ok(i4):
                        a.wait_ge(s_cum, i4 + 1)
                        if i4 >= 3:
                            a.wait_ge(s_mul, i4 - 2)
                    if ok(i1) and ok(i4):
                        a.activation(EE[:, i1 % 6, :, :].rearrange("p a t -> p (a t)"), ZC[i1 % 2][:], AF.Exp, scale=-1.0
                                     ).then_inc(s_x, 1)
                    elif ok(i1):
                        a.activation(E(i1), Zp(i1), AF.Exp, scale=-1.0).then_inc(s_x, 1)
                    else:
                        a.activation(EC(i4), Cp(i4), AF.Exp, scale=-1.0).then_inc(s_x, 1)

                def LN(i):
                    tl = tiles[i]
                    a.wait_ge(s_x, i + 1)
                    if tl["r"] is not None:
                        a.wait_ge(s_mask, nmask[i + 1])
                    if i >= 3:
                        a.wait_ge(s_cum, i - 2)
                        a.wait_ge(s_rup, nrup[i - 2])
                    a.activation(Lp[:, i % 3, :], E(i), AF.Ln, bias=1.0).then_inc(s_ln, 1)

                for s in range(NS):
                    if ok(s - 1) or ok(s - 4):
                        XEXP(s)
                    if ok(s - 2):
                        LN(s - 2)

            @block.gpsimd
            def _(p):
                p.memset(Rz[:], 0.0).then_inc(s_init, 1)
                p.wait_ge(s_init, 1)
                if g.gather:
                    for i_chunk in range(HL, 2 * HL):
                        cc(p, i_chunk)

                def MASK(i):
                    tl = tiles[i]
                    if tl["r"] is None:
                        return
                    p.wait_ge(s_x, i + 1)
                    p.tensor_tensor(E(i), E(i), dmask[:, tl["r"], :], ALU.mult).then_inc(s_mask, 1)

                def RUP(i):
                    tl = tiles[i]
                    if tl["last"]:
                        return
                    p.wait_ge(s_ln, i + 1)
                    if i >= 2:
                        p.wait_ge(s_cum, i - 1)
                    p.wait_ge(s_rup, nrup[i])
                    src = Rz[:] if tl["first"] else R[:, i % 3, :]
                    p.tensor_tensor(R[:, (i + 1) % 3, :], src, Lp[:, i % 3, :], ALU.add).then_inc(s_rup, 1)

                for s in range(NS):
                    if ok(s - 1):
                        MASK(s - 1)
                    if ok(s - 3):
                        RUP(s - 3)
                    if g.gather:
                        for h in range(HL - 1):
                            if s - 3 == min(first_of_head[h + 1] + 24, N - 1):
                                p.wait_ge(s_hdone, h + 1)
                                cc(p, h)

            @block.vector
            def _(v):
                for kc in range(KC):
                    v.wait_ge(s_wod[kc % 2], 16 * (kc // 2 + 1))
                    v.tensor_copy(g.d_Wo[:, kc, :], g.d_wst[:, kc % 2, :]).then_inc(s_woc, 1)
                v.wait_ge(s_woc, KC)

                def MUL(i):
                    v.wait_ge(s_x, i + 4)
                    if i >= 3:
                        v.wait_ge(s_pv, i - 2)
                    v.tensor_tensor(W[:, i % 3, :], E(i), EC(i), ALU.mult).then_inc(s_mul, 1)

                def FIN(G):
                    h, qt = G // NT, G % NT
                    v.wait_ge(s_pv, last_of_G[G] + 1)
                    if G >= 2:
                        v.wait_ge(s_mst[G % 2], 16 * ((G - 2) // 2 + 1))
                    v.tensor_tensor(mst[:, G % 2, :], Op[G % 2][:], ZG[:, h % 2, qt * 512:(qt + 1) * 512], ALU.mult
                                    ).then_inc(s_fin, 1)

                for s in range(NS):
                    if ok(s - 5):
                        MUL(s - 5)
                    if ok(s - 7) and tiles[s - 7]["last"]:
                        FIN(tiles[s - 7]["G"])


def phase_cd(g):
    nc, S, NT, NB = g.nc, g.S, g.NT, g.NB
    TPH = NT // 2
    gather = g.gather
    with ExitStack() as es:
        sb = lambda n, s, d: g.sb("d_" + n, s, d, es)
        ps = lambda n, s, d: g.ps("d_" + n, s, d, es)
        sem = lambda n: g.sem("d_" + n, es)
        Wo, wst = g.d_Wo, g.d_wst
        pre = g.wo_prefetched
        mg = sb("mg", [128, 2, KC, 512], BF16)
        xr = sb("xr", [128, 3, 1024], F32)
        yt = sb("yt", [128, 2, 512], F32)
        ost = sb("ost", [128, 3, 1024], F32)
        po = [ps(f"po{i}", [128, 512], F32) for i in range(4)]
        s_wst = [sem(f"wst{i}") for i in range(2)]
        s_wcv, s_wcg = sem("wcv"), sem("wcg")
        s_mg = [sem(f"mg{i}") for i in range(2)]
        s_xr = [sem(f"xr{i}") for i in range(3)]
        s_ost = [sem(f"ost{i}") for i in range(3)]
        s_cc = g.s_cc
        s_po, s_y1, s_y2 = sem("po"), sem("y1"), sem("y2")
        gate_bc = g.gate_bc
        NCC = 16

        with nc.Block() as block:
            @block.sync
            def _(e):
                for kc in range(0 if pre else KC):
                    if kc >= 2:
                        e.wait_ge(s_wcv if kc % 2 == 0 else s_wcg, kc // 2)
                    e.dma_start(out=wst[:, kc % 2, :], in_=g.wout_d[kc * 128:(kc + 1) * 128, :]
                                ).then_inc(s_wst[kc % 2], 16)
                if gather:
                    e.wait_ge(s_cc, NCC)
                    if g.gdump is not None:
                        s_gd = g.sem("d_gd")
                        for i in range(8):
                            for hf2 in range(2):
                                for r2 in range(2):
                                    e.dma_start(out=g.gdump[2 * i + hf2, r2 * 128:(r2 + 1) * 128, :],
                                                in_=g.gath[i][hf2][r2 * 128:(r2 + 1) * 128, :]).then_inc(s_gd, 16)
                        e.wait_ge(s_gd, 16 * 32)
                for T in range(NT):
                    hf, tt = T // TPH, T % TPH
                    if T >= 2:
                        e.wait_ge(s_po, 8 * (T - 1))
                    for i in range(8):
                        if gather:
                            for r in range(2):
                                e.dma_start(out=mg[:, T % 2, 2 * i + r, :],
                                            in_=g.gath[i][hf][r * 128:(r + 1) * 128, tt * 512:(tt + 1) * 512]
                                            ).then_inc(s_mg[T % 2], 16)
                        else:
                            for r in range(2):
                                e.dma_start(out=mg[:, T % 2, 2 * i + r, :], in_=g.mixc[i][hf][:, tt * 512:(tt + 1) * 512]
                                            ).then_inc(s_mg[T % 2], 16)
                    for b4 in range(4):
                        gb = 4 * T + b4
                        if gb >= 3:
                            e.wait_ge(s_y2, 2 * gb - 4)
                        e.dma_start(out=xr[:, gb % 3, :], in_=g.xres_d[gb * 128:(gb + 1) * 128, :]
                                    ).then_inc(s_xr[gb % 3], 16)

            @block.tensor
            def _(t):
                if not pre:
                    t.wait_ge(s_wcv, KC // 2)
                    t.wait_ge(s_wcg, KC // 2)
                for T in range(NT):
                    t.wait_ge(s_mg[T % 2], 256 * (T // 2 + 1))
                    for b4 in range(4):
                        gb = 4 * T + b4
                        for nh in range(2):
                            idx = 2 * gb + nh
                            if idx >= 4:
                                t.wait_ge(s_y1, idx - 3)
                            chs = range(KC) if gather else range(0, KC, 2)
                            for n_, ch in enumerate(chs):
                                ins = t.matmul(po[idx % 4][:], mg[:, T % 2, ch, b4 * 128:(b4 + 1) * 128],
                                               Wo[:, ch, nh * 512:(nh + 1) * 512],
                                               start=(n_ == 0), stop=(n_ == len(chs) - 1))
                            ins.then_inc(s_po, 1)

            @block.vector
            def _(v):
                for kc in range(0, 0 if pre else KC, 2):
                    v.wait_ge(s_wst[0], 16 * (kc // 2 + 1))
                    v.tensor_copy(Wo[:, kc, :], wst[:, 0, :]).then_inc(s_wcv, 1)
                for idx in range(2 * NB):
                    nh = idx % 2
                    v.wait_ge(s_po, idx + 1)
                    if idx >= 2:
                        v.wait_ge(s_y2, idx - 1)
                    v.tensor_tensor(yt[:, idx % 2, :], po[idx % 4][:], gate_bc[:, nh * 512:(nh + 1) * 512], ALU.mult
                                    ).then_inc(s_y1, 1)

            @block.gpsimd
            def _(p):
                if gather:
                    for hf in range(2):
                        p.collective_compute("AllGather", ALU.bypass, replica_groups=g.rgroups,
                                             ins=[g.mixc[HL - 1][hf].ap().opt()], outs=[g.gath[HL - 1][hf].ap().opt()]
                                             ).then_inc(s_cc, 1)
                for kc in range(1, 1 if pre else KC, 2):
                    p.wait_ge(s_wst[1], 16 * (kc // 2 + 1))
                    p.tensor_copy(Wo[:, kc, :], wst[:, 1, :]).then_inc(s_wcg, 1)
                for gb in range(NB):
                    for nh in range(2):
                        idx = 2 * gb + nh
                        p.wait_ge(s_y1, idx + 1)
                        if nh == 0:
                            p.wait_ge(s_xr[gb % 3], 16 * (gb // 3 + 1))
                            if gb >= 3:
                                p.wait_ge(s_ost[gb % 3], 16 * ((gb - 3) // 3 + 1))
                        p.tensor_tensor(ost[:, gb % 3, nh * 512:(nh + 1) * 512], yt[:, idx % 2, :],
                                        xr[:, gb % 3, nh * 512:(nh + 1) * 512], ALU.add).then_inc(s_y2, 1)
                    p.wait_ge(s_y2, 2 * gb + 2)
                    p.dma_start(out=g.out_d[gb * 128:(gb + 1) * 128, :], in_=ost[:, gb % 3, :]
                                ).then_inc(s_ost[gb % 3], 16)
                for i in range(3):
                    p.wait_ge(s_ost[i], 16 * len([x for x in range(NB) if x % 3 == i]))
```

```python
import math
from contextlib import ExitStack

import numpy as np
import ml_dtypes

import concourse.bass as bass
import concourse.mybir as mybir
from concourse.bass_utils import run_bass_kernel_spmd

F32 = mybir.dt.float32
BF16 = mybir.dt.bfloat16
AF = mybir.ActivationFunctionType
ALU = mybir.AluOpType
AX = mybir.AxisListType

D = 2048
KC = D // 128
HL = 4
EPS = 1e-6
RSQ_D = 1.0 / math.sqrt(128.0)


class G:
    pass


def build_program(S, n_cores=8, gather=True, phases="0ARBC", dbg=False):
    NT = S // 512
    NB = S // 128
    HALF = S // 2
    nc = bass.Bass("TRN2", target_bir_lowering=False)
    g = G()
    g.gather = gather

    def din(name, shape, dt=F32):
        return nc.dram_tensor(name, list(shape), dt, kind="ExternalInput")

    x_d = din("x", [S, D])
    xres_d = din("xres", [S, 1024])
    cT_d = din("cT", [128, KC])
    wss_d = din("w_ada_ss", [D, 4096])
    wg_d = din("w_ada_g", [D, 1024])
    bss_d = din("b_ss", [1, 4096])
    bg_d = din("b_g", [1, 1024])
    ng_d = din("ng", [128, KC])
    win_d = din("w_in_c", [D, 4096])
    gq_d = din("gq", [128, 1])
    gk_d = din("gk", [128, 1])
    rg_d = din("rgain", [128, 512])
    wout_d = din("w_out_c", [D, 1024])
    ident_d = din("ident", [128, 128], BF16)
    tri_d = din("tri", [128, 128], BF16)
    ones_d = din("ones", [128, 128], BF16)
    cm4_d = din("cmask4", [128, 512], BF16)
    dmask_d = din("dmask", [128, 4 * 512], BF16)
    rtab_d = din("rtab", [NB, 128, 4 * 256])
    gc_d = din("gc", [128, HL])
    out_d = nc.dram_tensor("out", [S, 1024], F32, kind="ExternalOutput")

    def dscr(name, shape, dt, d=dbg):
        if d:
            return nc.dram_tensor(name, list(shape), dt, kind="ExternalOutput")
        return nc.dram_tensor(name, list(shape), dt)

    hT_s = dscr("hT_s", [NT, 128, KC * 512], BF16)
    qT_s = dscr("qT_s", [HL, 128, S], BF16)
    kT_s = dscr("kT_s", [HL, 128, S], BF16)
    zg_s = dscr("zg_s", [HL, 128, S], BF16)
    v_s = dscr("v_s", [HL, 128, NB * 128], BF16)
    mixc = [[dscr(f"mixc_{i}_{hf}", [128, HALF], BF16, dbg and not gather) for hf in range(2)] for i in range(8)]
    if gather:
        gath = [[dscr(f"gath_{i}_{hf}", [256, HALF], BF16, False) for hf in range(2)] for i in range(8)]
    rgroups = [[2 * i, 2 * i + 1] for i in range(n_cores // 2)]
    gdump = nc.dram_tensor("gdump", [16, 256, HALF], BF16, kind="ExternalOutput") if (dbg and gather) else None

    es0 = ExitStack()
    with es0:
        def sb(name, shape, dt, es=es0):
            return es.enter_context(nc.sbuf_tensor("sb_" + name, list(shape), dt))

        def ps(name, shape, dt, es=es0):
            return es.enter_context(nc.psum_tensor("pt_" + name, list(shape), dt))

        phase_sems = []
        s_cc = nc.alloc_semaphore("s_cc_persist")

        def sem(name, es=None):
            h = nc.alloc_semaphore(name)
            phase_sems.append(h)
            return h

        def end_phase():
            nc.clear_and_free_semaphores(list(phase_sems))
            nc.all_engine_barrier()
            phase_sems.clear()

        ident = sb("ident", [128, 128], BF16)
        tri = sb("tri", [128, 128], BF16)
        ones = sb("ones", [128, 128], BF16)
        cm4 = sb("cm4", [128, 512], BF16)
        dmask = sb("dmask", [128, 4, 512], BF16)
        AB = sb("AB", [128, 32], F32)
        gate_bc = sb("gate_bc", [128, 1024], F32)
        gqs = sb("gqs", [128, 1], F32)
        gks = sb("gks", [128, 1], F32)
        rgain = sb("rgain", [128, 512], F32)
        gcs = sb("gcs", [128, HL], F32)
        onesf = sb("onesf", [1, 128], F32)

        es_a1w = ExitStack()
        a1_Wb = sb("a1_Wb", [128, KC, 2048], BF16, es_a1w)
        a1_wst = sb("a1_wst", [128, 2, 2048], F32, es_a1w)
        with ExitStack() as es:
            cT = sb("cT", [128, KC], F32, es)
            t0 = sb("t0", [128, KC], F32, es)
            t1 = sb("t1", [128, KC], F32, es)
            scv = sb("scv", [128, KC], F32, es)
            bssr = sb("bssr", [1, 4096], F32, es)
            rowss = sb("rowss", [1, 4096], F32, es)
            ngt = sb("ngt", [128, KC], F32, es)
            modsb = sb("modsb", [128, 32], F32, es)
            bgt = sb("bgt", [1, 1024], F32, es)
            grow = sb("grow", [1, 1024], F32, es)
            wss = sb("wss", [128, 3, 4096], F32, es)
            wgt = sb("wgt", [128, 2, 1024], F32, es)
            one1 = sb("one1", [1, 1], F32, es)
            pr = [ps(f"p0r{i}", [128, 512], F32, es) for i in range(8)]
            s_cst = sem("s_cst", es)
            s_wfull = [sem(f"s_wfull{i}", es) for i in range(3)]
            s_gfull = [sem(f"s_gfull{i}", es) for i in range(2)]
            s_wfree, s_gfree = sem("s_wfree", es), sem("s_gfree", es)
            s_a1w = [sem(f"s_a1w{i}", es) for i in range(2)]
            s_a1cv, s_a1cg = sem("s_a1cv", es), sem("s_a1cg", es)
            s_a, s_b, s_c, s_d, s_e, s_f, s_z, s_r, s_t = (sem("s_p0" + n, es) for n in "abcdefzrt")
            NCST = 13
            with nc.Block() as block:
                @block.sync
                def _(e):
                    for dst, src in ((ident[:], ident_d[:, :]), (tri[:], tri_d[:, :]), (ones[:], ones_d[:, :]),
                                     (cm4[:], cm4_d[:, :]),
                                     (dmask[:].rearrange("p r t -> p (r t)"), dmask_d[:, :]),
                                     (cT[:], cT_d[:, :]), (bssr[:], bss_d[:, :]), (ngt[:], ng_d[:, :]),
                                     (gqs[:], gq_d[:, :]), (gks[:], gk_d[:, :]), (rgain[:], rg_d[:, :]),
                                     (gcs[:], gc_d[:, :]), (bgt[:], bg_d[:, :])):
                        e.dma_start(out=dst, in_=src).then_inc(s_cst, 16)
                    for kc in range(KC):
                        if kc >= 3:
                            e.wait_ge(s_wfree, kc - 2)
                        e.dma_start(out=wss[:, kc % 3, :], in_=wss_d[kc * 128:(kc + 1) * 128, :]).then_inc(s_wfull[kc % 3], 16)
                        if kc >= 2:
                            e.wait_ge(s_a1cv if kc % 2 == 0 else s_a1cg, kc // 2)
                        e.dma_start(out=a1_wst[:, kc % 2, :], in_=win_d[kc * 128:(kc + 1) * 128, 0:2048]
                                    ).then_inc(s_a1w[kc % 2], 16)
                    for kc in range(KC):
                        if kc >= 2:
                            e.wait_ge(s_gfree, kc - 1)
                        e.dma_start(out=wgt[:, kc % 2, :], in_=wg_d[kc * 128:(kc + 1) * 128, :]).then_inc(s_gfull[kc % 2], 16)

                @block.scalar
                def _(a):
                    a.wait_ge(s_cst, 16 * NCST)
                    a.activation(t0[:], cT[:], AF.Exp, scale=-1.0).then_inc(s_a, 1)
                    a.wait_ge(s_a, 1)
                    a.activation(t1[:], t0[:], AF.Ln, bias=1.0).then_inc(s_a, 1)
                    a.wait_ge(s_a, 2)
                    a.activation(t0[:], t1[:], AF.Exp, scale=-1.0).then_inc(s_a, 1)

                @block.gpsimd
                def _(p):
                    for kc in range(1, KC, 2):
                        p.wait_ge(s_a1w[1], 16 * (kc // 2 + 1))
                        p.tensor_copy(a1_Wb[:, kc, :], a1_wst[:, 1, :]).then_inc(s_a1cg, 1)
                    p.wait_ge(s_a1cg, KC // 2)

                @block.vector
                def _(v):
                    v.memset(onesf[:], 1.0).then_inc(s_z, 1)
                    v.memset(one1[:], 1.0).then_inc(s_z, 1)
                    v.wait_ge(s_a, 3)
                    v.tensor_tensor(scv[:], cT[:], t0[:], ALU.mult).then_inc(s_b, 1)
                    v.tensor_scalar(gqs[:], gqs[:], -RSQ_D, None, ALU.mult).then_inc(s_z, 1)
                    for kc in range(0, KC, 2):
                        v.wait_ge(s_a1w[0], 16 * (kc // 2 + 1))
                        v.tensor_copy(a1_Wb[:, kc, :], a1_wst[:, 0, :]).then_inc(s_a1cv, 1)
                    v.wait_ge(s_c, 1)
                    for i in range(8):
                        v.tensor_tensor(rowss[:, i * 512:(i + 1) * 512], pr[i][0:1, :], bssr[:, i * 512:(i + 1) * 512],
                                        ALU.add).then_inc(s_r, 1)
                    v.wait_ge(s_d, 1)
                    for i in range(2):
                        v.tensor_tensor(grow[:, i * 512:(i + 1) * 512], pr[i][0:1, :], bgt[:, i * 512:(i + 1) * 512],
                                        ALU.add).then_inc(s_e, 1)
                    v.wait_ge(s_t, 1)
                    v.tensor_copy(modsb[:], pr[2][:, 0:32]).then_inc(s_f, 1)
                    v.wait_ge(s_f, 1)
                    v.tensor_scalar(AB[:, 0:16], modsb[:, 16:32], 1.0, None, ALU.add).then_inc(s_f, 1)
                    v.wait_ge(s_f, 2)
                    v.tensor_tensor(AB[:, 0:16], AB[:, 0:16], ngt[:], ALU.mult).then_inc(s_f, 1)
                    v.tensor_copy(AB[:, 16:32], modsb[:, 0:16]).then_inc(s_f, 1)
                    v.wait_ge(s_t, 3)
                    for i in range(2):
                        v.tensor_copy(gate_bc[:, i * 512:(i + 1) * 512], pr[3 + i][:]).then_inc(s_f, 1)
                    v.wait_ge(s_f, 6)
                    v.wait_ge(s_z, 3)
                    v.wait_ge(s_a1cv, KC // 2)

                @block.tensor
                def _(t):
                    t.wait_ge(s_b, 1)
                    t.wait_ge(s_z, 2)
                    for kc in range(KC):
                        t.wait_ge(s_wfull[kc % 3], 16 * (kc // 3 + 1))
                        for i in range(8):
                            ins = t.matmul(pr[i][0:1, :], scv[:, kc:kc + 1], wss[:, kc % 3, i * 512:(i + 1) * 512],
                                           start=(kc == 0), stop=(kc == KC - 1))
                        ins.then_inc(s_wfree, 1)
                    t.wait_ge(s_wfree, KC)
                    t.drain().then_inc(s_c, 1)
                    t.wait_ge(s_r, 8)
                    for kc in range(KC):
                        t.wait_ge(s_gfull[kc % 2], 16 * (kc // 2 + 1))
                        for i in range(2):
                            ins = t.matmul(pr[i][0:1, :], scv[:, kc:kc + 1], wgt[:, kc % 2, i * 512:(i + 1) * 512],
                                           start=(kc == 0), stop=(kc == KC - 1))
                        ins.then_inc(s_gfree, 1)
                    t.wait_ge(s_gfree, KC)
                    t.drain().then_inc(s_d, 1)
                    for j in range(32):
                        ins = t.matmul(pr[2][:, j:j + 1], rowss[0:1, j * 128:(j + 1) * 128], one1[0:1, 0:1],
                                       start=True, stop=True)
                    ins.then_inc(s_t, 1)
                    t.wait_ge(s_e, 2)
                    for i in range(2):
                        t.matmul(pr[3 + i][:], onesf[0:1, :], grow[0:1, i * 512:(i + 1) * 512],
                                 start=True, stop=True).then_inc(s_t, 1)

        end_phase()
        for k, v in list(locals().items()):
            if k not in ("g", "es", "block", "_"):
                setattr(g, k, v)
        if "A" in phases:
            phase_a1(g)
            end_phase()
        es_a1w.close()
        if "R" in phases:
            phase_a2(g)
            end_phase()
        es_wo = ExitStack()
        g.d_Wo = sb("d_Wo", [128, KC, 1024], BF16, es_wo)
        g.d_wst = sb("d_wst", [128, 2, 1024], F32, es_wo)
        g.wo_prefetched = False
        if "B" in phases:
            phase_b(g)
            end_phase()
        if "C" in phases:
            phase_cd(g)
            end_phase()
        es_wo.close()
    return nc


def _bf(a):
    return np.ascontiguousarray(a).astype(ml_dtypes.bfloat16)


def const_tables(S, hh):
    NB = S // 128
    j = np.arange(128)
    ident = np.eye(128, dtype=np.float32)
    tri = (j[:, None] >= j[None, :]).astype(np.float32)
    ones = np.ones((128, 128), np.float32)
    cm = (j[:, None] <= j[None, :]).astype(np.float32)
    cm4 = np.tile(cm, (1, 4))
    t = np.arange(512)
    dmask = np.stack([((r * 128 + j)[:, None] < t[None, :]).astype(np.float32) for r in range(4)], 1)
    half = 64
    inv_freq = (10000.0 ** (-np.arange(half, dtype=np.float32) / half)).astype(np.float32)
    ang = (np.arange(S, dtype=np.float32)[:, None] * inv_freq[None, :]).astype(np.float32)
    cos, sin = np.cos(ang).astype(np.float32), np.sin(ang).astype(np.float32)
    gh = hh * HL + np.arange(HL)
    log_gamma = np.log1p(-np.exp2(-5.0 - gh.astype(np.float32))).astype(np.float32)
    p = (np.arange(S) % 128).astype(np.float32)
    dq = np.exp((p[:, None] + 1.0) * log_gamma[None, :]).astype(np.float32)
    dk = (np.exp(-(p[:, None] + 1.0) * log_gamma[None, :]) * (128.0 ** -0.5)).astype(np.float32)
    cq = cos[:, None, :] * dq[:, :, None]
    sq = sin[:, None, :] * dq[:, :, None]
    ck = cos[:, None, :] * dk[:, :, None]
    sk = sin[:, None, :] * dk[:, :, None]
    rtab = np.stack([cq, sq, ck, sk], 1).astype(np.float32)
    rtab = rtab.reshape(NB, 128, 4 * 256)
    gc = np.broadcast_to(np.exp(128.0 * log_gamma)[None, :], (128, HL)).astype(np.float32)
    return dict(ident=_bf(ident), tri=_bf(tri), ones=_bf(ones), cmask4=_bf(cm4),
                dmask=_bf(dmask.reshape(128, 4 * 512)), rtab=np.ascontiguousarray(rtab),
                gc=np.ascontiguousarray(gc))


def chunkT(v):
    return np.ascontiguousarray(v.reshape(-1, 128).T)


def prep_core_inputs(inp, core, S):
    b, hh = core // 2, core % 2
    f = lambda a: np.ascontiguousarray(a, dtype=np.float32)
    x = inp["x"][b, :S]
    w_ada, b_ada = inp["w_ada"][0], inp["b_ada"][0]
    w_in, w_out = inp["w_in"][0], inp["w_out"][0]
    hs = slice(hh * 512, (hh + 1) * 512)
    grp = lambda gi: w_in[:, gi * 1024:(gi + 1) * 1024][:, hs]
    w_in_c = np.concatenate([grp(0), grp(1), grp(3), grp(2), grp(4), grp(5), grp(6), grp(7)], 1)
    rows = []
    for i in range(8):
        for r in range(2):
            gh = r * HL + (i % HL)
            base = gh * 128 if i < HL else 1024 + gh * 128
            rows.append(w_out[base:base + 128, hh * 1024:(hh + 1) * 1024])
    w_out_c = np.concatenate(rows, 0)
    d = dict(
        x=f(x), xres=f(x[:, hh * 1024:(hh + 1) * 1024]), cT=f(chunkT(inp["c"][b])),
        w_ada_ss=f(w_ada[:, 0:4096]), w_ada_g=f(w_ada[:, 4096 + hh * 1024:4096 + (hh + 1) * 1024]),
        b_ss=f(b_ada[0:4096][None, :]), b_g=f(b_ada[4096 + hh * 1024:4096 + (hh + 1) * 1024][None, :]),
        ng=f(chunkT(inp["norm_gain"][0])), w_in_c=f(w_in_c),
        gq=f(inp["sb_q_gain"][0][:, None]), gk=f(inp["sb_k_gain"][0][:, None]),
        rgain=f(np.broadcast_to(inp["ret_norm_gain"][0][hs][None, :], (128, 512))),
        w_out_c=f(w_out_c),
    )
    d.update(const_tables(S, hh))
    return d


_PROG = {}


def kernel(x, c, w_ada, b_ada, norm_gain, w_in, sb_q_gain, sb_k_gain, ret_norm_gain, w_out):
    inp = dict(x=np.asarray(x), c=np.asarray(c), w_ada=np.asarray(w_ada), b_ada=np.asarray(b_ada),
               norm_gain=np.asarray(norm_gain), w_in=np.asarray(w_in), sb_q_gain=np.asarray(sb_q_gain),
               sb_k_gain=np.asarray(sb_k_gain), ret_norm_gain=np.asarray(ret_norm_gain), w_out=np.asarray(w_out))
    B, S, _ = inp["x"].shape
    n = 2 * B
    if S not in _PROG:
        _PROG[S] = build_program(S, n_cores=n, gather=True)
    nc = _PROG[S]
    in_maps = [prep_core_inputs(inp, core, S) for core in range(n)]
    res = run_bass_kernel_spmd(nc, in_maps, core_ids=list(range(n)))
    out = np.empty((B, S, D), np.float32)
    for core in range(n):
        b, hh = core // 2, core % 2
        out[b, :, hh * 1024:(hh + 1) * 1024] = res.results[core]["out"]
    return out


def phase_a1(g):
    nc, S, NT, NB = g.nc, g.S, g.NT, g.NB
    with ExitStack() as es:
        sb = lambda n, s, d: g.sb("a1_" + n, s, d, es)
        ps = lambda n, s, d: g.ps("a1_" + n, s, d, es)
        sem = lambda n: g.sem("a1_" + n, es)
        Wb = g.a1_Wb
        xin = sb("xin", [128, 3, 2048], F32)
        junk = sb("junk", [128, 2048], BF16)
        xs = sb("xs", [128, 8, 2048], BF16)
        hT = sb("hT", [128, 2, KC, 512], BF16)
        sqb = sb("sqb", [128, 2, 512], BF16)
        lnb = sb("lnb", [128, 2, 512], F32)
        rsb = sb("rsb", [128, 2, 512], F32)
        stg = sb("stg", [128, 4, 512], BF16)
        vst = sb("vst", [128, 2, 512], BF16)
        ssq = sb("ssq", [128, NB], F32)
        lnx = sb("lnx", [128, NB], F32)
        rstd = sb("rstd", [128, NB], F32)
        tp = [ps(f"tp{i}", [128, 1024], BF16) for i in range(2)]
        pp = [ps(f"pp{i}", [128, 512], F32) for i in range(3)]
        mp = ps("mp", [128, 512], F32)
        pv = [ps(f"pv{i}", [128, 512], F32) for i in range(2)]
        s_xin = [sem(f"xin{i}") for i in range(3)]
        s_xs, s_tp, s_hT, s_pp, s_sq, s_ms, s_ln, s_rs = (sem(n) for n in
                                                         ("xs", "tp", "hT", "pp", "sq", "ms", "ln", "rs"))
        s_stgf, s_pv, s_vstf, s_act = sem("stgf"), sem("pv"), sem("vstf"), sem("act")
        s_stgd = [sem(f"stgd{i}") for i in range(4)]
        s_vstd = [sem(f"vstd{i}") for i in range(2)]
        s_hsp = [sem(f"hsp{i}") for i in range(2)]
        AB, gqs, gks, ident, ones = g.AB, g.gqs, g.gks, g.ident, g.ones
        dests = [g.qT_s] * 4 + [g.kT_s] * 4 + [g.zg_s] * 4

        with nc.Block() as block:
            @block.sync
            def _(e):
                for gb in range(NB):
                    if gb >= 3:
                        e.wait_ge(s_xs, gb - 2)
                    e.dma_start(out=xin[:, gb % 3, :], in_=g.x_d[gb * 128:(gb + 1) * 128, :]
                                ).then_inc(s_xin[gb % 3], 16)

            @block.scalar
            def _(a):
                na = [0]

                def selfsync(ins):
                    ins.then_inc(s_act, 1)
                    na[0] += 1
                    a.wait_ge(s_act, na[0])

                def blocks(T):
                    for b4 in range(4):
                        gb = 4 * T + b4
                        a.wait_ge(s_xin[gb % 3], 16 * (gb // 3 + 1))
                        selfsync(a.activation(junk[:], xin[:, gb % 3, :], AF.Square, accum_out=ssq[:, gb:gb + 1]))
                        selfsync(a.activation(lnx[:, gb:gb + 1], ssq[:, gb:gb + 1], AF.Ln, bias=EPS, scale=1.0 / D))
                        selfsync(a.activation(rstd[:, gb:gb + 1], lnx[:, gb:gb + 1], AF.Exp, scale=-0.5))
                        if T >= 2:
                            a.wait_ge(s_tp, 16 * (T - 1))
                        a.activation(xs[:, (T % 2) * 4 + b4, :], xin[:, gb % 3, :], AF.Copy,
                                     scale=rstd[:, gb:gb + 1]).then_inc(s_xs, 1)

                def lnexp(T, c):
                    n, m = 12 * T + c, 8 * T + c
                    a.wait_ge(s_ms, m + 1)
                    a.activation(lnb[:, m % 2, :], mp[:], AF.Ln, bias=EPS, scale=1.0 / 128).then_inc(s_ln, 1)
                    a.wait_ge(s_ln, m + 1)
                    if n >= 2:
                        a.wait_ge(s_stgf, n - 1)
                    a.activation(rsb[:, n % 2, :], lnb[:, m % 2, :], AF.Exp, scale=-0.5).then_inc(s_rs, 1)

                def chunks(T):
                    for c in range(12):
                        n = 12 * T + c
                        if c < 8:
                            m = 8 * T + c
                            a.wait_ge(s_pp, n + 1)
                            if m >= 2:
                                a.wait_ge(s_ms, m - 1)
                            a.activation(sqb[:, m % 2, :], pp[n % 3][:], AF.Square).then_inc(s_sq, 1)
                            if c >= 1:
                                lnexp(T, c - 1)
                        else:
                            if c == 8:
                                lnexp(T, 7)
                            a.wait_ge(s_pp, n + 1)
                            selfsync(a.activation(lnb[:, 0, :], pp[n % 3][:], AF.Exp, scale=-1.0))
                            selfsync(a.activation(lnb[:, 1, :], lnb[:, 0, :], AF.Ln, bias=1.0))
                            if n >= 2:
                                a.wait_ge(s_stgf, n - 1)
                            a.activation(rsb[:, n % 2, :], lnb[:, 1, :], AF.Exp, scale=-1.0).then_inc(s_rs, 1)

                blocks(0)
                for T in range(NT):
                    if T + 1 < NT:
                        blocks(T + 1)
                    chunks(T)

            @block.vector
            def _(v):
                def evac(T):
                    for kc in range(KC):
                        j = 16 * T + kc
                        v.wait_ge(s_tp, j + 1)
                        if kc == 0 and T >= 2:
                            v.wait_ge(s_pv, 4 * (T - 1))
                            v.wait_ge(s_hsp[T % 2], 16 * ((T - 2) // 2 + 1))
                        v.tensor_scalar(hT[:, T % 2, kc, :], tp[j % 2][:, 0:512], AB[:, kc:kc + 1],
                                        AB[:, 16 + kc:17 + kc], ALU.mult, ALU.add).then_inc(s_hT, 1)

                def finals(T):
                    for c in range(12):
                        n = 12 * T + c
                        v.wait_ge(s_rs, n + 1)
                        if n >= 4:
                            v.wait_ge(s_stgd[n % 4], 16 * ((n - 4) // 4 + 1))
                        if c < 8:
                            v.scalar_tensor_tensor(stg[:, n % 4, :], pp[n % 3][:], (gqs if c < 4 else gks)[:, 0:1],
                                                   rsb[:, n % 2, :], ALU.mult, ALU.mult).then_inc(s_stgf, 1)
                        else:
                            v.tensor_tensor(stg[:, n % 4, :], pp[n % 3][:], rsb[:, n % 2, :], ALU.mult
                                            ).then_inc(s_stgf, 1)
                    for b4 in range(4):
                        gb = 4 * T + b4
                        v.wait_ge(s_pv, gb + 1)
                        if gb >= 2:
                            v.wait_ge(s_vstd[gb % 2], 16 * ((gb - 2) // 2 + 1))
                        v.tensor_copy(vst[:, gb % 2, :], pv[gb % 2][:]).then_inc(s_vstf, 1)

                evac(0)
                for T in range(NT):
                    if T + 1 < NT:
                        evac(T + 1)
                    finals(T)

            @block.gpsimd
            def _(p):
                for T in range(NT):
                    p.wait_ge(s_hT, 16 * (T + 1))
                    p.dma_start(out=g.hT_s[T, :, :], in_=hT[:, T % 2, :, :].rearrange("p k t -> p (k t)")
                                ).then_inc(s_hsp[T % 2], 16)
                    for c in range(12):
                        n = 12 * T + c
                        p.wait_ge(s_stgf, n + 1)
                        p.dma_start(out=dests[c][c % 4, :, T * 512:(T + 1) * 512], in_=stg[:, n % 4, :]
                                    ).then_inc(s_stgd[n % 4], 16)
                    for b4 in range(4):
                        gb = 4 * T + b4
                        p.wait_ge(s_vstf, gb + 1)
                        p.dma_start(out=g.v_s.ap()[:, :, gb * 128:(gb + 1) * 128].rearrange("h p d -> p h d"),
                                    in_=vst[:, gb % 2, :].rearrange("p (h d) -> p h d", h=4)
                                    ).then_inc(s_vstd[gb % 2], 16)
                ntot = 12 * NT
                for i in range(4):
                    cnt = len([n for n in range(ntot) if n % 4 == i])
                    p.wait_ge(s_stgd[i], 16 * cnt)
                for i in range(2):
                    p.wait_ge(s_vstd[i], 16 * len([x for x in range(NB) if x % 2 == i]))
                    p.wait_ge(s_hsp[i], 16 * len([x for x in range(NT) if x % 2 == i]))

            @block.tensor
            def _(t):
                def trans(T):
                    t.wait_ge(s_xs, 4 * T + 4)
                    for kc in range(KC):
                        j = 16 * T + kc
                        if j >= 2:
                            t.wait_ge(s_hT, j - 1)
                        for b4 in range(4):
                            ins = t.transpose(tp[j % 2][:, b4 * 128:(b4 + 1) * 128],
                                              xs[:, (T % 2) * 4 + b4, kc * 128:(kc + 1) * 128], ident[:])
                        ins.then_inc(s_tp, 1)

                def ones_mm(T, c):
                    m = 8 * T + c
                    t.wait_ge(s_sq, m + 1)
                    if m >= 1:
                        t.wait_ge(s_ln, m)
                    t.matmul(mp[:], ones[:], sqb[:, m % 2, :], start=True, stop=True).then_inc(s_ms, 1)

                def inproj(T):
                    t.wait_ge(s_hT, 16 * (T + 1))
                    for c in range(12):
                        n = 12 * T + c
                        if n >= 3:
                            t.wait_ge(s_stgf, n - 2)
                        for kc in range(KC):
                            ins = t.matmul(pp[n % 3][:], Wb[:, kc, c * 128:(c + 1) * 128], hT[:, T % 2, kc, :],
                                           start=(kc == 0), stop=(kc == KC - 1))
                        ins.then_inc(s_pp, 1)
                        if 1 <= c <= 8:
                            ones_mm(T, c - 1)
                    for b4 in range(4):
                        gb = 4 * T + b4
                        if gb >= 2:
                            t.wait_ge(s_vstf, gb - 1)
                        for kc in range(KC):
                            ins = t.matmul(pv[gb % 2][:], hT[:, T % 2, kc, b4 * 128:(b4 + 1) * 128],
                                           Wb[:, kc, 1536:2048], start=(kc == 0), stop=(kc == KC - 1))
                        ins.then_inc(s_pv, 1)

                trans(0)
                for T in range(NT):
                    if T + 1 < NT:
                        trans(T + 1)
                    inproj(T)


def phase_a2(g):
    nc, S, NT, NB = g.nc, g.S, g.NT, g.NB
    TPH = NT // 2
    with ExitStack() as es:
        sb = lambda n, s, d: g.sb("a2_" + n, s, d, es)
        ps = lambda n, s, d: g.ps("a2_" + n, s, d, es)
        sem = lambda n: g.sem("a2_" + n, es)
        Wb = sb("Wb", [128, KC, 2048], BF16)
        wst = sb("wst", [128, 4, 1024], F32)
        hT = sb("hT", [128, 2, KC, 512], BF16)
        rtab = sb("rtab", [128, 3, 1024], F32)
        rt = sb("rt", [128, 8, 256], F32)
        qktm = sb("qktm", [128, 4, 2, 512], BF16)
        vb = sb("vb", [128, 6, 512], BF16)
        sgt = sb("sgt", [128, 2, 512], F32)
        sig = sb("sig", [128, 2, 512], F32)
        zgb = sb("zgb", [128, 6, 512], BF16)
        qkT = sb("qkT", [128, 4, 1024], BF16)
        PT = sb("PT", [128, 3, 512], BF16)
        Tst = sb("Tst", [128, 512], F32)
        Sb = sb("Sb", [128, 4, 512], BF16)
        st6 = sb("st6", [128, 4, 6], F32)
        mv = sb("mv", [128, 4, 2], F32)
        lnr = sb("lnr", [128, 4], F32)
        rs4 = sb("rs4", [128, 2, 4], F32)
        nmr = sb("nmr", [128, 2, 4], F32)
        yb = sb("yb", [128, 2, 512], F32)
        y2 = sb("y2", [128, 512], F32)
        mixtm = sb("mixtm", [128, 2, 512], BF16)
        mTs = sb("mTs", [128, 2, 4, 512], BF16)
        ip = [ps(f"ip{i}", [128, 512], F32) for i in range(3)]
        tq = ps("tq", [128, 1024], BF16)
        sc = ps("sc", [128, 512], F32)
        dsp = ps("ds", [128, 512], F32)
        ou = ps("ou", [128, 512], F32)
        mt = ps("mt", [128, 1024], BF16)
        s_wst = [sem(f"wst{i}") for i in range(4)]
        s_wcv, s_wcg = sem("wcv"), sem("wcg")
        s_h2 = [sem(f"h2{i}") for i in range(2)]
        s_rt = [sem(f"rt{i}") for i in range(3)]
        s_mst = [sem(f"mst{i}") for i in range(2)]
        (s_ip, s_rq, s_rk, s_cq, s_ck, s_v, s_sig, s_zg, s_tq, s_tqc, s_sc, s_ds, s_pt, s_tu, s_sbc, s_ou,
         s_bn, s_mv, s_r4, s_nm, s_yd, s_y2, s_mx, s_mt, s_mtc, s_act, s_init) = (
            sem(n) for n in ("ip", "rq", "rk", "cq", "ck", "v", "sig", "zg", "tq", "tqc", "sc", "ds", "pt", "tu",
                             "sbc", "ou", "bn", "mv", "r4", "nm", "yd", "y2", "mx", "mt", "mtc", "act", "init"))
        ident, cm4, rgain, gcs = g.ident, g.cm4, g.rgain, g.gcs
        NS = NB + 6
        okb = lambda b: 0 <= b < NB

        def v4(ap2d, j):
            return ap2d.rearrange("p (h two i) -> p h two i", h=4, two=2)[:, :, j, :]

        def t4(ap2d):
            return ap2d.rearrange("p (h i) -> p h i", h=4)

        with nc.Block() as block:
            @block.sync
            def _(e):
                for i in range(2 * KC):
                    if i >= 4:
                        e.wait_ge(s_wcv if i % 2 == 0 else s_wcg, i // 2 - 1)
                    kc, hf = i // 2, i % 2
                    e.dma_start(out=wst[:, i % 4, :],
                                in_=g.win_d[kc * 128:(kc + 1) * 128, 2048 + hf * 1024:2048 + (hf + 1) * 1024]
                                ).then_inc(s_wst[i % 4], 16)

                def loads(T):
                    if T >= 2:
                        e.wait_ge(s_ip, 16 * (T - 1))
                    e.dma_start(out=hT[:, T % 2, :, :].rearrange("p k t -> p (k t)"), in_=g.hT_s[T, :, :]
                                ).then_inc(s_h2[T % 2], 16)
                    for b4 in range(4):
                        gb = 4 * T + b4
                        if gb >= 3:
                            e.wait_ge(s_rk, gb - 2)
                        e.dma_start(out=rtab[:, gb % 3, :], in_=g.rtab_d[gb, :, :]).then_inc(s_rt[gb % 3], 16)

                def stores(T):
                    e.wait_ge(s_mtc, 4 * (T + 1))
                    for h in range(4):
                        e.dma_start(out=g.mixc[4 + h][T // TPH][:, (T % TPH) * 512:(T % TPH + 1) * 512],
                                    in_=mTs[:, T % 2, h, :]).then_inc(s_mst[T % 2], 16)

                for T in range(NT):
                    loads(T)
                    if T >= 2:
                        stores(T - 2)
                for T in range(max(NT - 2, 0), NT):
                    stores(T)
                for i in range(2):
                    e.wait_ge(s_mst[i], 64 * len([x for x in range(NT) if x % 2 == i]))

            @block.tensor
            def _(t):
                def IP(b):
                    T, b4 = b // 4, b % 4
                    if b4 == 0:
                        t.wait_ge(s_h2[T % 2], 16 * (T // 2 + 1))
                    for tau in range(4):
                        u = 4 * b + tau
                        if u >= 3:
                            pu = u - 3
                            t.wait_ge((s_rq, s_rk, s_v, s_zg)[pu % 4], pu // 4 + 1)
                        for kc in range(KC):
                            ins = t.matmul(ip[u % 3][:], hT[:, T % 2, kc, b4 * 128:(b4 + 1) * 128],
                                           Wb[:, kc, tau * 512:(tau + 1) * 512], start=(kc == 0), stop=(kc == KC - 1))
                        ins.then_inc(s_ip, 1)

                def TQ(b):
                    t.wait_ge(s_cq, b + 1)
                    t.wait_ge(s_ck, b + 1)
                    if b >= 1:
                        t.wait_ge(s_tqc, b)
                    for j in range(2):
                        for h in range(4):
                            ins = t.transpose(tq[:, j * 512 + h * 128:j * 512 + (h + 1) * 128],
                                              qktm[:, b % 4, j, h * 128:(h + 1) * 128], ident[:])
                    ins.then_inc(s_tq, 1)

                def SC(b):
                    t.wait_ge(s_tqc, b + 1)
                    if b >= 1:
                        t.wait_ge(s_pt, b)
                    for h in range(4):
                        ins = t.matmul(sc[:, h * 128:(h + 1) * 128], qkT[:, b % 4, 512 + h * 128:512 + (h + 1) * 128],
                                       qkT[:, b % 4, h * 128:(h + 1) * 128], start=True, stop=True)
                    ins.then_inc(s_sc, 1)
                    t.wait_ge(s_v, b + 1)
                    if b >= 1:
                        t.wait_ge(s_tu, 4 * b)
                    for h in range(4):
                        ins = t.matmul(dsp[:, h * 128:(h + 1) * 128], qktm[:, b % 4, 1, h * 128:(h + 1) * 128],
                                       vb[:, b % 6, h * 128:(h + 1) * 128], start=True, stop=True)
                    ins.then_inc(s_ds, 1)

                def OU(b):
                    t.wait_ge(s_pt, b + 1)
                    if b >= 1:
                        t.wait_ge(s_sbc, 4 * b)
                        t.wait_ge(s_yd, 4 * b)
                    for h in range(4):
                        hsl = slice(h * 128, (h + 1) * 128)
                        ins = t.matmul(ou[:, hsl], PT[:, b % 3, hsl], vb[:, b % 6, hsl], start=True, stop=(b == 0))
                        if b >= 1:
                            ins = t.matmul(ou[:, hsl], qkT[:, b % 4, hsl], Sb[:, b % 4, hsl], start=False, stop=True)
                    ins.then_inc(s_ou, 1)

                def MT(b):
                    t.wait_ge(s_mx, b + 1)
                    if b >= 1:
                        t.wait_ge(s_mtc, b)
                    for h in range(4):
                        ins = t.transpose(mt[:, h * 128:(h + 1) * 128], mixtm[:, b % 2, h * 128:(h + 1) * 128], ident[:])
                    ins.then_inc(s_mt, 1)

                t.wait_ge(s_wcv, KC)
                t.wait_ge(s_wcg, KC)
                for s in range(NS):
                    for fn, lag in ((IP, 0), (TQ, 1), (SC, 2), (OU, 3), (MT, 5)):
                        if okb(s - lag):
                            fn(s - lag)

            @block.vector
            def _(v):
                v.memset(Tst[:], 0.0).then_inc(s_init, 1)
                for i in range(0, 2 * KC, 2):
                    v.wait_ge(s_wst[i % 4], 16 * (i // 4 + 1))
                    v.tensor_copy(Wb[:, i // 2, 0:1024], wst[:, i % 4, :]).then_inc(s_wcv, 1)

                def ZG(b):
                    v.wait_ge(s_sig, b + 1)
                    if b >= 6:
                        v.wait_ge(s_mx, b - 5)
                    v.tensor_tensor(zgb[:, b % 6, :], ip[(4 * b + 3) % 3][:], sig[:, b % 2, :], ALU.mult
                                    ).then_inc(s_zg, 1)

                def PTM(b):
                    v.wait_ge(s_sc, b + 1)
                    if b >= 3:
                        v.wait_ge(s_ou, b - 2)
                    v.tensor_tensor(PT[:, b % 3, :], sc[:], cm4[:], ALU.mult).then_inc(s_pt, 1)

                def TU(b):
                    v.wait_ge(s_ds, b + 1)
                    if b >= 1:
                        v.wait_ge(s_sbc, 4 * b)
                    for h in range(4):
                        hsl = slice(h * 128, (h + 1) * 128)
                        v.scalar_tensor_tensor(Tst[:, hsl], Tst[:, hsl], gcs[:, h:h + 1], dsp[:, hsl],
                                               ALU.mult, ALU.add).then_inc(s_tu, 1)

                def ST(b):
                    v.wait_ge(s_ou, b + 1)
                    if b >= 1:
                        v.wait_ge(s_nm, b)
                        v.wait_ge(s_r4, b)
                    for h in range(4):
                        v.bn_stats(st6[:, h, :], ou[:, h * 128:(h + 1) * 128]).then_inc(s_bn, 1)
                    v.wait_ge(s_bn, 4 * (b + 1))
                    for h in range(4):
                        v.bn_aggr(mv[:, h, :], st6[:, h, :]).then_inc(s_mv, 1)

                def RR(b, which):
                    u = 4 * b + which
                    if which == 0:
                        v.wait_ge(s_rt[b % 3], 16 * (b // 3 + 1))
                    v.wait_ge(s_ip, u + 1)
                    if b >= 1:
                        v.wait_ge((s_cq, s_ck)[which], b)
                    src = ip[u % 3][:]
                    x1, x2 = v4(src, 0), v4(src, 1)
                    ct = t4(rtab[:, b % 3, which * 512:which * 512 + 256])
                    stt = t4(rtab[:, b % 3, which * 512 + 256:which * 512 + 512])
                    o = which * 4
                    v.tensor_tensor(t4(rt[:, o + 0, :]), x1, ct, ALU.mult)
                    v.tensor_tensor(t4(rt[:, o + 1, :]), x2, stt, ALU.mult)
                    v.tensor_tensor(t4(rt[:, o + 2, :]), x1, stt, ALU.mult)
                    v.tensor_tensor(t4(rt[:, o + 3, :]), x2, ct, ALU.mult).then_inc((s_rq, s_rk)[which], 1)

                def NMR(b):
                    v.wait_ge(s_r4, b + 1)
                    if b >= 2:
                        v.wait_ge(s_yd, 4 * (b - 1))
                    v.scalar_tensor_tensor(nmr[:, b % 2, :], mv[:, :, 0], -1.0, rs4[:, b % 2, :], ALU.mult, ALU.mult
                                           ).then_inc(s_nm, 1)

                for s in range(NS):
                    if okb(s - 1):
                        ZG(s - 1)
                    if okb(s - 3):
                        PTM(s - 3)
                        TU(s - 3)
                    if okb(s - 4):
                        ST(s - 4)
                    if okb(s):
                        RR(s, 0)
                        RR(s, 1)
                    if okb(s - 4):
                        NMR(s - 4)

            @block.gpsimd
            def _(p):
                for i in range(1, 2 * KC, 2):
                    p.wait_ge(s_wst[i % 4], 16 * (i // 4 + 1))
                    p.tensor_copy(Wb[:, i // 2, 1024:2048], wst[:, i % 4, :]).then_inc(s_wcg, 1)

                def SBC(b):
                    p.wait_ge(s_tu, 4 * (b + 1))
                    if b >= 3:
                        p.wait_ge(s_ou, b - 2)
                    for h in range(4):
                        hsl = slice(h * 128, (h + 1) * 128)
                        p.tensor_scalar(Sb[:, (b + 1) % 4, hsl], Tst[:, hsl], gcs[:, h:h + 1], 1.0, ALU.mult, ALU.mult
                                        ).then_inc(s_sbc, 1)

                def CC(b, which):
                    p.wait_ge((s_rq, s_rk)[which], b + 1)
                    if b >= 4:
                        p.wait_ge(s_ds, b - 3)
                    o = which * 4
                    dst = qktm[:, b % 4, which, :]
                    p.tensor_tensor(v4(dst, 0), t4(rt[:, o + 0, :]), t4(rt[:, o + 1, :]), ALU.subtract)
                    p.tensor_tensor(v4(dst, 1), t4(rt[:, o + 2, :]), t4(rt[:, o + 3, :]), ALU.add
                                    ).then_inc((s_cq, s_ck)[which], 1)

                def MIX(b):
                    p.wait_ge(s_yd, 4 * (b + 1))
                    if b >= 1:
                        p.wait_ge(s_mx, b)
                    p.tensor_tensor(y2[:], yb[:, b % 2, :], rgain[:], ALU.mult).then_inc(s_y2, 1)
                    p.wait_ge(s_y2, b + 1)
                    p.wait_ge(s_zg, b + 1)
                    if b >= 2:
                        p.wait_ge(s_mt, b - 1)
                    p.tensor_tensor(mixtm[:, b % 2, :], y2[:], zgb[:, b % 6, :], ALU.mult).then_inc(s_mx, 1)

                for s in range(NS):
                    if okb(s - 3):
                        SBC(s - 3)
                    if okb(s):
                        CC(s, 0)
                        CC(s, 1)
                    if okb(s - 4):
                        MIX(s - 4)

            @block.scalar
            def _(a):
                na = [0]

                def selfsync(ins):
                    ins.then_inc(s_act, 1)
                    na[0] += 1
                    a.wait_ge(s_act, na[0])

                def MTC(b):
                    T, b4 = b // 4, b % 4
                    a.wait_ge(s_mt, b + 1)
                    if b4 == 0 and T >= 2:
                        a.wait_ge(s_mst[T % 2], 64 * ((T - 2) // 2 + 1))
                    a.activation(mTs[:, T % 2, :, b4 * 128:(b4 + 1) * 128], t4(mt[:, 0:512]), AF.Copy
                                 ).then_inc(s_mtc, 1)

                def R4(b):
                    a.wait_ge(s_mv, 4 * (b + 1))
                    selfsync(a.activation(lnr[:], mv[:, :, 1], AF.Ln, bias=EPS))
                    if b >= 2:
                        a.wait_ge(s_yd, 4 * (b - 1))
                    a.activation(rs4[:, b % 2, :], lnr[:], AF.Exp, scale=-0.5).then_inc(s_r4, 1)

                def NORM(b):
                    a.wait_ge(s_nm, b + 1)
                    if b >= 2:
                        a.wait_ge(s_y2, b - 1)
                    for h in range(4):
                        hsl = slice(h * 128, (h + 1) * 128)
                        a.activation(yb[:, b % 2, hsl], ou[:, hsl], AF.Identity, bias=nmr[:, b % 2, h:h + 1],
                                     scale=rs4[:, b % 2, h:h + 1]).then_inc(s_yd, 1)

                def VZ(b):
                    a.wait_ge(s_ip, 4 * b + 3)
                    if b >= 6:
                        a.wait_ge(s_ou, b - 5)
                    a.activation(vb[:, b % 6, :], ip[(4 * b + 2) % 3][:], AF.Copy).then_inc(s_v, 1)
                    a.wait_ge(s_ip, 4 * b + 4)
                    pz = ip[(4 * b + 3) % 3][:]
                    selfsync(a.activation(sgt[:, 0, :], pz, AF.Exp, scale=-1.0))
                    selfsync(a.activation(sgt[:, 1, :], sgt[:, 0, :], AF.Ln, bias=1.0))
                    if b >= 2:
                        a.wait_ge(s_zg, b - 1)
                    a.activation(sig[:, b % 2, :], sgt[:, 1, :], AF.Exp, scale=-1.0).then_inc(s_sig, 1)

                def TQC(b):
                    a.wait_ge(s_tq, b + 1)
                    if b >= 4:
                        a.wait_ge(s_ou, b - 3)
                    a.activation(qkT[:, b % 4, :], tq[:], AF.Copy).then_inc(s_tqc, 1)

                for s in range(NS):
                    if okb(s - 6):
                        MTC(s - 6)
                    if okb(s - 4):
                        R4(s - 4)
                        NORM(s - 4)
                    if okb(s):
                        VZ(s)
                    if okb(s - 1):
                        TQC(s - 1)


def phase_b(g):
    nc, S, NT, NB = g.nc, g.S, g.NT, g.NB
    TPH = NT // 2
    tiles = []
    for h in range(HL):
        for qt in range(NT):
            for kb in range(4 * qt + 3, -1, -1):
                tiles.append(dict(h=h, qt=qt, kb=kb, first=(kb == 4 * qt + 3), last=(kb == 0),
                                  r=(kb - 4 * qt if kb >= 4 * qt else None), G=h * NT + qt))
    N = len(tiles)
    NG = HL * NT
    nmask = [0] * (N + 1)
    nrup = [0] * (N + 1)
    for i, tl in enumerate(tiles):
        nmask[i + 1] = nmask[i] + (1 if tl["r"] is not None else 0)
        nrup[i + 1] = nrup[i] + (0 if tl["last"] else 1)
    last_of_G = {}
    for i, tl in enumerate(tiles):
        last_of_G[tl["G"]] = i

    with ExitStack() as es:
        sb = lambda n, s, d: g.sb("b_" + n, s, d, es)
        ps = lambda n, s, d: g.ps("b_" + n, s, d, es)
        sem = lambda n: g.sem("b_" + n, es)
        KT = sb("KT", [128, 2, S], BF16)
        VV = sb("VV", [128, 2, S], BF16)
        QT = sb("QT", [128, 2, S], BF16)
        ZG = sb("ZG", [128, 2, S], BF16)
        EE = sb("EE", [128, 6, 2, 512], BF16)
        Lp = sb("Lp", [128, 3, 512], BF16)
        W = sb("W", [128, 3, 512], BF16)
        R = sb("R", [128, 3, 512], BF16)
        Rz = sb("Rz", [128, 512], BF16)
        mst = sb("mst", [128, 2, 512], BF16)
        ZC = [ps(f"ZC{i}", [128, 1024], F32) for i in range(2)]
        Zp = lambda i: ZC[i % 2][:, 0:512]
        Cp = lambda i: ZC[(i + 1) % 2][:, 512:1024]
        E = lambda i: EE[:, i % 6, 0, :]
        EC = lambda i: EE[:, (i + 3) % 6, 1, :]
        Op = [ps(f"Op{i}", [128, 512], F32) for i in range(2)]
        s_hd = [sem(f"hd{i}") for i in range(2)]
        s_hdB = [sem(f"hdB{i}") for i in range(2)]
        s_mst = [sem(f"mst{i}") for i in range(2)]
        s_qk, s_x, s_mask, s_ln, s_rup, s_cum, s_mul, s_pv, s_fin, s_init, s_hdone = (
            sem(n) for n in ("qk", "x", "mask", "ln", "rup", "cum", "mul", "pv", "fin", "init", "hdone"))
        s_wod = [sem(f"wod{i}") for i in range(2)]
        s_woc = sem("woc")
        g.wo_prefetched = True
        first_of_head = {}
        for i_, tl_ in enumerate(tiles):
            first_of_head.setdefault(tl_["h"], i_)

        def cc(p, i_chunk, halves=(0, 1)):
            for hf in halves:
                p.collective_compute("AllGather", ALU.bypass, replica_groups=g.rgroups,
                                     ins=[g.mixc[i_chunk][hf].ap().opt()], outs=[g.gath[i_chunk][hf].ap().opt()]
                                     ).then_inc(g.s_cc, 1)
        tri, ones, dmask = g.tri, g.ones, g.dmask
        NS = N + 8
        ok = lambda i: 0 <= i < N

        with nc.Block() as block:
            @block.sync
            def _(e):
                def load(h):
                    if h >= 2:
                        e.wait_ge(s_fin, (h - 1) * NT)
                    for dst, src in ((KT, g.kT_s), (QT, g.qT_s)):
                        e.dma_start(out=dst[:, h % 2, :], in_=src[h, :, :]).then_inc(s_hd[h % 2], 16)
                    for dst, src in ((VV, g.v_s), (ZG, g.zg_s)):
                        e.dma_start(out=dst[:, h % 2, :], in_=src[h, :, :]).then_inc(s_hdB[h % 2], 16)

                def wo_load(kc):
                    if kc >= 2:
                        e.wait_ge(s_woc, kc - 1)
                    e.dma_start(out=g.d_wst[:, kc % 2, :], in_=g.wout_d[kc * 128:(kc + 1) * 128, :]
                                ).then_inc(s_wod[kc % 2], 16)

                def stores(h):
                    for qt in range(NT):
                        G = h * NT + qt
                        if 2 <= G < 2 + KC // 2:
                            wo_load(2 * (G - 2))
                            wo_load(2 * (G - 2) + 1)
                        e.wait_ge(s_fin, G + 1)
                        e.dma_start(out=g.mixc[h][qt // TPH][:, (qt % TPH) * 512:(qt % TPH + 1) * 512],
                                    in_=mst[:, G % 2, :]).then_inc(s_mst[G % 2], 16)
                        if h == HL - 1 and qt == TPH - 1:
                            for i in range(2):
                                e.wait_ge(s_mst[i], 16 * len([x for x in range(G + 1) if x % 2 == i]))
                            e.nop().then_inc(s_hdone, 1)

                load(0)
                e.wait_ge(s_hd[0], 32)
                e.wait_ge(s_hdB[0], 32)
                if HL > 1:
                    load(1)
                for h in range(HL):
                    stores(h)
                    for i in range(2):
                        e.wait_ge(s_mst[i], 16 * len([x for x in range((h + 1) * NT) if x % 2 == i]))
                    e.nop().then_inc(s_hdone, 1)
                    if h + 2 < HL:
                        load(h + 2)

            @block.tensor
            def _(t):
                def QK(i):
                    tl = tiles[i]
                    h, qt, kb = tl["h"], tl["qt"], tl["kb"]
                    if tl["first"] and qt == 0:
                        t.wait_ge(s_hd[h % 2], 32 * (h // 2 + 1))
                    if i >= 2:
                        t.wait_ge(s_x, i - 1)
                    t.matmul(Zp(i), KT[:, h % 2, kb * 128:(kb + 1) * 128], QT[:, h % 2, qt * 512:(qt + 1) * 512],
                             start=True, stop=True).then_inc(s_qk, 1)

                def CUM(i):
                    tl = tiles[i]
                    t.wait_ge(s_ln, i + 1)
                    if i >= 2:
                        t.wait_ge(s_x, i + 2)
                    ins = t.matmul(Cp(i), tri[:], Lp[:, i % 3, :], start=True, stop=tl["first"])
                    if not tl["first"]:
                        t.wait_ge(s_rup, nrup[i])
                        ins = t.matmul(Cp(i), ones[:], R[:, i % 3, :], start=False, stop=True)
                    ins.then_inc(s_cum, 1)

                def PV(i):
                    tl = tiles[i]
                    h, kb, G = tl["h"], tl["kb"], tl["G"]
                    t.wait_ge(s_mul, i + 1)
                    if tl["first"] and tl["qt"] == 0:
                        t.wait_ge(s_hdB[h % 2], 32 * (h // 2 + 1))
                    if tl["first"] and G >= 2:
                        t.wait_ge(s_fin, G - 1)
                    t.matmul(Op[G % 2][:], VV[:, h % 2, kb * 128:(kb + 1) * 128], W[:, i % 3, :],
                             start=tl["first"], stop=tl["last"]).then_inc(s_pv, 1)

                for s in range(NS):
                    if ok(s):
                        QK(s)
                    if ok(s - 3):
                        CUM(s - 3)
                    if ok(s - 6):
                        PV(s - 6)

            @block.scalar
            def _(a):
                def XEXP(s_):
                    i1, i4 = s_ - 1, s_ - 4
                    if ok(i1):
                        a.wait_ge(s_qk, i1 + 1)
                        if i1 >= 6:
                            a.wait_ge(s_mul, i1 - 5)
                    if ok(i4):
                        a.wait_ge(s_cum, i4 + 1)
                        if i4 >= 3:
                            a.wait_ge(s_mul, i4 - 2)
                    if ok(i1) and ok(i4):
                        a.activation(EE[:, i1 % 6, :, :].rearrange("p a t -> p (a t)"), ZC[i1 % 2][:], AF.Exp, scale=-1.0
                                     ).then_inc(s_x, 1)
                    elif ok(i1):
                        a.activation(E(i1), Zp(i1), AF.Exp, scale=-1.0).then_inc(s_x, 1)
                    else:
                        a.activation(EC(i4), Cp(i4), AF.Exp, scale=-1.0).then_inc(s_x, 1)

                def LN(i):
                    tl = tiles[i]
                    a.wait_ge(s_x, i + 1)
                    if tl["r"] is not None:
                        a.wait_ge(s_mask, nmask[i + 1])
                    if i >= 3:
                        a.wait_ge(s_cum, i - 2)
                        a.wait_ge(s_rup, nrup[i - 2])
                    a.activation(Lp[:, i % 3, :], E(i), AF.Ln, bias=1.0).then_inc(s_ln, 1)

                for s in range(NS):
                    if ok(s - 1) or ok(s - 4):
                        XEXP(s)
                    if ok(s - 2):
                        LN(s - 2)

            @block.gpsimd
            def _(p):
                p.memset(Rz[:], 0.0).then_inc(s_init, 1)
                p.wait_ge(s_init, 1)
                if g.gather:
                    for i_chunk in range(HL, 2 * HL):
                        cc(p, i_chunk)

                if g.gather:
                    for h in range(HL - 1):
                        p.wait_ge(s_hdone, h + 1)
                        cc(p, h)
                    p.wait_ge(s_hdone, HL)
                    cc(p, HL - 1, halves=(0,))

            @block.vector
            def _(v):
                v.wait_ge(s_init, 1)

                def WCAST(kc):
                    v.wait_ge(s_wod[kc % 2], 16 * (kc // 2 + 1))
                    v.tensor_copy(g.d_Wo[:, kc, :], g.d_wst[:, kc % 2, :]).then_inc(s_woc, 1)

                def MASK(i):
                    tl = tiles[i]
                    if tl["r"] is None:
                        return
                    v.wait_ge(s_x, i + 1)
                    v.tensor_tensor(E(i), E(i), dmask[:, tl["r"], :], ALU.mult).then_inc(s_mask, 1)

                def RUP(i):
                    tl = tiles[i]
                    if tl["last"]:
                        return
                    v.wait_ge(s_ln, i + 1)
                    if i >= 2:
                        v.wait_ge(s_cum, i - 1)
                    v.wait_ge(s_rup, nrup[i])
                    src = Rz[:] if tl["first"] else R[:, i % 3, :]
                    v.tensor_tensor(R[:, (i + 1) % 3, :], src, Lp[:, i % 3, :], ALU.add).then_inc(s_rup, 1)

                def MUL(i):
                    v.wait_ge(s_x, i + 4)
                    if i >= 3:
                        v.wait_ge(s_pv, i - 2)
                    v.tensor_tensor(W[:, i % 3, :], E(i), EC(i), ALU.mult).then_inc(s_mul, 1)

                def FIN(G):
                    h, qt = G // NT, G % NT
                    v.wait_ge(s_pv, last_of_G[G] + 1)
                    if G >= 2:
                        v.wait_ge(s_mst[G % 2], 16 * ((G - 2) // 2 + 1))
                    v.tensor_tensor(mst[:, G % 2, :], Op[G % 2][:], ZG[:, h % 2, qt * 512:(qt + 1) * 512], ALU.mult
                                    ).then_inc(s_fin, 1)

                for s in range(NS):
                    if ok(s - 1):
                        MASK(s - 1)
                    if ok(s - 3):
                        RUP(s - 3)
                    if ok(s - 5):
                        MUL(s - 5)
                    if ok(s - 7) and tiles[s - 7]["last"]:
                        G_ = tiles[s - 7]["G"]
                        FIN(G_)
                        if 3 <= G_ < 3 + KC // 2:
                            WCAST(2 * (G_ - 3))
                            WCAST(2 * (G_ - 3) + 1)
                assert NG >= 3 + KC // 2
                v.wait_ge(s_woc, KC)


def phase_cd(g):
    nc, S, NT, NB = g.nc, g.S, g.NT, g.NB
    TPH = NT // 2
    gather = g.gather
    with ExitStack() as es:
        sb = lambda n, s, d: g.sb("d_" + n, s, d, es)
        ps = lambda n, s, d: g.ps("d_" + n, s, d, es)
        sem = lambda n: g.sem("d_" + n, es)
        Wo, wst = g.d_Wo, g.d_wst
        pre = g.wo_prefetched
        mg = sb("mg", [128, 2, KC, 512], BF16)
        xr = sb("xr", [128, 3, 1024], F32)
        yt = sb("yt", [128, 2, 512], F32)
        ost = sb("ost", [128, 3, 1024], F32)
        po = [ps(f"po{i}", [128, 512], F32) for i in range(4)]
        s_wst = [sem(f"wst{i}") for i in range(2)]
        s_wcv, s_wcg = sem("wcv"), sem("wcg")
        s_mg = [sem(f"mg{i}") for i in range(2)]
        s_xr = [sem(f"xr{i}") for i in range(3)]
        s_ost = [sem(f"ost{i}") for i in range(3)]
        s_cc = g.s_cc
        s_po, s_y1, s_y2 = sem("po"), sem("y1"), sem("y2")
        gate_bc = g.gate_bc
        NCC = 16

        with nc.Block() as block:
            @block.sync
            def _(e):
                for kc in range(0 if pre else KC):
                    if kc >= 2:
                        e.wait_ge(s_wcv if kc % 2 == 0 else s_wcg, kc // 2)
                    e.dma_start(out=wst[:, kc % 2, :], in_=g.wout_d[kc * 128:(kc + 1) * 128, :]
                                ).then_inc(s_wst[kc % 2], 16)
                if gather:
                    e.wait_ge(s_cc, NCC if g.gdump is not None else NCC - 1)
                    if g.gdump is not None:
                        s_gd = g.sem("d_gd")
                        for i in range(8):
                            for hf2 in range(2):
                                for r2 in range(2):
                                    e.dma_start(out=g.gdump[2 * i + hf2, r2 * 128:(r2 + 1) * 128, :],
                                                in_=g.gath[i][hf2][r2 * 128:(r2 + 1) * 128, :]).then_inc(s_gd, 16)
                        e.wait_ge(s_gd, 16 * 32)
                for T in range(NT):
                    hf, tt = T // TPH, T % TPH
                    if gather and T == TPH:
                        e.wait_ge(s_cc, NCC)
                    if T >= 2:
                        e.wait_ge(s_po, 8 * (T - 1))
                    for i in range(8):
                        if gather:
                            for r in range(2):
                                e.dma_start(out=mg[:, T % 2, 2 * i + r, :],
                                            in_=g.gath[i][hf][r * 128:(r + 1) * 128, tt * 512:(tt + 1) * 512]
                                            ).then_inc(s_mg[T % 2], 16)
                        else:
                            for r in range(2):
                                e.dma_start(out=mg[:, T % 2, 2 * i + r, :], in_=g.mixc[i][hf][:, tt * 512:(tt + 1) * 512]
                                            ).then_inc(s_mg[T % 2], 16)
                    for b4 in range(4):
                        gb = 4 * T + b4
                        if gb >= 3:
                            e.wait_ge(s_y2, 2 * gb - 4)
                        e.dma_start(out=xr[:, gb % 3, :], in_=g.xres_d[gb * 128:(gb + 1) * 128, :]
                                    ).then_inc(s_xr[gb % 3], 16)

            @block.tensor
            def _(t):
                if not pre:
                    t.wait_ge(s_wcv, KC // 2)
                    t.wait_ge(s_wcg, KC // 2)
                for T in range(NT):
                    t.wait_ge(s_mg[T % 2], 256 * (T // 2 + 1))
                    for b4 in range(4):
                        gb = 4 * T + b4
                        for nh in range(2):
                            idx = 2 * gb + nh
                            if idx >= 4:
                                t.wait_ge(s_y1, idx - 3)
                            chs = range(KC) if gather else range(0, KC, 2)
                            for n_, ch in enumerate(chs):
                                ins = t.matmul(po[idx % 4][:], mg[:, T % 2, ch, b4 * 128:(b4 + 1) * 128],
                                               Wo[:, ch, nh * 512:(nh + 1) * 512],
                                               start=(n_ == 0), stop=(n_ == len(chs) - 1))
                            ins.then_inc(s_po, 1)

            @block.vector
            def _(v):
                for kc in range(0, 0 if pre else KC, 2):
                    v.wait_ge(s_wst[0], 16 * (kc // 2 + 1))
                    v.tensor_copy(Wo[:, kc, :], wst[:, 0, :]).then_inc(s_wcv, 1)
                for idx in range(2 * NB):
                    nh = idx % 2
                    v.wait_ge(s_po, idx + 1)
                    if idx >= 2:
                        v.wait_ge(s_y2, idx - 1)
                    v.tensor_tensor(yt[:, idx % 2, :], po[idx % 4][:], gate_bc[:, nh * 512:(nh + 1) * 512], ALU.mult
                                    ).then_inc(s_y1, 1)

            @block.gpsimd
            def _(p):
                if gather:
                    for hf in range(1, 2):
                        p.collective_compute("AllGather", ALU.bypass, replica_groups=g.rgroups,
                                             ins=[g.mixc[HL - 1][hf].ap().opt()], outs=[g.gath[HL - 1][hf].ap().opt()]
                                             ).then_inc(s_cc, 1)
                for kc in range(1, 1 if pre else KC, 2):
                    p.wait_ge(s_wst[1], 16 * (kc // 2 + 1))
                    p.tensor_copy(Wo[:, kc, :], wst[:, 1, :]).then_inc(s_wcg, 1)
                for gb in range(NB):
                    for nh in range(2):
                        idx = 2 * gb + nh
                        p.wait_ge(s_y1, idx + 1)
                        if nh == 0:
                            p.wait_ge(s_xr[gb % 3], 16 * (gb // 3 + 1))
                            if gb >= 3:
                                p.wait_ge(s_ost[gb % 3], 16 * ((gb - 3) // 3 + 1))
                        p.tensor_tensor(ost[:, gb % 3, nh * 512:(nh + 1) * 512], yt[:, idx % 2, :],
                                        xr[:, gb % 3, nh * 512:(nh + 1) * 512], ALU.add).then_inc(s_y2, 1)
                    p.wait_ge(s_y2, 2 * gb + 2)
                    p.dma_start(out=g.out_d[gb * 128:(gb + 1) * 128, :], in_=ost[:, gb % 3, :]
                                ).then_inc(s_ost[gb % 3], 16)
                for i in range(3):
                    p.wait_ge(s_ost[i], 16 * len([x for x in range(NB) if x % 3 == i]))
```

```python
import math
from contextlib import ExitStack

import numpy as np
import ml_dtypes

import concourse.bass as bass
import concourse.mybir as mybir
from concourse.bass_utils import run_bass_kernel_spmd

F32 = mybir.dt.float32
BF16 = mybir.dt.bfloat16
AF = mybir.ActivationFunctionType
ALU = mybir.AluOpType
AX = mybir.AxisListType

D = 2048
KC = D // 128
HL = 4
EPS = 1e-6
RSQ_D = 1.0 / math.sqrt(128.0)


class G:
    pass


def build_program(S, n_cores=8, gather=True, phases="0ARBC", dbg=False):
    NT = S // 512
    NB = S // 128
    HALF = S // 2
    nc = bass.Bass("TRN2", target_bir_lowering=False)
    g = G()
    g.gather = gather

    def din(name, shape, dt=F32):
        return nc.dram_tensor(name, list(shape), dt, kind="ExternalInput")

    x_d = din("x", [S, D])
    xres_d = din("xres", [S, 1024])
    cT_d = din("cT", [128, KC])
    wss_d = din("w_ada_ss", [D, 4096])
    wg_d = din("w_ada_g", [D, 1024])
    bss_d = din("b_ss", [1, 4096])
    bg_d = din("b_g", [1, 1024])
    ng_d = din("ng", [128, KC])
    win_d = din("w_in_c", [D, 4096])
    gq_d = din("gq", [128, 1])
    gk_d = din("gk", [128, 1])
    rg_d = din("rgain", [128, 512])
    wout_d = din("w_out_c", [D, 1024])
    ident_d = din("ident", [128, 128], BF16)
    tri_d = din("tri", [128, 128], BF16)
    ones_d = din("ones", [128, 128], BF16)
    cm4_d = din("cmask4", [128, 512], BF16)
    dmask_d = din("dmask", [128, 4 * 512], BF16)
    rtab_d = din("rtab", [NB, 128, 4 * 256])
    gc_d = din("gc", [128, HL])
    out_d = nc.dram_tensor("out", [S, 1024], F32, kind="ExternalOutput")

    def dscr(name, shape, dt, d=dbg):
        if d:
            return nc.dram_tensor(name, list(shape), dt, kind="ExternalOutput")
        return nc.dram_tensor(name, list(shape), dt)

    hT_s = dscr("hT_s", [NT, 128, KC * 512], BF16)
    qT_s = dscr("qT_s", [HL, 128, S], BF16)
    kT_s = dscr("kT_s", [HL, 128, S], BF16)
    zg_s = dscr("zg_s", [HL, 128, S], BF16)
    v_s = dscr("v_s", [HL, 128, NB * 128], BF16)
    mixc = [[dscr(f"mixc_{i}_{hf}", [128, HALF], BF16, dbg and not gather) for hf in range(2)] for i in range(8)]
    if gather:
        gath = [[dscr(f"gath_{i}_{hf}", [256, HALF], BF16, False) for hf in range(2)] for i in range(8)]
    rgroups = [[2 * i, 2 * i + 1] for i in range(n_cores // 2)]
    gdump = nc.dram_tensor("gdump", [16, 256, HALF], BF16, kind="ExternalOutput") if (dbg and gather) else None

    es0 = ExitStack()
    with es0:
        def sb(name, shape, dt, es=es0):
            return es.enter_context(nc.sbuf_tensor("sb_" + name, list(shape), dt))

        def ps(name, shape, dt, es=es0):
            return es.enter_context(nc.psum_tensor("pt_" + name, list(shape), dt))

        phase_sems = []
        s_cc = nc.alloc_semaphore("s_cc_persist")

        def sem(name, es=None):
            h = nc.alloc_semaphore(name)
            phase_sems.append(h)
            return h

        def end_phase():
            nc.clear_and_free_semaphores(list(phase_sems))
            nc.all_engine_barrier()
            phase_sems.clear()

        ident = sb("ident", [128, 128], BF16)
        tri = sb("tri", [128, 128], BF16)
        ones = sb("ones", [128, 128], BF16)
        cm4 = sb("cm4", [128, 512], BF16)
        dmask = sb("dmask", [128, 4, 512], BF16)
        AB = sb("AB", [128, 32], F32)
        gate_bc = sb("gate_bc", [128, 1024], F32)
        gqs = sb("gqs", [128, 1], F32)
        gks = sb("gks", [128, 1], F32)
        rgain = sb("rgain", [128, 512], F32)
        gcs = sb("gcs", [128, HL], F32)
        onesf = sb("onesf", [1, 128], F32)

        es_a1w = ExitStack()
        a1_Wb = sb("a1_Wb", [128, KC, 2048], BF16, es_a1w)
        a1_wst = sb("a1_wst", [128, 2, 2048], F32, es_a1w)
        with ExitStack() as es:
            cT = sb("cT", [128, KC], F32, es)
            t0 = sb("t0", [128, KC], F32, es)
            t1 = sb("t1", [128, KC], F32, es)
            scv = sb("scv", [128, KC], F32, es)
            bssr = sb("bssr", [1, 4096], F32, es)
            rowss = sb("rowss", [1, 4096], F32, es)
            ngt = sb("ngt", [128, KC], F32, es)
            modsb = sb("modsb", [128, 32], F32, es)
            bgt = sb("bgt", [1, 1024], F32, es)
            grow = sb("grow", [1, 1024], F32, es)
            wss = sb("wss", [128, 3, 4096], F32, es)
            wgt = sb("wgt", [128, 2, 1024], F32, es)
            one1 = sb("one1", [1, 1], F32, es)
            pr = [ps(f"p0r{i}", [128, 512], F32, es) for i in range(8)]
            s_cst = sem("s_cst", es)
            s_wfull = [sem(f"s_wfull{i}", es) for i in range(3)]
            s_gfull = [sem(f"s_gfull{i}", es) for i in range(2)]
            s_wfree, s_gfree = sem("s_wfree", es), sem("s_gfree", es)
            s_a1w = [sem(f"s_a1w{i}", es) for i in range(2)]
            s_a1cv, s_a1cg = sem("s_a1cv", es), sem("s_a1cg", es)
            s_a, s_b, s_c, s_d, s_e, s_f, s_z, s_r, s_t = (sem("s_p0" + n, es) for n in "abcdefzrt")
            NCST = 13
            with nc.Block() as block:
                @block.sync
                def _(e):
                    for dst, src in ((ident[:], ident_d[:, :]), (tri[:], tri_d[:, :]), (ones[:], ones_d[:, :]),
                                     (cm4[:], cm4_d[:, :]),
                                     (dmask[:].rearrange("p r t -> p (r t)"), dmask_d[:, :]),
                                     (cT[:], cT_d[:, :]), (bssr[:], bss_d[:, :]), (ngt[:], ng_d[:, :]),
                                     (gqs[:], gq_d[:, :]), (gks[:], gk_d[:, :]), (rgain[:], rg_d[:, :]),
                                     (gcs[:], gc_d[:, :]), (bgt[:], bg_d[:, :])):
                        e.dma_start(out=dst, in_=src).then_inc(s_cst, 16)
                    for kc in range(KC):
                        if kc >= 3:
                            e.wait_ge(s_wfree, kc - 2)
                        e.dma_start(out=wss[:, kc % 3, :], in_=wss_d[kc * 128:(kc + 1) * 128, :]).then_inc(s_wfull[kc % 3], 16)
                        if kc >= 2:
                            e.wait_ge(s_a1cv if kc % 2 == 0 else s_a1cg, kc // 2)
                        e.dma_start(out=a1_wst[:, kc % 2, :], in_=win_d[kc * 128:(kc + 1) * 128, 0:2048]
                                    ).then_inc(s_a1w[kc % 2], 16)
                    for kc in range(KC):
                        if kc >= 2:
                            e.wait_ge(s_gfree, kc - 1)
                        e.dma_start(out=wgt[:, kc % 2, :], in_=wg_d[kc * 128:(kc + 1) * 128, :]).then_inc(s_gfull[kc % 2], 16)

                @block.scalar
                def _(a):
                    a.wait_ge(s_cst, 16 * NCST)
                    a.activation(t0[:], cT[:], AF.Exp, scale=-1.0).then_inc(s_a, 1)
                    a.wait_ge(s_a, 1)
                    a.activation(t1[:], t0[:], AF.Ln, bias=1.0).then_inc(s_a, 1)
                    a.wait_ge(s_a, 2)
                    a.activation(t0[:], t1[:], AF.Exp, scale=-1.0).then_inc(s_a, 1)

                @block.gpsimd
                def _(p):
                    for kc in range(1, KC, 2):
                        p.wait_ge(s_a1w[1], 16 * (kc // 2 + 1))
                        p.tensor_copy(a1_Wb[:, kc, :], a1_wst[:, 1, :]).then_inc(s_a1cg, 1)
                    p.wait_ge(s_a1cg, KC // 2)

                @block.vector
                def _(v):
                    v.memset(onesf[:], 1.0).then_inc(s_z, 1)
                    v.memset(one1[:], 1.0).then_inc(s_z, 1)
                    v.wait_ge(s_a, 3)
                    v.tensor_tensor(scv[:], cT[:], t0[:], ALU.mult).then_inc(s_b, 1)
                    v.tensor_scalar(gqs[:], gqs[:], -RSQ_D, None, ALU.mult).then_inc(s_z, 1)
                    for kc in range(0, KC, 2):
                        v.wait_ge(s_a1w[0], 16 * (kc // 2 + 1))
                        v.tensor_copy(a1_Wb[:, kc, :], a1_wst[:, 0, :]).then_inc(s_a1cv, 1)
                    v.wait_ge(s_c, 1)
                    for i in range(8):
                        v.tensor_tensor(rowss[:, i * 512:(i + 1) * 512], pr[i][0:1, :], bssr[:, i * 512:(i + 1) * 512],
                                        ALU.add).then_inc(s_r, 1)
                    v.wait_ge(s_d, 1)
                    for i in range(2):
                        v.tensor_tensor(grow[:, i * 512:(i + 1) * 512], pr[i][0:1, :], bgt[:, i * 512:(i + 1) * 512],
                                        ALU.add).then_inc(s_e, 1)
                    v.wait_ge(s_t, 1)
                    v.tensor_copy(modsb[:], pr[2][:, 0:32]).then_inc(s_f, 1)
                    v.wait_ge(s_f, 1)
                    v.tensor_scalar(AB[:, 0:16], modsb[:, 16:32], 1.0, None, ALU.add).then_inc(s_f, 1)
                    v.wait_ge(s_f, 2)
                    v.tensor_tensor(AB[:, 0:16], AB[:, 0:16], ngt[:], ALU.mult).then_inc(s_f, 1)
                    v.tensor_copy(AB[:, 16:32], modsb[:, 0:16]).then_inc(s_f, 1)
                    v.wait_ge(s_t, 3)
                    for i in range(2):
                        v.tensor_copy(gate_bc[:, i * 512:(i + 1) * 512], pr[3 + i][:]).then_inc(s_f, 1)
                    v.wait_ge(s_f, 6)
                    v.wait_ge(s_z, 3)
                    v.wait_ge(s_a1cv, KC // 2)

                @block.tensor
                def _(t):
                    t.wait_ge(s_b, 1)
                    t.wait_ge(s_z, 2)
                    for kc in range(KC):
                        t.wait_ge(s_wfull[kc % 3], 16 * (kc // 3 + 1))
                        for i in range(8):
                            ins = t.matmul(pr[i][0:1, :], scv[:, kc:kc + 1], wss[:, kc % 3, i * 512:(i + 1) * 512],
                                           start=(kc == 0), stop=(kc == KC - 1))
                        ins.then_inc(s_wfree, 1)
                    t.wait_ge(s_wfree, KC)
                    t.drain().then_inc(s_c, 1)
                    t.wait_ge(s_r, 8)
                    for kc in range(KC):
                        t.wait_ge(s_gfull[kc % 2], 16 * (kc // 2 + 1))
                        for i in range(2):
                            ins = t.matmul(pr[i][0:1, :], scv[:, kc:kc + 1], wgt[:, kc % 2, i * 512:(i + 1) * 512],
                                           start=(kc == 0), stop=(kc == KC - 1))
                        ins.then_inc(s_gfree, 1)
                    t.wait_ge(s_gfree, KC)
                    t.drain().then_inc(s_d, 1)
                    for j in range(32):
                        ins = t.matmul(pr[2][:, j:j + 1], rowss[0:1, j * 128:(j + 1) * 128], one1[0:1, 0:1],
                                       start=True, stop=True)
                    ins.then_inc(s_t, 1)
                    t.wait_ge(s_e, 2)
                    for i in range(2):
                        t.matmul(pr[3 + i][:], onesf[0:1, :], grow[0:1, i * 512:(i + 1) * 512],
                                 start=True, stop=True).then_inc(s_t, 1)

        end_phase()
        for k, v in list(locals().items()):
            if k not in ("g", "es", "block", "_"):
                setattr(g, k, v)
        if "A" in phases:
            phase_a1(g)
            end_phase()
        es_a1w.close()
        if "R" in phases:
            phase_a2(g)
            end_phase()
        es_wo = ExitStack()
        g.d_Wo = sb("d_Wo", [128, KC, 1024], BF16, es_wo)
        g.d_wst = sb("d_wst", [128, 2, 1024], F32, es_wo)
        g.wo_prefetched = False
        if "B" in phases:
            phase_b(g)
            end_phase()
        if "C" in phases:
            phase_cd(g)
            end_phase()
        es_wo.close()
    return nc


def _bf(a):
    return np.ascontiguousarray(a).astype(ml_dtypes.bfloat16)


def const_tables(S, hh):
    NB = S // 128
    j = np.arange(128)
    ident = np.eye(128, dtype=np.float32)
    tri = (j[:, None] >= j[None, :]).astype(np.float32)
    ones = np.ones((128, 128), np.float32)
    cm = (j[:, None] <= j[None, :]).astype(np.float32)
    cm4 = np.tile(cm, (1, 4))
    t = np.arange(512)
    dmask = np.stack([((r * 128 + j)[:, None] < t[None, :]).astype(np.float32) for r in range(4)], 1)
    half = 64
    inv_freq = (10000.0 ** (-np.arange(half, dtype=np.float32) / half)).astype(np.float32)
    ang = (np.arange(S, dtype=np.float32)[:, None] * inv_freq[None, :]).astype(np.float32)
    cos, sin = np.cos(ang).astype(np.float32), np.sin(ang).astype(np.float32)
    gh = hh * HL + np.arange(HL)
    log_gamma = np.log1p(-np.exp2(-5.0 - gh.astype(np.float32))).astype(np.float32)
    p = (np.arange(S) % 128).astype(np.float32)
    dq = np.exp((p[:, None] + 1.0) * log_gamma[None, :]).astype(np.float32)
    dk = (np.exp(-(p[:, None] + 1.0) * log_gamma[None, :]) * (128.0 ** -0.5)).astype(np.float32)
    cq = cos[:, None, :] * dq[:, :, None]
    sq = sin[:, None, :] * dq[:, :, None]
    ck = cos[:, None, :] * dk[:, :, None]
    sk = sin[:, None, :] * dk[:, :, None]
    rtab = np.stack([cq, sq, ck, sk], 1).astype(np.float32)
    rtab = rtab.reshape(NB, 128, 4 * 256)
    gc = np.broadcast_to(np.exp(128.0 * log_gamma)[None, :], (128, HL)).astype(np.float32)
    return dict(ident=_bf(ident), tri=_bf(tri), ones=_bf(ones), cmask4=_bf(cm4),
                dmask=_bf(dmask.reshape(128, 4 * 512)), rtab=np.ascontiguousarray(rtab),
                gc=np.ascontiguousarray(gc))


def chunkT(v):
    return np.ascontiguousarray(v.reshape(-1, 128).T)


def prep_core_inputs(inp, core, S):
    b, hh = core // 2, core % 2
    f = lambda a: np.ascontiguousarray(a, dtype=np.float32)
    x = inp["x"][b, :S]
    w_ada, b_ada = inp["w_ada"][0], inp["b_ada"][0]
    w_in, w_out = inp["w_in"][0], inp["w_out"][0]
    hs = slice(hh * 512, (hh + 1) * 512)
    grp = lambda gi: w_in[:, gi * 1024:(gi + 1) * 1024][:, hs]
    w_in_c = np.concatenate([grp(0), grp(1), grp(3), grp(2), grp(4), grp(5), grp(6), grp(7)], 1)
    rows = []
    for i in range(8):
        for r in range(2):
            gh = r * HL + (i % HL)
            base = gh * 128 if i < HL else 1024 + gh * 128
            rows.append(w_out[base:base + 128, hh * 1024:(hh + 1) * 1024])
    w_out_c = np.concatenate(rows, 0)
    d = dict(
        x=f(x), xres=f(x[:, hh * 1024:(hh + 1) * 1024]), cT=f(chunkT(inp["c"][b])),
        w_ada_ss=f(w_ada[:, 0:4096]), w_ada_g=f(w_ada[:, 4096 + hh * 1024:4096 + (hh + 1) * 1024]),
        b_ss=f(b_ada[0:4096][None, :]), b_g=f(b_ada[4096 + hh * 1024:4096 + (hh + 1) * 1024][None, :]),
        ng=f(chunkT(inp["norm_gain"][0])), w_in_c=f(w_in_c),
        gq=f(inp["sb_q_gain"][0][:, None]), gk=f(inp["sb_k_gain"][0][:, None]),
        rgain=f(np.broadcast_to(inp["ret_norm_gain"][0][hs][None, :], (128, 512))),
        w_out_c=f(w_out_c),
    )
    d.update(const_tables(S, hh))
    return d


_PROG = {}


def kernel(x, c, w_ada, b_ada, norm_gain, w_in, sb_q_gain, sb_k_gain, ret_norm_gain, w_out):
    inp = dict(x=np.asarray(x), c=np.asarray(c), w_ada=np.asarray(w_ada), b_ada=np.asarray(b_ada),
               norm_gain=np.asarray(norm_gain), w_in=np.asarray(w_in), sb_q_gain=np.asarray(sb_q_gain),
               sb_k_gain=np.asarray(sb_k_gain), ret_norm_gain=np.asarray(ret_norm_gain), w_out=np.asarray(w_out))
    B, S, _ = inp["x"].shape
    n = 2 * B
    if S not in _PROG:
        _PROG[S] = build_program(S, n_cores=n, gather=True)
    nc = _PROG[S]
    in_maps = [prep_core_inputs(inp, core, S) for core in range(n)]
    res = run_bass_kernel_spmd(nc, in_maps, core_ids=list(range(n)))
    out = np.empty((B, S, D), np.float32)
    for core in range(n):
        b, hh = core // 2, core % 2
        out[b, :, hh * 1024:(hh + 1) * 1024] = res.results[core]["out"]
    return out


def phase_a1(g):
    nc, S, NT, NB = g.nc, g.S, g.NT, g.NB
    with ExitStack() as es:
        sb = lambda n, s, d: g.sb("a1_" + n, s, d, es)
        ps = lambda n, s, d: g.ps("a1_" + n, s, d, es)
        sem = lambda n: g.sem("a1_" + n, es)
        Wb = g.a1_Wb
        xin = sb("xin", [128, 3, 2048], F32)
        junk = sb("junk", [128, 2048], BF16)
        xs = sb("xs", [128, 8, 2048], BF16)
        hT = sb("hT", [128, 2, KC, 512], BF16)
        sqb = sb("sqb", [128, 2, 512], BF16)
        lnb = sb("lnb", [128, 2, 512], F32)
        rsb = sb("rsb", [128, 2, 512], F32)
        stg = sb("stg", [128, 4, 512], BF16)
        vst = sb("vst", [128, 2, 512], BF16)
        ssq = sb("ssq", [128, NB], F32)
        lnx = sb("lnx", [128, NB], F32)
        rstd = sb("rstd", [128, NB], F32)
        tp = [ps(f"tp{i}", [128, 1024], BF16) for i in range(2)]
        pp = [ps(f"pp{i}", [128, 512], F32) for i in range(3)]
        mp = ps("mp", [128, 512], F32)
        pv = [ps(f"pv{i}", [128, 512], F32) for i in range(2)]
        s_xin = [sem(f"xin{i}") for i in range(3)]
        s_xs, s_tp, s_hT, s_pp, s_sq, s_ms, s_ln, s_rs = (sem(n) for n in
                                                         ("xs", "tp", "hT", "pp", "sq", "ms", "ln", "rs"))
        s_stgf, s_pv, s_vstf, s_act = sem("stgf"), sem("pv"), sem("vstf"), sem("act")
        s_stgd = [sem(f"stgd{i}") for i in range(4)]
        s_vstd = [sem(f"vstd{i}") for i in range(2)]
        s_hsp = [sem(f"hsp{i}") for i in range(2)]
        AB, gqs, gks, ident, ones = g.AB, g.gqs, g.gks, g.ident, g.ones
        dests = [g.qT_s] * 4 + [g.kT_s] * 4 + [g.zg_s] * 4

        with nc.Block() as block:
            @block.sync
            def _(e):
                for gb in range(NB):
                    if gb >= 3:
                        e.wait_ge(s_xs, gb - 2)
                    e.dma_start(out=xin[:, gb % 3, :], in_=g.x_d[gb * 128:(gb + 1) * 128, :]
                                ).then_inc(s_xin[gb % 3], 16)

            @block.scalar
            def _(a):
                na = [0]

                def selfsync(ins):
                    ins.then_inc(s_act, 1)
                    na[0] += 1
                    a.wait_ge(s_act, na[0])

                def blocks(T, only=None):
                    for b4 in (range(4) if only is None else (only,)):
                        gb = 4 * T + b4
                        a.wait_ge(s_xin[gb % 3], 16 * (gb // 3 + 1))
                        selfsync(a.activation(junk[:], xin[:, gb % 3, :], AF.Square, accum_out=ssq[:, gb:gb + 1]))
                        selfsync(a.activation(lnx[:, gb:gb + 1], ssq[:, gb:gb + 1], AF.Ln, bias=EPS, scale=1.0 / D))
                        selfsync(a.activation(rstd[:, gb:gb + 1], lnx[:, gb:gb + 1], AF.Exp, scale=-0.5))
                        if T >= 2:
                            a.wait_ge(s_tp, 16 * (T - 1))
                        a.activation(xs[:, (T % 2) * 4 + b4, :], xin[:, gb % 3, :], AF.Copy,
                                     scale=rstd[:, gb:gb + 1]).then_inc(s_xs, 1)

                def lnexp(T, c):
                    n, m = 12 * T + c, 8 * T + c
                    a.wait_ge(s_ms, m + 1)
                    a.activation(lnb[:, m % 2, :], mp[:], AF.Ln, bias=EPS, scale=1.0 / 128).then_inc(s_ln, 1)
                    a.wait_ge(s_ln, m + 1)
                    if n >= 2:
                        a.wait_ge(s_stgf, n - 1)
                    a.activation(rsb[:, n % 2, :], lnb[:, m % 2, :], AF.Exp, scale=-0.5).then_inc(s_rs, 1)

                def chunks(T, after=None):
                    for c in range(12):
                        if after is not None and c in (3, 6, 9):
                            after(c // 3 - 1)
                        n = 12 * T + c
                        if c < 8:
                            m = 8 * T + c
                            a.wait_ge(s_pp, n + 1)
                            if m >= 2:
                                a.wait_ge(s_ms, m - 1)
                            a.activation(sqb[:, m % 2, :], pp[n % 3][:], AF.Square).then_inc(s_sq, 1)
                            if c >= 1:
                                lnexp(T, c - 1)
                        else:
                            if c == 8:
                                lnexp(T, 7)
                            a.wait_ge(s_pp, n + 1)
                            selfsync(a.activation(lnb[:, 0, :], pp[n % 3][:], AF.Exp, scale=-1.0))
                            selfsync(a.activation(lnb[:, 1, :], lnb[:, 0, :], AF.Ln, bias=1.0))
                            if n >= 2:
                                a.wait_ge(s_stgf, n - 1)
                            a.activation(rsb[:, n % 2, :], lnb[:, 1, :], AF.Exp, scale=-1.0).then_inc(s_rs, 1)

                blocks(0)
                if NT > 1:
                    blocks(1)
                for T in range(NT):
                    if T + 2 < NT:
                        chunks(T, after=lambda b4, T2=T + 2: blocks(T2, only=b4))
                        blocks(T + 2, only=3)
                    else:
                        chunks(T)

            @block.vector
            def _(v):
                def evac(T):
                    for kc in range(KC):
                        j = 16 * T + kc
                        v.wait_ge(s_tp, j + 1)
                        if kc == 0 and T >= 2:
                            v.wait_ge(s_pv, 4 * (T - 1))
                            v.wait_ge(s_hsp[T % 2], 16 * ((T - 2) // 2 + 1))
                        v.tensor_scalar(hT[:, T % 2, kc, :], tp[j % 2][:, 0:512], AB[:, kc:kc + 1],
                                        AB[:, 16 + kc:17 + kc], ALU.mult, ALU.add).then_inc(s_hT, 1)

                def finals(T):
                    for c in range(12):
                        n = 12 * T + c
                        v.wait_ge(s_rs, n + 1)
                        if n >= 4:
                            v.wait_ge(s_stgd[n % 4], 16 * ((n - 4) // 4 + 1))
                        if c < 8:
                            v.scalar_tensor_tensor(stg[:, n % 4, :], pp[n % 3][:], (gqs if c < 4 else gks)[:, 0:1],
                                                   rsb[:, n % 2, :], ALU.mult, ALU.mult).then_inc(s_stgf, 1)
                        else:
                            v.tensor_tensor(stg[:, n % 4, :], pp[n % 3][:], rsb[:, n % 2, :], ALU.mult
                                            ).then_inc(s_stgf, 1)
                    for b4 in range(4):
                        gb = 4 * T + b4
                        v.wait_ge(s_pv, gb + 1)
                        if gb >= 2:
                            v.wait_ge(s_vstd[gb % 2], 16 * ((gb - 2) // 2 + 1))
                        v.tensor_copy(vst[:, gb % 2, :], pv[gb % 2][:]).then_inc(s_vstf, 1)

                evac(0)
                for T in range(NT):
                    if T + 1 < NT:
                        evac(T + 1)
                    finals(T)

            @block.gpsimd
            def _(p):
                for T in range(NT):
                    p.wait_ge(s_hT, 16 * (T + 1))
                    p.dma_start(out=g.hT_s[T, :, :], in_=hT[:, T % 2, :, :].rearrange("p k t -> p (k t)")
                                ).then_inc(s_hsp[T % 2], 16)
                    for c in range(12):
                        n = 12 * T + c
                        p.wait_ge(s_stgf, n + 1)
                        p.dma_start(out=dests[c][c % 4, :, T * 512:(T + 1) * 512], in_=stg[:, n % 4, :]
                                    ).then_inc(s_stgd[n % 4], 16)
                    for b4 in range(4):
                        gb = 4 * T + b4
                        p.wait_ge(s_vstf, gb + 1)
                        p.dma_start(out=g.v_s.ap()[:, :, gb * 128:(gb + 1) * 128].rearrange("h p d -> p h d"),
                                    in_=vst[:, gb % 2, :].rearrange("p (h d) -> p h d", h=4)
                                    ).then_inc(s_vstd[gb % 2], 16)
                ntot = 12 * NT
                for i in range(4):
                    cnt = len([n for n in range(ntot) if n % 4 == i])
                    p.wait_ge(s_stgd[i], 16 * cnt)
                for i in range(2):
                    p.wait_ge(s_vstd[i], 16 * len([x for x in range(NB) if x % 2 == i]))
                    p.wait_ge(s_hsp[i], 16 * len([x for x in range(NT) if x % 2 == i]))

            @block.tensor
            def _(t):
                def trans(T):
                    t.wait_ge(s_xs, 4 * T + 4)
                    for kc in range(KC):
                        j = 16 * T + kc
                        if j >= 2:
                            t.wait_ge(s_hT, j - 1)
                        for b4 in range(4):
                            ins = t.transpose(tp[j % 2][:, b4 * 128:(b4 + 1) * 128],
                                              xs[:, (T % 2) * 4 + b4, kc * 128:(kc + 1) * 128], ident[:])
                        ins.then_inc(s_tp, 1)

                def ones_mm(T, c):
                    m = 8 * T + c
                    t.wait_ge(s_sq, m + 1)
                    if m >= 1:
                        t.wait_ge(s_ln, m)
                    t.matmul(mp[:], ones[:], sqb[:, m % 2, :], start=True, stop=True).then_inc(s_ms, 1)

                def inproj(T):
                    t.wait_ge(s_hT, 16 * (T + 1))
                    for c in range(12):
                        n = 12 * T + c
                        if n >= 3:
                            t.wait_ge(s_stgf, n - 2)
                        for kc in range(KC):
                            ins = t.matmul(pp[n % 3][:], Wb[:, kc, c * 128:(c + 1) * 128], hT[:, T % 2, kc, :],
                                           start=(kc == 0), stop=(kc == KC - 1))
                        ins.then_inc(s_pp, 1)
                        if 1 <= c <= 8:
                            ones_mm(T, c - 1)
                    for b4 in range(4):
                        gb = 4 * T + b4
                        if gb >= 2:
                            t.wait_ge(s_vstf, gb - 1)
                        for kc in range(KC):
                            ins = t.matmul(pv[gb % 2][:], hT[:, T % 2, kc, b4 * 128:(b4 + 1) * 128],
                                           Wb[:, kc, 1536:2048], start=(kc == 0), stop=(kc == KC - 1))
                        ins.then_inc(s_pv, 1)

                trans(0)
                for T in range(NT):
                    if T + 1 < NT:
                        trans(T + 1)
                    inproj(T)


def phase_a2(g):
    nc, S, NT, NB = g.nc, g.S, g.NT, g.NB
    TPH = NT // 2
    with ExitStack() as es:
        sb = lambda n, s, d: g.sb("a2_" + n, s, d, es)
        ps = lambda n, s, d: g.ps("a2_" + n, s, d, es)
        sem = lambda n: g.sem("a2_" + n, es)
        Wb = sb("Wb", [128, KC, 2048], BF16)
        wst = sb("wst", [128, 4, 1024], F32)
        hT = sb("hT", [128, 2, KC, 512], BF16)
        rtab = sb("rtab", [128, 3, 1024], F32)
        rt = sb("rt", [128, 8, 256], F32)
        qktm = sb("qktm", [128, 4, 2, 512], BF16)
        vb = sb("vb", [128, 6, 512], BF16)
        sgt = sb("sgt", [128, 2, 512], F32)
        sig = sb("sig", [128, 2, 512], F32)
        zgb = sb("zgb", [128, 6, 512], BF16)
        qkT = sb("qkT", [128, 4, 1024], BF16)
        PT = sb("PT", [128, 3, 512], BF16)
        Tst = sb("Tst", [128, 512], F32)
        Sb = sb("Sb", [128, 4, 512], BF16)
        st6 = sb("st6", [128, 4, 6], F32)
        mv = sb("mv", [128, 4, 2], F32)
        lnr = sb("lnr", [128, 4], F32)
        rs4 = sb("rs4", [128, 2, 4], F32)
        nmr = sb("nmr", [128, 2, 4], F32)
        yb = sb("yb", [128, 2, 512], F32)
        y2 = sb("y2", [128, 512], F32)
        mixtm = sb("mixtm", [128, 2, 512], BF16)
        mTs = sb("mTs", [128, 2, 4, 512], BF16)
        ip = [ps(f"ip{i}", [128, 512], F32) for i in range(3)]
        tq = ps("tq", [128, 1024], BF16)
        sc = ps("sc", [128, 512], F32)
        dsp = ps("ds", [128, 512], F32)
        ou = ps("ou", [128, 512], F32)
        mt = ps("mt", [128, 1024], BF16)
        s_wst = [sem(f"wst{i}") for i in range(4)]
        s_wcv, s_wcg = sem("wcv"), sem("wcg")
        s_h2 = [sem(f"h2{i}") for i in range(2)]
        s_rt = [sem(f"rt{i}") for i in range(3)]
        s_mst = [sem(f"mst{i}") for i in range(2)]
        (s_ip, s_rq, s_rk, s_cq, s_ck, s_v, s_sig, s_zg, s_tq, s_tqc, s_sc, s_ds, s_pt, s_tu, s_sbc, s_ou,
         s_bn, s_mv, s_r4, s_nm, s_yd, s_y2, s_mx, s_mt, s_mtc, s_act, s_init) = (
            sem(n) for n in ("ip", "rq", "rk", "cq", "ck", "v", "sig", "zg", "tq", "tqc", "sc", "ds", "pt", "tu",
                             "sbc", "ou", "bn", "mv", "r4", "nm", "yd", "y2", "mx", "mt", "mtc", "act", "init"))
        ident, cm4, rgain, gcs = g.ident, g.cm4, g.rgain, g.gcs
        NS = NB + 6
        okb = lambda b: 0 <= b < NB

        def v4(ap2d, j):
            return ap2d.rearrange("p (h two i) -> p h two i", h=4, two=2)[:, :, j, :]

        def t4(ap2d):
            return ap2d.rearrange("p (h i) -> p h i", h=4)

        with nc.Block() as block:
            @block.sync
            def _(e):
                for i in range(2 * KC):
                    if i >= 4:
                        e.wait_ge(s_wcv if i % 2 == 0 else s_wcg, i // 2 - 1)
                    kc, hf = i // 2, i % 2
                    e.dma_start(out=wst[:, i % 4, :],
                                in_=g.win_d[kc * 128:(kc + 1) * 128, 2048 + hf * 1024:2048 + (hf + 1) * 1024]
                                ).then_inc(s_wst[i % 4], 16)

                def loads(T):
                    if T >= 2:
                        e.wait_ge(s_ip, 16 * (T - 1))
                    e.dma_start(out=hT[:, T % 2, :, :].rearrange("p k t -> p (k t)"), in_=g.hT_s[T, :, :]
                                ).then_inc(s_h2[T % 2], 16)
                    for b4 in range(4):
                        gb = 4 * T + b4
                        if gb >= 3:
                            e.wait_ge(s_rk, gb - 2)
                        e.dma_start(out=rtab[:, gb % 3, :], in_=g.rtab_d[gb, :, :]).then_inc(s_rt[gb % 3], 16)

                def stores(T):
                    e.wait_ge(s_mtc, 4 * (T + 1))
                    for h in range(4):
                        e.dma_start(out=g.mixc[4 + h][T // TPH][:, (T % TPH) * 512:(T % TPH + 1) * 512],
                                    in_=mTs[:, T % 2, h, :]).then_inc(s_mst[T % 2], 16)

                for T in range(NT):
                    loads(T)
                    if T >= 2:
                        stores(T - 2)
                for T in range(max(NT - 2, 0), NT):
                    stores(T)
                for i in range(2):
                    e.wait_ge(s_mst[i], 64 * len([x for x in range(NT) if x % 2 == i]))

            @block.tensor
            def _(t):
                def IP(b):
                    T, b4 = b // 4, b % 4
                    if b4 == 0:
                        t.wait_ge(s_h2[T % 2], 16 * (T // 2 + 1))
                    for tau in range(4):
                        u = 4 * b + tau
                        if u >= 3:
                            pu = u - 3
                            t.wait_ge((s_rq, s_rk, s_v, s_zg)[pu % 4], pu // 4 + 1)
                        for kc in range(KC):
                            ins = t.matmul(ip[u % 3][:], hT[:, T % 2, kc, b4 * 128:(b4 + 1) * 128],
                                           Wb[:, kc, tau * 512:(tau + 1) * 512], start=(kc == 0), stop=(kc == KC - 1))
                        ins.then_inc(s_ip, 1)

                def TQ(b):
                    t.wait_ge(s_cq, b + 1)
                    t.wait_ge(s_ck, b + 1)
                    if b >= 1:
                        t.wait_ge(s_tqc, b)
                    for j in range(2):
                        for h in range(4):
                            ins = t.transpose(tq[:, j * 512 + h * 128:j * 512 + (h + 1) * 128],
                                              qktm[:, b % 4, j, h * 128:(h + 1) * 128], ident[:])
                    ins.then_inc(s_tq, 1)

                def SC(b):
                    t.wait_ge(s_tqc, b + 1)
                    if b >= 1:
                        t.wait_ge(s_pt, b)
                    for h in range(4):
                        ins = t.matmul(sc[:, h * 128:(h + 1) * 128], qkT[:, b % 4, 512 + h * 128:512 + (h + 1) * 128],
                                       qkT[:, b % 4, h * 128:(h + 1) * 128], start=True, stop=True)
                    ins.then_inc(s_sc, 1)
                    t.wait_ge(s_v, b + 1)
                    if b >= 1:
                        t.wait_ge(s_tu, 4 * b)
                    for h in range(4):
                        ins = t.matmul(dsp[:, h * 128:(h + 1) * 128], qktm[:, b % 4, 1, h * 128:(h + 1) * 128],
                                       vb[:, b % 6, h * 128:(h + 1) * 128], start=True, stop=True)
                    ins.then_inc(s_ds, 1)

                def OU(b):
                    t.wait_ge(s_pt, b + 1)
                    if b >= 1:
                        t.wait_ge(s_sbc, 4 * b)
                        t.wait_ge(s_yd, 4 * b)
                    for h in range(4):
                        hsl = slice(h * 128, (h + 1) * 128)
                        ins = t.matmul(ou[:, hsl], PT[:, b % 3, hsl], vb[:, b % 6, hsl], start=True, stop=(b == 0))
                        if b >= 1:
                            ins = t.matmul(ou[:, hsl], qkT[:, b % 4, hsl], Sb[:, b % 4, hsl], start=False, stop=True)
                    ins.then_inc(s_ou, 1)

                def MT(b):
                    t.wait_ge(s_mx, b + 1)
                    if b >= 1:
                        t.wait_ge(s_mtc, b)
                    for h in range(4):
                        ins = t.transpose(mt[:, h * 128:(h + 1) * 128], mixtm[:, b % 2, h * 128:(h + 1) * 128], ident[:])
                    ins.then_inc(s_mt, 1)

                t.wait_ge(s_wcv, KC)
                t.wait_ge(s_wcg, KC)
                for s in range(NS):
                    for fn, lag in ((IP, 0), (TQ, 1), (SC, 2), (OU, 3), (MT, 5)):
                        if okb(s - lag):
                            fn(s - lag)

            @block.vector
            def _(v):
                v.memset(Tst[:], 0.0).then_inc(s_init, 1)
                for i in range(0, 2 * KC, 2):
                    v.wait_ge(s_wst[i % 4], 16 * (i // 4 + 1))
                    v.tensor_copy(Wb[:, i // 2, 0:1024], wst[:, i % 4, :]).then_inc(s_wcv, 1)

                def ZG(b):
                    v.wait_ge(s_sig, b + 1)
                    if b >= 6:
                        v.wait_ge(s_mx, b - 5)
                    v.tensor_tensor(zgb[:, b % 6, :], ip[(4 * b + 3) % 3][:], sig[:, b % 2, :], ALU.mult
                                    ).then_inc(s_zg, 1)

                def PTM(b):
                    v.wait_ge(s_sc, b + 1)
                    if b >= 3:
                        v.wait_ge(s_ou, b - 2)
                    v.tensor_tensor(PT[:, b % 3, :], sc[:], cm4[:], ALU.mult).then_inc(s_pt, 1)

                def TU(b):
                    v.wait_ge(s_ds, b + 1)
                    if b >= 1:
                        v.wait_ge(s_sbc, 4 * b)
                    for h in range(4):
                        hsl = slice(h * 128, (h + 1) * 128)
                        v.scalar_tensor_tensor(Tst[:, hsl], Tst[:, hsl], gcs[:, h:h + 1], dsp[:, hsl],
                                               ALU.mult, ALU.add).then_inc(s_tu, 1)

                def ST(b):
                    v.wait_ge(s_ou, b + 1)
                    if b >= 1:
                        v.wait_ge(s_nm, b)
                        v.wait_ge(s_r4, b)
                    for h in range(4):
                        v.bn_stats(st6[:, h, :], ou[:, h * 128:(h + 1) * 128]).then_inc(s_bn, 1)
                    v.wait_ge(s_bn, 4 * (b + 1))
                    for h in range(4):
                        v.bn_aggr(mv[:, h, :], st6[:, h, :]).then_inc(s_mv, 1)

                def RR(b, which):
                    u = 4 * b + which
                    if which == 0:
                        v.wait_ge(s_rt[b % 3], 16 * (b // 3 + 1))
                    v.wait_ge(s_ip, u + 1)
                    if b >= 1:
                        v.wait_ge((s_cq, s_ck)[which], b)
                    src = ip[u % 3][:]
                    x1, x2 = v4(src, 0), v4(src, 1)
                    ct = t4(rtab[:, b % 3, which * 512:which * 512 + 256])
                    stt = t4(rtab[:, b % 3, which * 512 + 256:which * 512 + 512])
                    o = which * 4
                    v.tensor_tensor(t4(rt[:, o + 0, :]), x1, ct, ALU.mult)
                    v.tensor_tensor(t4(rt[:, o + 1, :]), x2, stt, ALU.mult)
                    v.tensor_tensor(t4(rt[:, o + 2, :]), x1, stt, ALU.mult)
                    v.tensor_tensor(t4(rt[:, o + 3, :]), x2, ct, ALU.mult).then_inc((s_rq, s_rk)[which], 1)

                def NMR(b):
                    v.wait_ge(s_r4, b + 1)
                    if b >= 2:
                        v.wait_ge(s_yd, 4 * (b - 1))
                    v.scalar_tensor_tensor(nmr[:, b % 2, :], mv[:, :, 0], -1.0, rs4[:, b % 2, :], ALU.mult, ALU.mult
                                           ).then_inc(s_nm, 1)

                for s in range(NS):
                    if okb(s - 1):
                        ZG(s - 1)
                    if okb(s - 3):
                        PTM(s - 3)
                        TU(s - 3)
                    if okb(s - 4):
                        ST(s - 4)
                    if okb(s):
                        RR(s, 0)
                        RR(s, 1)
                    if okb(s - 4):
                        NMR(s - 4)

            @block.gpsimd
            def _(p):
                for i in range(1, 2 * KC, 2):
                    p.wait_ge(s_wst[i % 4], 16 * (i // 4 + 1))
                    p.tensor_copy(Wb[:, i // 2, 1024:2048], wst[:, i % 4, :]).then_inc(s_wcg, 1)

                def SBC(b):
                    p.wait_ge(s_tu, 4 * (b + 1))
                    if b >= 3:
                        p.wait_ge(s_ou, b - 2)
                    for h in range(4):
                        hsl = slice(h * 128, (h + 1) * 128)
                        p.tensor_scalar(Sb[:, (b + 1) % 4, hsl], Tst[:, hsl], gcs[:, h:h + 1], 1.0, ALU.mult, ALU.mult
                                        ).then_inc(s_sbc, 1)

                def CC(b, which):
                    p.wait_ge((s_rq, s_rk)[which], b + 1)
                    if b >= 4:
                        p.wait_ge(s_ds, b - 3)
                    o = which * 4
                    dst = qktm[:, b % 4, which, :]
                    p.tensor_tensor(v4(dst, 0), t4(rt[:, o + 0, :]), t4(rt[:, o + 1, :]), ALU.subtract)
                    p.tensor_tensor(v4(dst, 1), t4(rt[:, o + 2, :]), t4(rt[:, o + 3, :]), ALU.add
                                    ).then_inc((s_cq, s_ck)[which], 1)

                def MIX(b):
                    p.wait_ge(s_yd, 4 * (b + 1))
                    if b >= 1:
                        p.wait_ge(s_mx, b)
                    p.tensor_tensor(y2[:], yb[:, b % 2, :], rgain[:], ALU.mult).then_inc(s_y2, 1)
                    p.wait_ge(s_y2, b + 1)
                    p.wait_ge(s_zg, b + 1)
                    if b >= 2:
                        p.wait_ge(s_mt, b - 1)
                    p.tensor_tensor(mixtm[:, b % 2, :], y2[:], zgb[:, b % 6, :], ALU.mult).then_inc(s_mx, 1)

                for s in range(NS):
                    if okb(s - 3):
                        SBC(s - 3)
                    if okb(s):
                        CC(s, 0)
                        CC(s, 1)
                    if okb(s - 4):
                        MIX(s - 4)

            @block.scalar
            def _(a):
                na = [0]

                def selfsync(ins):
                    ins.then_inc(s_act, 1)
                    na[0] += 1
                    a.wait_ge(s_act, na[0])

                def MTC(b):
                    T, b4 = b // 4, b % 4
                    a.wait_ge(s_mt, b + 1)
                    if b4 == 0 and T >= 2:
                        a.wait_ge(s_mst[T % 2], 64 * ((T - 2) // 2 + 1))
                    a.activation(mTs[:, T % 2, :, b4 * 128:(b4 + 1) * 128], t4(mt[:, 0:512]), AF.Copy
                                 ).then_inc(s_mtc, 1)

                def R4(b):
                    a.wait_ge(s_mv, 4 * (b + 1))
                    selfsync(a.activation(lnr[:], mv[:, :, 1], AF.Ln, bias=EPS))
                    if b >= 2:
                        a.wait_ge(s_yd, 4 * (b - 1))
                    a.activation(rs4[:, b % 2, :], lnr[:], AF.Exp, scale=-0.5).then_inc(s_r4, 1)

                def NORM(b):
                    a.wait_ge(s_nm, b + 1)
                    if b >= 2:
                        a.wait_ge(s_y2, b - 1)
                    for h in range(4):
                        hsl = slice(h * 128, (h + 1) * 128)
                        a.activation(yb[:, b % 2, hsl], ou[:, hsl], AF.Identity, bias=nmr[:, b % 2, h:h + 1],
                                     scale=rs4[:, b % 2, h:h + 1]).then_inc(s_yd, 1)

                def VZ(b):
                    a.wait_ge(s_ip, 4 * b + 3)
                    if b >= 6:
                        a.wait_ge(s_ou, b - 5)
                    a.activation(vb[:, b % 6, :], ip[(4 * b + 2) % 3][:], AF.Copy).then_inc(s_v, 1)
                    a.wait_ge(s_ip, 4 * b + 4)
                    pz = ip[(4 * b + 3) % 3][:]
                    selfsync(a.activation(sgt[:, 0, :], pz, AF.Exp, scale=-1.0))
                    selfsync(a.activation(sgt[:, 1, :], sgt[:, 0, :], AF.Ln, bias=1.0))
                    if b >= 2:
                        a.wait_ge(s_zg, b - 1)
                    a.activation(sig[:, b % 2, :], sgt[:, 1, :], AF.Exp, scale=-1.0).then_inc(s_sig, 1)

                def TQC(b):
                    a.wait_ge(s_tq, b + 1)
                    if b >= 4:
                        a.wait_ge(s_ou, b - 3)
                    a.activation(qkT[:, b % 4, :], tq[:], AF.Copy).then_inc(s_tqc, 1)

                for s in range(NS):
                    if okb(s - 6):
                        MTC(s - 6)
                    if okb(s - 4):
                        R4(s - 4)
                        NORM(s - 4)
                    if okb(s):
                        VZ(s)
                    if okb(s - 1):
                        TQC(s - 1)


def phase_b(g):
    nc, S, NT, NB = g.nc, g.S, g.NT, g.NB
    TPH = NT // 2
    tiles = []
    for h in range(HL):
        for qt in range(NT):
            for kb in range(4 * qt + 3, -1, -1):
                tiles.append(dict(h=h, qt=qt, kb=kb, first=(kb == 4 * qt + 3), last=(kb == 0),
                                  r=(kb - 4 * qt if kb >= 4 * qt else None), G=h * NT + qt))
    N = len(tiles)
    NG = HL * NT
    nmask = [0] * (N + 1)
    nrup = [0] * (N + 1)
    for i, tl in enumerate(tiles):
        nmask[i + 1] = nmask[i] + (1 if tl["r"] is not None else 0)
        nrup[i + 1] = nrup[i] + (0 if tl["last"] else 1)
    last_of_G = {}
    for i, tl in enumerate(tiles):
        last_of_G[tl["G"]] = i

    with ExitStack() as es:
        sb = lambda n, s, d: g.sb("b_" + n, s, d, es)
        ps = lambda n, s, d: g.ps("b_" + n, s, d, es)
        sem = lambda n: g.sem("b_" + n, es)
        KT = sb("KT", [128, 2, S], BF16)
        VV = sb("VV", [128, 2, S], BF16)
        QT = sb("QT", [128, 2, S], BF16)
        ZG = sb("ZG", [128, 2, S], BF16)
        EE = sb("EE", [128, 6, 2, 512], BF16)
        Lp = sb("Lp", [128, 3, 512], BF16)
        W = sb("W", [128, 3, 512], BF16)
        R = sb("R", [128, 3, 512], BF16)
        Rz = sb("Rz", [128, 512], BF16)
        mst = sb("mst", [128, 2, 512], BF16)
        ZC = [ps(f"ZC{i}", [128, 1024], F32) for i in range(2)]
        Zp = lambda i: ZC[i % 2][:, 0:512]
        Cp = lambda i: ZC[(i + 1) % 2][:, 512:1024]
        E = lambda i: EE[:, i % 6, 0, :]
        EC = lambda i: EE[:, (i + 3) % 6, 1, :]
        Op = [ps(f"Op{i}", [128, 512], F32) for i in range(2)]
        s_hd = [sem(f"hd{i}") for i in range(2)]
        s_hdB = [sem(f"hdB{i}") for i in range(2)]
        s_mst = [sem(f"mst{i}") for i in range(2)]
        s_qk, s_x, s_mask, s_ln, s_rup, s_cum, s_mul, s_pv, s_fin, s_init, s_hdone = (
            sem(n) for n in ("qk", "x", "mask", "ln", "rup", "cum", "mul", "pv", "fin", "init", "hdone"))
        s_wod = [sem(f"wod{i}") for i in range(2)]
        s_woc = sem("woc")
        g.wo_prefetched = True
        first_of_head = {}
        for i_, tl_ in enumerate(tiles):
            first_of_head.setdefault(tl_["h"], i_)

        def cc(p, i_chunk):
            for hf in range(2):
                p.collective_compute("AllGather", ALU.bypass, replica_groups=g.rgroups,
                                     ins=[g.mixc[i_chunk][hf].ap().opt()], outs=[g.gath[i_chunk][hf].ap().opt()]
                                     ).then_inc(g.s_cc, 1)
        tri, ones, dmask = g.tri, g.ones, g.dmask
        NS = N + 8
        ok = lambda i: 0 <= i < N

        with nc.Block() as block:
            @block.sync
            def _(e):
                def load(h):
                    if h >= 2:
                        e.wait_ge(s_fin, (h - 1) * NT)
                    for dst, src in ((KT, g.kT_s), (QT, g.qT_s)):
                        e.dma_start(out=dst[:, h % 2, :], in_=src[h, :, :]).then_inc(s_hd[h % 2], 16)
                    for dst, src in ((VV, g.v_s), (ZG, g.zg_s)):
                        e.dma_start(out=dst[:, h % 2, :], in_=src[h, :, :]).then_inc(s_hdB[h % 2], 16)

                def wo_load(kc):
                    if kc >= 2:
                        e.wait_ge(s_woc, kc - 1)
                    e.dma_start(out=g.d_wst[:, kc % 2, :], in_=g.wout_d[kc * 128:(kc + 1) * 128, :]
                                ).then_inc(s_wod[kc % 2], 16)

                def stores(h):
                    for qt in range(NT):
                        G = h * NT + qt
                        if 2 <= G < 2 + KC // 2:
                            wo_load(2 * (G - 2))
                            wo_load(2 * (G - 2) + 1)
                        e.wait_ge(s_fin, G + 1)
                        e.dma_start(out=g.mixc[h][qt // TPH][:, (qt % TPH) * 512:(qt % TPH + 1) * 512],
                                    in_=mst[:, G % 2, :]).then_inc(s_mst[G % 2], 16)

                load(0)
                e.wait_ge(s_hd[0], 32)
                e.wait_ge(s_hdB[0], 32)
                if HL > 1:
                    load(1)
                for h in range(HL):
                    stores(h)
                    for i in range(2):
                        e.wait_ge(s_mst[i], 16 * len([x for x in range((h + 1) * NT) if x % 2 == i]))
                    e.nop().then_inc(s_hdone, 1)
                    if h + 2 < HL:
                        load(h + 2)

            @block.tensor
            def _(t):
                def QK(i):
                    tl = tiles[i]
                    h, qt, kb = tl["h"], tl["qt"], tl["kb"]
                    if tl["first"] and qt == 0:
                        t.wait_ge(s_hd[h % 2], 32 * (h // 2 + 1))
                    if i >= 2:
                        t.wait_ge(s_x, i - 1)
                    t.matmul(Zp(i), KT[:, h % 2, kb * 128:(kb + 1) * 128], QT[:, h % 2, qt * 512:(qt + 1) * 512],
                             start=True, stop=True).then_inc(s_qk, 1)

                def CUM(i):
                    tl = tiles[i]
                    t.wait_ge(s_ln, i + 1)
                    if i >= 2:
                        t.wait_ge(s_x, i + 2)
                    ins = t.matmul(Cp(i), tri[:], Lp[:, i % 3, :], start=True, stop=tl["first"])
                    if not tl["first"]:
                        t.wait_ge(s_rup, nrup[i])
                        ins = t.matmul(Cp(i), ones[:], R[:, i % 3, :], start=False, stop=True)
                    ins.then_inc(s_cum, 1)

                def PV(i):
                    tl = tiles[i]
                    h, kb, G = tl["h"], tl["kb"], tl["G"]
                    t.wait_ge(s_mul, i + 1)
                    if tl["first"] and tl["qt"] == 0:
                        t.wait_ge(s_hdB[h % 2], 32 * (h // 2 + 1))
                    if tl["first"] and G >= 2:
                        t.wait_ge(s_fin, G - 1)
                    t.matmul(Op[G % 2][:], VV[:, h % 2, kb * 128:(kb + 1) * 128], W[:, i % 3, :],
                             start=tl["first"], stop=tl["last"]).then_inc(s_pv, 1)

                for s in range(NS):
                    if ok(s):
                        QK(s)
                    if ok(s - 3):
                        CUM(s - 3)
                    if ok(s - 6):
                        PV(s - 6)

            @block.scalar
            def _(a):
                def XEXP(s_):
                    i1, i4 = s_ - 1, s_ - 4
                    if ok(i1):
                        a.wait_ge(s_qk, i1 + 1)
                        if i1 >= 6:
                            a.wait_ge(s_mul, i1 - 5)
                    if ok(i4):
                        a.wait_ge(s_cum, i4 + 1)
                        if i4 >= 3:
                            a.wait_ge(s_mul, i4 - 2)
                    if ok(i1) and ok(i4):
                        a.activation(EE[:, i1 % 6, :, :].rearrange("p a t -> p (a t)"), ZC[i1 % 2][:], AF.Exp, scale=-1.0
                                     ).then_inc(s_x, 1)
                    elif ok(i1):
                        a.activation(E(i1), Zp(i1), AF.Exp, scale=-1.0).then_inc(s_x, 1)
                    else:
                        a.activation(EC(i4), Cp(i4), AF.Exp, scale=-1.0).then_inc(s_x, 1)

                def LN(i):
                    tl = tiles[i]
                    a.wait_ge(s_x, i + 1)
                    if tl["r"] is not None:
                        a.wait_ge(s_mask, nmask[i + 1])
                    if i >= 3:
                        a.wait_ge(s_cum, i - 2)
                        a.wait_ge(s_rup, nrup[i - 2])
                    a.activation(Lp[:, i % 3, :], E(i), AF.Ln, bias=1.0).then_inc(s_ln, 1)

                for s in range(NS):
                    if ok(s - 1) or ok(s - 4):
                        XEXP(s)
                    if ok(s - 2):
                        LN(s - 2)

            @block.gpsimd
            def _(p):
                p.memset(Rz[:], 0.0).then_inc(s_init, 1)
                p.wait_ge(s_init, 1)
                if g.gather:
                    for i_chunk in range(HL, 2 * HL):
                        cc(p, i_chunk)

                if g.gather:
                    for h in range(HL - 1):
                        p.wait_ge(s_hdone, h + 1)
                        cc(p, h)

            @block.vector
            def _(v):
                v.wait_ge(s_init, 1)

                def WCAST(kc):
                    v.wait_ge(s_wod[kc % 2], 16 * (kc // 2 + 1))
                    v.tensor_copy(g.d_Wo[:, kc, :], g.d_wst[:, kc % 2, :]).then_inc(s_woc, 1)

                def MASK(i):
                    tl = tiles[i]
                    if tl["r"] is None:
                        return
                    v.wait_ge(s_x, i + 1)
                    v.tensor_tensor(E(i), E(i), dmask[:, tl["r"], :], ALU.mult).then_inc(s_mask, 1)

                def RUP(i):
                    tl = tiles[i]
                    if tl["last"]:
                        return
                    v.wait_ge(s_ln, i + 1)
                    if i >= 2:
                        v.wait_ge(s_cum, i - 1)
                    v.wait_ge(s_rup, nrup[i])
                    src = Rz[:] if tl["first"] else R[:, i % 3, :]
                    v.tensor_tensor(R[:, (i + 1) % 3, :], src, Lp[:, i % 3, :], ALU.add).then_inc(s_rup, 1)

                def MUL(i):
                    v.wait_ge(s_x, i + 4)
                    if i >= 3:
                        v.wait_ge(s_pv, i - 2)
                    v.tensor_tensor(W[:, i % 3, :], E(i), EC(i), ALU.mult).then_inc(s_mul, 1)

                def FIN(G):
                    h, qt = G // NT, G % NT
                    v.wait_ge(s_pv, last_of_G[G] + 1)
                    if G >= 2:
                        v.wait_ge(s_mst[G % 2], 16 * ((G - 2) // 2 + 1))
                    v.tensor_tensor(mst[:, G % 2, :], Op[G % 2][:], ZG[:, h % 2, qt * 512:(qt + 1) * 512], ALU.mult
                                    ).then_inc(s_fin, 1)

                for s in range(NS):
                    if ok(s - 1):
                        MASK(s - 1)
                    if ok(s - 3):
                        RUP(s - 3)
                    if ok(s - 5):
                        MUL(s - 5)
                    if ok(s - 7) and tiles[s - 7]["last"]:
                        G_ = tiles[s - 7]["G"]
                        FIN(G_)
                        if 3 <= G_ < 3 + KC // 2:
                            WCAST(2 * (G_ - 3))
                            WCAST(2 * (G_ - 3) + 1)
                assert NG >= 3 + KC // 2
                v.wait_ge(s_woc, KC)


def phase_cd(g):
    nc, S, NT, NB = g.nc, g.S, g.NT, g.NB
    TPH = NT // 2
    gather = g.gather
    with ExitStack() as es:
        sb = lambda n, s, d: g.sb("d_" + n, s, d, es)
        ps = lambda n, s, d: g.ps("d_" + n, s, d, es)
        sem = lambda n: g.sem("d_" + n, es)
        Wo, wst = g.d_Wo, g.d_wst
        pre = g.wo_prefetched
        mg = sb("mg", [128, 2, KC, 512], BF16)
        xr = sb("xr", [128, 3, 1024], F32)
        yt = sb("yt", [128, 2, 512], F32)
        ost = sb("ost", [128, 3, 1024], F32)
        po = [ps(f"po{i}", [128, 512], F32) for i in range(4)]
        s_wst = [sem(f"wst{i}") for i in range(2)]
        s_wcv, s_wcg = sem("wcv"), sem("wcg")
        s_mg = [sem(f"mg{i}") for i in range(2)]
        s_xr = [sem(f"xr{i}") for i in range(3)]
        s_ost = [sem(f"ost{i}") for i in range(3)]
        s_cc = g.s_cc
        s_po, s_y1, s_y2 = sem("po"), sem("y1"), sem("y2")
        gate_bc = g.gate_bc
        NCC = 16

        with nc.Block() as block:
            @block.sync
            def _(e):
                for kc in range(0 if pre else KC):
                    if kc >= 2:
                        e.wait_ge(s_wcv if kc % 2 == 0 else s_wcg, kc // 2)
                    e.dma_start(out=wst[:, kc % 2, :], in_=g.wout_d[kc * 128:(kc + 1) * 128, :]
                                ).then_inc(s_wst[kc % 2], 16)
                if gather:
                    e.wait_ge(s_cc, NCC)
                    if g.gdump is not None:
                        s_gd = g.sem("d_gd")
                        for i in range(8):
                            for hf2 in range(2):
                                for r2 in range(2):
                                    e.dma_start(out=g.gdump[2 * i + hf2, r2 * 128:(r2 + 1) * 128, :],
                                                in_=g.gath[i][hf2][r2 * 128:(r2 + 1) * 128, :]).then_inc(s_gd, 16)
                        e.wait_ge(s_gd, 16 * 32)
                for T in range(NT):
                    hf, tt = T // TPH, T % TPH
                    if T >= 2:
                        e.wait_ge(s_po, 8 * (T - 1))
                    for i in range(8):
                        if gather:
                            for r in range(2):
                                e.dma_start(out=mg[:, T % 2, 2 * i + r, :],
                                            in_=g.gath[i][hf][r * 128:(r + 1) * 128, tt * 512:(tt + 1) * 512]
                                            ).then_inc(s_mg[T % 2], 16)
                        else:
                            for r in range(2):
                                e.dma_start(out=mg[:, T % 2, 2 * i + r, :], in_=g.mixc[i][hf][:, tt * 512:(tt + 1) * 512]
                                            ).then_inc(s_mg[T % 2], 16)
                    for b4 in range(4):
                        gb = 4 * T + b4
                        if gb >= 3:
                            e.wait_ge(s_y2, 2 * gb - 4)
                        e.dma_start(out=xr[:, gb % 3, :], in_=g.xres_d[gb * 128:(gb + 1) * 128, :]
                                    ).then_inc(s_xr[gb % 3], 16)

            @block.tensor
            def _(t):
                if not pre:
                    t.wait_ge(s_wcv, KC // 2)
                    t.wait_ge(s_wcg, KC // 2)
                for T in range(NT):
                    t.wait_ge(s_mg[T % 2], 256 * (T // 2 + 1))
                    for b4 in range(4):
                        gb = 4 * T + b4
                        for nh in range(2):
                            idx = 2 * gb + nh
                            if idx >= 4:
                                t.wait_ge(s_y1, idx - 3)
                            chs = range(KC) if gather else range(0, KC, 2)
                            for n_, ch in enumerate(chs):
                                ins = t.matmul(po[idx % 4][:], mg[:, T % 2, ch, b4 * 128:(b4 + 1) * 128],
                                               Wo[:, ch, nh * 512:(nh + 1) * 512],
                                               start=(n_ == 0), stop=(n_ == len(chs) - 1))
                            ins.then_inc(s_po, 1)

            @block.vector
            def _(v):
                for kc in range(0, 0 if pre else KC, 2):
                    v.wait_ge(s_wst[0], 16 * (kc // 2 + 1))
                    v.tensor_copy(Wo[:, kc, :], wst[:, 0, :]).then_inc(s_wcv, 1)
                for idx in range(2 * NB):
                    nh = idx % 2
                    v.wait_ge(s_po, idx + 1)
                    if idx >= 2:
                        v.wait_ge(s_y2, idx - 1)
                    v.tensor_tensor(yt[:, idx % 2, :], po[idx % 4][:], gate_bc[:, nh * 512:(nh + 1) * 512], ALU.mult
                                    ).then_inc(s_y1, 1)

            @block.gpsimd
            def _(p):
                if gather:
                    for hf in range(2):
                        p.collective_compute("AllGather", ALU.bypass, replica_groups=g.rgroups,
                                             ins=[g.mixc[HL - 1][hf].ap().opt()], outs=[g.gath[HL - 1][hf].ap().opt()]
                                             ).then_inc(s_cc, 1)
                for kc in range(1, 1 if pre else KC, 2):
                    p.wait_ge(s_wst[1], 16 * (kc // 2 + 1))
                    p.tensor_copy(Wo[:, kc, :], wst[:, 1, :]).then_inc(s_wcg, 1)
                for gb in range(NB):
                    for nh in range(2):
                        idx = 2 * gb + nh
                        p.wait_ge(s_y1, idx + 1)
                        if nh == 0:
                            p.wait_ge(s_xr[gb % 3], 16 * (gb // 3 + 1))
                            if gb >= 3:
                                p.wait_ge(s_ost[gb % 3], 16 * ((gb - 3) // 3 + 1))
                        p.tensor_tensor(ost[:, gb % 3, nh * 512:(nh + 1) * 512], yt[:, idx % 2, :],
                                        xr[:, gb % 3, nh * 512:(nh + 1) * 512], ALU.add).then_inc(s_y2, 1)
                    p.wait_ge(s_y2, 2 * gb + 2)
                    p.dma_start(out=g.out_d[gb * 128:(gb + 1) * 128, :], in_=ost[:, gb % 3, :]
                                ).then_inc(s_ost[gb % 3], 16)
                for i in range(3):
                    p.wait_ge(s_ost[i], 16 * len([x for x in range(NB) if x % 3 == i]))
```
